# Optimizing a Trainium2 kernel written in Bass

```python
import jax, jax.numpy as jnp
from jax import lax
import numpy as np

D_MODEL = 1024
BATCH = 4
SEQ = 8192
DEPTH = 1

CHUNK = 64
WIDTH_A = D_MODEL
A_GROUPS = 8
A_GROUP_DIM = WIDTH_A // A_GROUPS
SPATIAL_CHUNK = 128
WIDTH_B = D_MODEL
B_GROUPS = 8
CONV_WIDTH = 3
SEG_WIDTHS = [WIDTH_A, WIDTH_A, WIDTH_A,
              WIDTH_B, WIDTH_B, WIDTH_B, WIDTH_B,
              D_MODEL, D_MODEL]
PROJ_WIDTH = int(sum(SEG_WIDTHS))
SPLIT_POINTS = [int(v) for v in np.cumsum(SEG_WIDTHS)[:-1]]
EPS = 1e-6

kernel_name = "hybrid_gmlp_shortconv_gated_block"


def rmsnorm(x, g):
    xf = x.astype(jnp.float32)
    r = lax.rsqrt(jnp.mean(xf * xf, axis=-1, keepdims=True) + EPS)
    return (xf * r).astype(x.dtype) * g


def spatial_gating(u, v, v_norm_g, w_spatial, b_spatial):
    bsz, seq, _ = v.shape
    n_chunks = seq // SPATIAL_CHUNK
    u = jax.nn.gelu(u, approximate=False)
    v = rmsnorm(jax.nn.gelu(v, approximate=False), v_norm_g)
    v = v.reshape(bsz, n_chunks, SPATIAL_CHUNK, A_GROUPS, A_GROUP_DIM)
    tril = jnp.tril(jnp.ones((SPATIAL_CHUNK, SPATIAL_CHUNK), dtype=bool))
    w_s = jnp.where(tril[None], w_spatial, jnp.zeros((), w_spatial.dtype))
    mixed = jnp.einsum('gts,bnsgc->bntgc', w_s, v) + b_spatial.T[None, None, :, :, None]
    return u * mixed.reshape(bsz, seq, WIDTH_A)


def short_gated_conv(x_b, c_b, b_b, conv_w):
    seq = x_b.shape[1]
    hc = c_b * x_b
    padded = jnp.pad(hc, ((0, 0), (CONV_WIDTH - 1, 0), (0, 0)))
    conv = padded[:, 0:seq, :] * conv_w[0]
    for k in range(1, CONV_WIDTH):
        conv = conv + padded[:, k:k + seq, :] * conv_w[k]
    return b_b * conv


def setup_inputs(seed: int = 0) -> dict:
    key = jax.random.key(seed)
    ks = jax.random.split(key, 11)
    x = jax.random.normal(ks[0], (BATCH, SEQ, D_MODEL), jnp.float32)
    norm_g = 1.0 + 0.02 * jax.random.normal(ks[1], (DEPTH, D_MODEL), jnp.float32)
    w_in = jax.random.normal(ks[2], (DEPTH, D_MODEL, PROJ_WIDTH), jnp.float32) * D_MODEL ** -0.5
    v_norm_g = 1.0 + 0.02 * jax.random.normal(ks[3], (DEPTH, WIDTH_A), jnp.float32)
    w_spatial = jax.random.normal(ks[4], (DEPTH, A_GROUPS, SPATIAL_CHUNK, SPATIAL_CHUNK), jnp.float32) * (0.5 * SPATIAL_CHUNK ** -0.5)
    b_spatial = 1.0 + 0.02 * jax.random.normal(ks[5], (DEPTH, A_GROUPS, SPATIAL_CHUNK), jnp.float32)
    conv_w = jax.random.normal(ks[6], (DEPTH, CONV_WIDTH, WIDTH_B), jnp.float32) * CONV_WIDTH ** -0.5
    w_branch_a = jax.random.normal(ks[7], (DEPTH, WIDTH_A, D_MODEL), jnp.float32) * WIDTH_A ** -0.5
    w_branch_b = jax.random.normal(ks[8], (DEPTH, WIDTH_B, D_MODEL), jnp.float32) * WIDTH_B ** -0.5
    w_out = jax.random.normal(ks[9], (DEPTH, D_MODEL, D_MODEL), jnp.float32) * D_MODEL ** -0.5
    final_norm_g = 1.0 + 0.02 * jax.random.normal(ks[10], (D_MODEL,), jnp.float32)
    return {"x": x, "norm_g": norm_g, "w_in": w_in, "v_norm_g": v_norm_g,
            "w_spatial": w_spatial, "b_spatial": b_spatial, "conv_w": conv_w,
            "w_branch_a": w_branch_a, "w_branch_b": w_branch_b, "w_out": w_out,
            "final_norm_g": final_norm_g}


def reference(x, norm_g, w_in, v_norm_g, w_spatial, b_spatial, conv_w,
              w_branch_a, w_branch_b, w_out, final_norm_g):
    for layer in range(DEPTH):
        h = rmsnorm(x, norm_g[layer])
        proj = jnp.einsum('bsd,de->bse', h, w_in[layer])
        u_a, v_a, z_a, x_b, c_b, b_b, z_b, g_a, g_b = jnp.split(proj, SPLIT_POINTS, axis=-1)
        y_a = spatial_gating(u_a, v_a, v_norm_g[layer], w_spatial[layer], b_spatial[layer]) * jax.nn.silu(z_a)
        y_b = short_gated_conv(x_b, c_b, b_b, conv_w[layer]) * jax.nn.silu(z_b)
        ya_d = jnp.einsum('bse,ed->bsd', y_a, w_branch_a[layer])
        yb_d = jnp.einsum('bse,ed->bsd', y_b, w_branch_b[layer])
        merged = jax.nn.sigmoid(g_a) * ya_d + jax.nn.sigmoid(g_b) * yb_d
        x = x + jnp.einsum('bsd,de->bse', merged, w_out[layer])
    return rmsnorm(x, final_norm_g)
```

```python
import numpy as np
from contextlib import ExitStack

import concourse.bass as bass
import concourse.mybir as mybir
from concourse.bass_utils import run_bass_kernel_spmd

F32 = mybir.dt.float32
BF16 = mybir.dt.bfloat16
AF = mybir.ActivationFunctionType
ALU = mybir.AluOpType

D = 1024
NCORES = 8
TOK = 4096
T = 512
NT = TOK // T
NS = 4
KC = 8
EPS = 1e-6
RING = 4
NSLOT = 24
SLOT_ELEMS = 4096
WB_SPREAD = 4

S_WV = [0, 1]
S_P3 = list(range(2, 10))
S_P2 = list(range(10, 14))
S_P4 = list(range(14, 22))
S_WO = [22, 23]


class Sem:
    def __init__(self, h, name):
        self.h = h
        self.name = name
        self.count = 0


class Buf:
    __slots__ = ("name", "writer", "readers")

    def __init__(self, name):
        self.name = name
        self.writer = None
        self.readers = {}


class Eng:
    def __init__(self, name, sem, self_sync=True):
        self.name = name
        self.sem = sem
        self.ops = []
        self.waited = {}
        self.self_sync = self_sync


def _collect_waits(eng, reads, writes):
    deps = {}

    def add(tok):
        if tok is None:
            return
        s, v = tok
        if deps.get(s, 0) < v:
            deps[s] = v

    for b in reads:
        add(b.writer)
    for b in writes:
        add(b.writer)
        for s, v in b.readers.items():
            add((s, v))
    waits = []
    for s, v in deps.items():
        if s is eng.sem and not eng.self_sync:
            continue
        if eng.waited.get(s, 0) < v:
            eng.waited[s] = v
            waits.append((s, v))
    return waits


def _commit(tok, reads, writes):
    s, v = tok
    for b in reads:
        if b.readers.get(s, 0) < v:
            b.readers[s] = v
    for b in writes:
        b.writer = tok
        b.readers = {}


def op(eng, fn, reads=(), writes=()):
    waits = _collect_waits(eng, reads, writes)
    eng.sem.count += 1
    tok = (eng.sem, eng.sem.count)
    eng.ops.append((waits, fn, eng.sem, 1))
    _commit(tok, reads, writes)
    return tok


def dma(eng, fn, dsem, reads=(), writes=()):
    waits = _collect_waits(eng, reads, writes)
    dsem.count += 16
    tok = (dsem, dsem.count)
    eng.ops.append((waits, fn, dsem, 16))
    _commit(tok, reads, writes)
    return tok


def emit(eng, e):
    for waits, fn, sem, inc in eng.ops:
        for s, v in waits:
            e.wait_ge(s.h, v)
        ins = fn(e)
        ins.then_inc(sem.h, inc)


class Rot:
    def __init__(self, items):
        self.items = items
        self.i = 0

    def next(self):
        it = self.items[self.i % len(self.items)]
        self.i += 1
        return it


def build_program():
    nc = bass.Bass("TRN2", target_bir_lowering=False)
    x_d = nc.dram_tensor("x", [TOK, D], F32, kind="ExternalInput").ap()
    xh_d = nc.dram_tensor("xh", [2, D], F32, kind="ExternalInput").ap()
    xT_d = nc.dram_tensor("xT", [D, TOK], F32, kind="ExternalInput").ap()
    gT_d = nc.dram_tensor("gT", [128, KC], F32, kind="ExternalInput").ap()
    wst_d = nc.dram_tensor("wst", [NSLOT, 128, SLOT_ELEMS], F32, kind="ExternalInput").ap()
    gains_d = nc.dram_tensor("gains", [1, 3 * D], F32, kind="ExternalInput").ap()
    bsp_d = nc.dram_tensor("bsp", [1, 8 * 128], F32, kind="ExternalInput").ap()
    convw_d = nc.dram_tensor("convw", [128, 24], F32, kind="ExternalInput").ap()
    wsT_d = nc.dram_tensor("wsT", [128, 8 * 128], F32, kind="ExternalInput").ap()
    y_d = nc.dram_tensor("y", [TOK, D], F32, kind="ExternalOutput").ap()
    wbf_d = nc.dram_tensor("wbf", [NSLOT, 128, SLOT_ELEMS], BF16).ap()

    with ExitStack() as es:
        def sb(name, shape, dt):
            return es.enter_context(nc.sbuf_tensor("sb_" + name, shape, dt))

        def new_sem(name):
            return Sem(es.enter_context(nc.semaphore(name)), name)

        xbuf = sb("xbuf", [128, 2, NS, D], F32)
        xsb = sb("xsb", [128, 1, D], BF16)
        hT = sb("hT", [128, 2, KC, T], BF16)
        hTh = sb("hTh", [128, KC, 128], BF16)
        xhb = sb("xhb", [128, D], F32)
        gvb = sb("gvb", [128, 2, D], F32)
        junk = sb("junk", [128, D], BF16)
        vpp = sb("vpp", [128, NS, D], BF16)
        yA = sb("yA", [128, KC, T], BF16)
        yB = sb("yB", [128, KC, T], BF16)
        mg = sb("mg", [128, KC, T], BF16)
        NTMP = 19
        tmp = sb("tmp", [128, NTMP, T], F32)
        hcb = sb("hcb", [128, 2, T + 4], F32)
        gb = sb("gb", [128, 3, D], F32)
        bt = sb("bt", [128, 8, 128], F32)
        wsT32 = sb("wsT32", [128, 8, 128], F32)
        WT = sb("WT", [128, 8, 128], BF16)
        ident = sb("ident", [128, 128], BF16)
        ones = sb("ones", [128, 128], F32)
        convw = sb("convw", [128, 24], F32)
        halo = sb("halo", [128, 8, 2], F32)
        tiny = sb("tiny", [128, 3, 16], F32)
        mhalf = sb("mhalf", [128, 1], F32)
        hxc = sb("hxc", [128, 2, 4], F32)
        ring = sb("ring", [128, RING, SLOT_ELEMS], BF16)
        xTc = sb("xTc", [128, 3, T], F32)
        diag = sb("diag", [128, 3, NS * 128], BF16)
        onesb = sb("onesb", [128, 128], BF16)
        gT = sb("gT", [128, KC], F32)
        rs32 = sb("rs32", [128, NS, 2], F32)
        rsb = sb("rsb", [128, NS, 2], BF16)

        ps = es.enter_context(nc.psum_tensor("ps_main", [128, 7, 512], F32))
        psT = es.enter_context(nc.psum_tensor("ps_tr", [128, D], BF16))

        PE = Eng("pe", new_sem("s_pe"), self_sync=False)
        ACT = Eng("act", new_sem("s_act"))
        DVE = Eng("dve", new_sem("s_dve"))
        POOL = Eng("pool", new_sem("s_pool"))
        SP = Eng("sp", new_sem("s_sp"))
        sem_setup = new_sem("d_setup")
        sem_setup2 = new_sem("d_setup2")
        sem_g0 = new_sem("d_g0")
        sem_gT = new_sem("d_gT")
        sem_xT = [new_sem(f"d_xT{q}") for q in range(3)]
        sem_ring = [new_sem(f"d_ring{r}") for r in range(RING)]
        sem_xl = [[new_sem(f"d_xl{b}{s}") for s in range(NS)] for b in range(2)]
        sem_st = [[new_sem(f"d_st{b}{s}") for s in range(NS)] for b in range(2)]
        sem_wb = [new_sem(f"d_wb{r}") for r in range(RING)]
        sem_ringc = [new_sem(f"d_ringc{r}") for r in range(RING)]

        B_x = [[Buf(f"x{b}{s}") for s in range(NS)] for b in range(2)]
        B_xsb = [Buf(f"xsb{b}") for b in range(1)]
        B_hT = [[Buf(f"hT{b}{s}") for s in range(NS)] for b in range(2)]
        B_hTh = Buf("hTh")
        B_xhb = Buf("xhb")
        B_gvb = [Buf(f"gvb{b}") for b in range(2)]
        B_vpp = [Buf(f"vpp{s}") for s in range(NS)]
        B_yA = [Buf(f"yA{j}") for j in range(KC)]
        B_yB = [Buf(f"yB{j}") for j in range(KC)]
        B_mg = [Buf(f"mg{j}") for j in range(KC)]
        B_bank = [Buf(f"bank{b}") for b in range(7)]
        B_psT = Buf("psT")
        B_ring = [Buf(f"ring{r}") for r in range(RING)]
        B_wbf = [Buf(f"wbf{k}") for k in range(NSLOT)]
        B_gb = Buf("gb")
        B_gb0 = Buf("gb0")
        B_bt = Buf("bt")
        B_wsT32 = Buf("wsT32")
        B_WT = Buf("WT")
        B_ident = Buf("ident")
        B_ones = Buf("ones")
        B_convw = Buf("convw")
        B_halo = [Buf(f"halo{j}") for j in range(KC)]
        B_mhalf = Buf("mhalf")
        B_junk = Buf("junk")
        B_xTc = [Buf(f"xTc{q}") for q in range(3)]
        B_diag = [Buf(f"diag{s}") for s in range(NS)]
        B_rs = [Buf(f"rs{s}") for s in range(NS)]
        B_onesb = Buf("onesb")
        B_gT = Buf("gT")
        psR = psT[:].bitcast(F32)

        tmp_rot = Rot([(tmp[:, q, :], Buf(f"tmp{q}")) for q in range(NTMP)])
        hc_rot = Rot([(hcb[:, q, :], Buf(f"hc{q}")) for q in range(2)])
        hxc_rot = Rot([(hxc[:, q, :], Buf(f"hxc{q}")) for q in range(2)])
        tiny_rot = Rot([((tiny[:, 0, q:q + 1], tiny[:, 1, q:q + 1], tiny[:, 2, q:q + 1]),
                         (Buf(f"ss{q}"), Buf(f"ms{q}"), Buf(f"r{q}"))) for q in range(16)])

        bank_ptr = [0]

        def alloc_bank():
            b = bank_ptr[0] % 7
            bank_ptr[0] += 1
            return b

        def alloc_pair():
            while (bank_ptr[0] % 7) % 2 == 1 or (bank_ptr[0] % 7) == 6:
                bank_ptr[0] += 1
            b = bank_ptr[0] % 7
            bank_ptr[0] += 2
            return b

        stream = []
        stream += [(0, s) for s in S_WV]
        for i in range(NT):
            stream += [(i, s) for s in S_P3 + S_P2 + S_P4]
            if i + 1 < NT:
                stream += [(i + 1, s) for s in S_WV]
            stream += [(i, s) for s in S_WO]
        stream_pos = {ts: n for n, ts in enumerate(stream)}
        next_load = [0]

        pending_wb = []

        def flush_wb(upto):
            while pending_wb and pending_wb[0][0] <= upto:
                _, r, slot = pending_wb.pop(0)
                dma(SP, lambda e, r=r, slot=slot: e.dma_start(out=wbf_d[slot], in_=ring[:, r, :]),
                    sem_wb[r], reads=[B_ring[r]], writes=[B_wbf[slot]])

        def issue_load():
            n = next_load[0]
            flush_wb(n - 2)
            if n >= len(stream):
                return
            next_load[0] += 1
            ti, slot = stream[n]
            r = n % RING
            wbt = slot % WB_SPREAD
            if ti <= wbt:
                extra = []
                dma(POOL, lambda e, r=r, slot=slot: e.dma_start(out=ring[:, r, :], in_=wst_d[slot]),
                    sem_ringc[r], reads=extra, writes=[B_ring[r]])
                if ti == wbt:
                    pending_wb.append((n, r, slot))
            else:
                dma(SP, lambda e, r=r, slot=slot: e.dma_start(out=ring[:, r, :], in_=wbf_d[slot]),
                    sem_ring[r], reads=[B_wbf[slot]], writes=[B_ring[r]])

        def ring_of(i, slot):
            return stream_pos[(i, slot)] % RING

        def slot_done(i, slot):
            n = stream_pos[(i, slot)]
            assert next_load[0] == n + RING or next_load[0] >= len(stream), (next_load[0], n)
            issue_load()

        def rms_scale(src_ap, src_buf, width):
            (ss, ms, r), (b_ss, b_ms, b_r) = tiny_rot.next()
            op(ACT, lambda e: e.activation(out=junk[:, 0:width], in_=src_ap, func=AF.Square, accum_out=ss),
               reads=[src_buf], writes=[b_ss, B_junk])
            op(DVE, lambda e: e.tensor_scalar(out=ms, in0=ss, scalar1=1.0 / width, scalar2=EPS,
                                              op0=ALU.mult, op1=ALU.add),
               reads=[b_ss], writes=[b_ms])
            op(POOL, lambda e: e.tensor_tensor(out=r, in0=ms, in1=mhalf[:, 0:1], op=ALU.pow),
               reads=[b_ms, B_mhalf], writes=[b_r])
            return r, b_r

        def mm_group(out_ap, pairs, reads, writes):
            n = len(pairs)

            def fn(e):
                ins = None
                for q, (l, r_) in enumerate(pairs):
                    ins = e.matmul(out_ap, lhsT=l, rhs=r_, start=(q == 0), stop=(q == n - 1))
                return ins
            return op(PE, fn, reads=reads, writes=writes)

        def blk(r, q):
            return ring[:, r, q * 1024:(q + 1) * 1024].rearrange("p (k c) -> p k c", k=KC)

        def wide(r):
            return ring[:, r, :].rearrange("p (k c) -> p k c", k=KC)

        p0_state = {}
        xs_ctr = [0]

        def prepA(key, x_ap, x_buf):
            p0_state[key] = rms_scale(x_ap, x_buf, D)

        def prepB(key, x_ap, x_buf):
            r, b_r = p0_state[key]
            q = 0
            p0_state[key] = q
            op(DVE, lambda e: e.scalar_tensor_tensor(out=xsb[:, q, :], in0=x_ap, scalar=r, in1=gb[:, 0, :],
                                                     op0=ALU.mult, op1=ALU.mult),
               reads=[x_buf, b_r, B_gb0], writes=[B_xsb[q]])

        def prepC(key, hT_out_ap, hT_buf):
            q = p0_state.pop(key)

            def tr(e):
                ins = None
                for k in range(KC):
                    ins = e.transpose(psT[:, k * 128:(k + 1) * 128], xsb[:, q, k * 128:(k + 1) * 128], ident[:])
                return ins
            op(PE, tr, reads=[B_xsb[q], B_ident], writes=[B_psT])
            op(ACT, lambda e: e.activation(out=hT_out_ap, in_=psT[:].rearrange("p (k t) -> p k t", k=KC),
                                           func=AF.Copy),
               reads=[B_psT], writes=[hT_buf])

        def P0A(i, s):
            prepA((i, s), xbuf[:, i % 2, s, :], B_x[i % 2][s])

        def P0B(i, s):
            prepB((i, s), xbuf[:, i % 2, s, :], B_x[i % 2][s])

        def P0C(i, s):
            prepC((i, s), hT[:, i % 2, :, s * 128:(s + 1) * 128], B_hT[i % 2][s])

        xT_ctr = [0]

        def prepB2(i, s):
            r, b_r = p0_state.pop((i, s))
            c1, c2 = rsb[:, s, 0:1], rsb[:, s, 1:2]
            d1, d2 = rs32[:, s, 0:1], rs32[:, s, 1:2]
            cols = slice(s * 128, (s + 1) * 128)
            op(DVE, lambda e: e.tensor_copy(out=c1, in_=r), reads=[b_r], writes=[B_rs[s]])
            op(DVE, lambda e: e.tensor_tensor(out=d1, in0=r, in1=c1, op=ALU.subtract),
               reads=[b_r, B_rs[s]], writes=[B_rs[s]])
            op(DVE, lambda e: e.tensor_copy(out=c2, in_=d1), reads=[B_rs[s]], writes=[B_rs[s]])
            op(DVE, lambda e: e.tensor_tensor(out=d2, in0=d1, in1=c2, op=ALU.subtract),
               reads=[B_rs[s]], writes=[B_rs[s]])
            for t, sc in enumerate((r, d1, d2)):
                op(DVE, lambda e, t=t, sc=sc: e.tensor_scalar(out=diag[:, t, cols], in0=ident[:], scalar1=sc,
                                                               scalar2=None, op0=ALU.mult),
                   reads=[b_r, B_rs[s], B_ident], writes=[B_diag[s]])

        def Rmm(i):
            def fn(e):
                ins = None
                for t in range(3):
                    ins = e.matmul(psR, lhsT=onesb[:], rhs=diag[:, t, :], start=(t == 0), stop=(t == 2))
                return ins
            op(PE, fn, reads=B_diag + [B_onesb], writes=[B_psT])

        def loadT(i, k):
            q = xT_ctr[0] % 3
            xT_ctr[0] += 1
            dma(SP, lambda e: e.dma_start(out=xTc[:, q, :], in_=xT_d[k * 128:(k + 1) * 128, i * T:(i + 1) * T]),
                sem_xT[q], reads=[], writes=[B_xTc[q]])
            return q

        def C2(i, k, q):
            b = i % 2
            op(DVE, lambda e: e.scalar_tensor_tensor(out=hT[:, b, k, :], in0=xTc[:, q, :], scalar=gT[:, k:k + 1],
                                                     in1=psR, op0=ALU.mult, op1=ALU.mult),
               reads=[B_xTc[q], B_gT, B_psT], writes=B_hT[b])

        def load_x(i, s):
            b = i % 2
            row0 = i * T + s * 128
            dma(SP, lambda e: e.dma_start(out=xbuf[:, b, s, :], in_=x_d[row0:row0 + 128, :]),
                sem_xl[b][s], reads=[], writes=[B_x[b][s]])

        def store_y(i, s):
            b = i % 2
            row0 = i * T + s * 128
            dma(SP, lambda e: e.dma_start(out=y_d[row0:row0 + 128, :], in_=xbuf[:, b, s, :]),
                sem_st[b][s], reads=[B_x[b][s]], writes=[])

        def P1(i):
            b = i % 2
            r0, r1 = ring_of(i, S_WV[0]), ring_of(i, S_WV[1])
            for s in range(NS):
                pb = alloc_pair()
                for h, rr in enumerate((r0, r1)):
                    w = wide(rr)
                    pairs = [(hT[:, b, k, s * 128:(s + 1) * 128], w[:, k, :]) for k in range(KC)]
                    mm_group(ps[:, pb + h, :], pairs, reads=[B_hT[b][s], B_ring[rr]], writes=[B_bank[pb + h]])
                    if s == NS - 1:
                        slot_done(i, S_WV[h])
                gq = s % 2
                gv_ap = gvb[:, gq, :]
                pair_ap = ps[:, pb:pb + 2, :].rearrange("p a b -> p (a b)")
                op(ACT, lambda e, gv_ap=gv_ap, pair_ap=pair_ap: e.activation(out=gv_ap, in_=pair_ap, func=AF.Gelu),
                   reads=[B_bank[pb], B_bank[pb + 1]], writes=[B_gvb[gq]])
                r, b_r = rms_scale(gv_ap, B_gvb[gq], D)
                op(DVE, lambda e, gv_ap=gv_ap, r=r, s=s: e.scalar_tensor_tensor(
                    out=vpp[:, s, :], in0=gv_ap, scalar=r, in1=gb[:, 1, :], op0=ALU.mult, op1=ALU.mult),
                   reads=[B_gvb[gq], b_r, B_gb], writes=[B_vpp[s]])

        def P3(i, j):
            b = i % 2
            rr = ring_of(i, S_P3[j])
            hTb = [B_hT[b][s] for s in range(NS)]

            def proj(q, N=T, rhs_fn=None):
                bk = alloc_bank()
                w = blk(rr, q)
                pairs = [(w[:, k, :], hT[:, b, k, :]) for k in range(KC)]
                mm_group(ps[:, bk, :], pairs, reads=hTb + [B_ring[rr]], writes=[B_bank[bk]])
                return bk

            hc_ap, hc_buf = hc_rot.next()
            bX = proj(0)
            if i == 0:
                bH = alloc_bank()
                wx, wc = blk(rr, 0), blk(rr, 1)
                mm_group(ps[:, bH, 0:2], [(wx[:, k, :], hTh[:, k, 0:2]) for k in range(KC)],
                         reads=[B_hTh, B_ring[rr]], writes=[B_bank[bH]])
            bC = proj(1)
            if i == 0:
                mm_group(ps[:, bH, 2:4], [(wc[:, k, :], hTh[:, k, 0:2]) for k in range(KC)],
                         reads=[B_hTh, B_ring[rr]], writes=[B_bank[bH]])
            bB = proj(2)
            bZ = proj(3)
            slot_done(i, S_P3[j])

            xs_ap, xs_buf = tmp_rot.next()
            op(ACT, lambda e: e.activation(out=xs_ap, in_=ps[:, bX, :], func=AF.Copy),
               reads=[B_bank[bX]], writes=[xs_buf])
            if i == 0:
                hx_ap, hx_buf = hxc_rot.next()
                op(ACT, lambda e: e.activation(out=hx_ap, in_=ps[:, bH, 0:4], func=AF.Copy),
                   reads=[B_bank[bH]], writes=[hx_buf])
                op(DVE, lambda e: e.tensor_tensor(out=halo[:, j, :], in0=hx_ap[:, 0:2], in1=hx_ap[:, 2:4],
                                                  op=ALU.mult),
                   reads=[hx_buf], writes=[B_halo[j]])
            op(ACT, lambda e: e.activation(out=hc_ap[:, 0:2], in_=halo[:, j, :], func=AF.Copy),
               reads=[B_halo[j]], writes=[hc_buf])
            op(DVE, lambda e: e.tensor_tensor(out=hc_ap[:, 2:T + 2], in0=xs_ap, in1=ps[:, bC, :], op=ALU.mult),
               reads=[xs_buf, B_bank[bC]], writes=[hc_buf])
            tz_ap, tz_buf = tmp_rot.next()
            op(ACT, lambda e: e.activation(out=tz_ap, in_=ps[:, bZ, :], func=AF.Tanh, scale=0.5),
               reads=[B_bank[bZ]], writes=[tz_buf])
            acc_ap, acc_buf = tmp_rot.next()
            op(ACT, lambda e: e.activation(out=acc_ap, in_=hc_ap[:, 0:T], func=AF.Copy,
                                           scale=convw[:, 3 * j:3 * j + 1]),
               reads=[hc_buf, B_convw], writes=[acc_buf])
            a_ap, a_buf = tmp_rot.next()
            op(DVE, lambda e: e.scalar_tensor_tensor(out=a_ap, in0=tz_ap, scalar=1.0, in1=ps[:, bZ, :],
                                                     op0=ALU.add, op1=ALU.mult),
               reads=[tz_buf, B_bank[bZ]], writes=[a_buf])
            for kk in (1, 2):
                op(DVE, lambda e, kk=kk: e.scalar_tensor_tensor(
                    out=acc_ap, in0=hc_ap[:, kk:kk + T], scalar=convw[:, 3 * j + kk:3 * j + kk + 1], in1=acc_ap,
                    op0=ALU.mult, op1=ALU.add),
                   reads=[hc_buf, acc_buf, B_convw], writes=[acc_buf])
            op(ACT, lambda e: e.activation(out=halo[:, j, :], in_=hc_ap[:, T:T + 2], func=AF.Copy),
               reads=[hc_buf], writes=[B_halo[j]])
            op(DVE, lambda e: e.tensor_tensor(out=acc_ap, in0=acc_ap, in1=ps[:, bB, :], op=ALU.mult),
               reads=[acc_buf, B_bank[bB]], writes=[acc_buf])
            op(POOL, lambda e: e.tensor_tensor(out=yB[:, j, :], in0=acc_ap, in1=a_ap, op=ALU.mult),
               reads=[acc_buf, a_buf], writes=[B_yB[j]])

        def P2(i, g):
            b = i % 2
            rr = ring_of(i, S_P2[g // 2])
            q0 = (g % 2) * 2
            hTb = [B_hT[b][s] for s in range(NS)]
            bS = alloc_bank()

            def sp(e):
                ins = None
                for s in range(NS):
                    ins = e.matmul(ps[:, bS, s * 128:(s + 1) * 128], lhsT=vpp[:, s, g * 128:(g + 1) * 128],
                                   rhs=WT[:, g, :], start=True, stop=True)
                return ins
            op(PE, sp, reads=B_vpp + [B_WT], writes=[B_bank[bS]])
            bU = alloc_bank()
            wu = blk(rr, q0)
            mm_group(ps[:, bU, :], [(wu[:, k, :], hT[:, b, k, :]) for k in range(KC)],
                     reads=hTb + [B_ring[rr]], writes=[B_bank[bU]])
            bZ = alloc_bank()
            wz = blk(rr, q0 + 1)
            mm_group(ps[:, bZ, :], [(wz[:, k, :], hT[:, b, k, :]) for k in range(KC)],
                     reads=hTb + [B_ring[rr]], writes=[B_bank[bZ]])
            if g % 2 == 1:
                slot_done(i, S_P2[g // 2])

            gu_ap, gu_buf = tmp_rot.next()
            op(ACT, lambda e: e.activation(out=gu_ap, in_=ps[:, bU, :], func=AF.Gelu),
               reads=[B_bank[bU]], writes=[gu_buf])
            tz_ap, tz_buf = tmp_rot.next()
            op(ACT, lambda e: e.activation(out=tz_ap, in_=ps[:, bZ, :], func=AF.Tanh, scale=0.5),
               reads=[B_bank[bZ]], writes=[tz_buf])
            m_ap, m_buf = tmp_rot.next()
            op(DVE, lambda e: e.tensor_tensor(
                out=m_ap.rearrange("p (a t) -> p a t", a=NS),
                in0=ps[:, bS, :].rearrange("p (a t) -> p a t", a=NS),
                in1=bt[:, g:g + 1, :].to_broadcast([128, NS, 128]), op=ALU.add),
               reads=[B_bank[bS], B_bt], writes=[m_buf])
            op(DVE, lambda e: e.tensor_tensor(out=m_ap, in0=m_ap, in1=gu_ap, op=ALU.mult),
               reads=[m_buf, gu_buf], writes=[m_buf])
            a_ap, a_buf = tmp_rot.next()
            op(DVE, lambda e: e.scalar_tensor_tensor(out=a_ap, in0=tz_ap, scalar=1.0, in1=ps[:, bZ, :],
                                                     op0=ALU.add, op1=ALU.mult),
               reads=[tz_buf, B_bank[bZ]], writes=[a_buf])
            op(POOL, lambda e: e.tensor_tensor(out=yA[:, g, :], in0=m_ap, in1=a_ap, op=ALU.mult),
               reads=[m_buf, a_buf], writes=[B_yA[g]])

        def P4(i, j):
            b = i % 2
            rr = ring_of(i, S_P4[j])
            hTb = [B_hT[b][s] for s in range(NS)]
            bGA = alloc_bank()
            w = blk(rr, 0)
            mm_group(ps[:, bGA, :], [(w[:, k, :], hT[:, b, k, :]) for k in range(KC)],
                     reads=hTb + [B_ring[rr]], writes=[B_bank[bGA]])
            bGB = alloc_bank()
            w = blk(rr, 1)
            mm_group(ps[:, bGB, :], [(w[:, k, :], hT[:, b, k, :]) for k in range(KC)],
                     reads=hTb + [B_ring[rr]], writes=[B_bank[bGB]])
            bYB = alloc_bank()
            w = blk(rr, 2)
            mm_group(ps[:, bYB, :], [(w[:, k, :], yB[:, k, :]) for k in range(KC)],
                     reads=B_yB + [B_ring[rr]], writes=[B_bank[bYB]])
            bYA = alloc_bank()
            w = blk(rr, 3)
            mm_group(ps[:, bYA, :], [(w[:, k, :], yA[:, k, :]) for k in range(KC)],
                     reads=B_yA + [B_ring[rr]], writes=[B_bank[bYA]])
            slot_done(i, S_P4[j])

            tA_ap, tA_buf = tmp_rot.next()
            op(ACT, lambda e: e.activation(out=tA_ap, in_=ps[:, bGA, :], func=AF.Tanh, scale=0.5),
               reads=[B_bank[bGA]], writes=[tA_buf])
            tB_ap, tB_buf = tmp_rot.next()
            op(ACT, lambda e: e.activation(out=tB_ap, in_=ps[:, bGB, :], func=AF.Tanh, scale=0.5),
               reads=[B_bank[bGB]], writes=[tB_buf])
            m2_ap, m2_buf = tmp_rot.next()
            op(DVE, lambda e: e.scalar_tensor_tensor(out=m2_ap, in0=tB_ap, scalar=1.0, in1=ps[:, bYB, :],
                                                     op0=ALU.add, op1=ALU.mult),
               reads=[tB_buf, B_bank[bYB]], writes=[m2_buf])
            m1_ap, m1_buf = tmp_rot.next()
            op(DVE, lambda e: e.scalar_tensor_tensor(out=m1_ap, in0=tA_ap, scalar=1.0, in1=ps[:, bYA, :],
                                                     op0=ALU.add, op1=ALU.mult),
               reads=[tA_buf, B_bank[bYA]], writes=[m1_buf])
            op(POOL, lambda e: e.tensor_tensor(out=mg[:, j, :], in0=m1_ap, in1=m2_ap, op=ALU.add),
               reads=[m1_buf, m2_buf], writes=[B_mg[j]])

        def P5(i):
            b = i % 2
            r0, r1 = ring_of(i, S_WO[0]), ring_of(i, S_WO[1])
            for s in range(NS):
                pb = alloc_pair()
                for h, rr in enumerate((r0, r1)):
                    w = wide(rr)
                    pairs = [(mg[:, k, s * 128:(s + 1) * 128], w[:, k, :]) for k in range(KC)]
                    mm_group(ps[:, pb + h, :], pairs, reads=B_mg + [B_ring[rr]], writes=[B_bank[pb + h]])
                    if s == NS - 1:
                        slot_done(i, S_WO[h])
                x_ap = xbuf[:, b, s, :]
                pair_ap = ps[:, pb:pb + 2, :].rearrange("p a b -> p (a b)")
                op(DVE, lambda e, x_ap=x_ap, pair_ap=pair_ap: e.scalar_tensor_tensor(
                    out=x_ap, in0=pair_ap, scalar=0.25, in1=x_ap, op0=ALU.mult, op1=ALU.add),
                   reads=[B_bank[pb], B_bank[pb + 1], B_x[b][s]], writes=[B_x[b][s]])
                r, b_r = rms_scale(x_ap, B_x[b][s], D)
                op(DVE, lambda e, x_ap=x_ap, r=r: e.scalar_tensor_tensor(
                    out=x_ap, in0=x_ap, scalar=r, in1=gb[:, 2, :], op0=ALU.mult, op1=ALU.mult),
                   reads=[B_x[b][s], b_r, B_gb], writes=[B_x[b][s]])

        load_x(0, 0)
        op(POOL, lambda e: e.memset(mhalf[:], -0.5), writes=[B_mhalf])
        op(ACT, lambda e: e.activation(out=junk[:, 0:1], in_=mhalf[:, 0:1], func=AF.Square),
           reads=[B_mhalf], writes=[B_junk])
        op(POOL, lambda e: e.memset(ones[:], 1.0), writes=[B_ones])
        op(POOL, lambda e: e.memset(onesb[:], 1.0), writes=[B_onesb])
        op(POOL, lambda e: e.memset(xhb[:], 0.0), writes=[B_xhb])
        op(POOL, lambda e: e.affine_select(out=ident[:], in_=ones[:], pattern=[[1, 128]], compare_op=ALU.is_equal,
                                           fill=0.0, base=0, channel_multiplier=-1),
           reads=[B_ones], writes=[B_ident])
        dma(SP, lambda e: e.dma_start(out=gb[:, 0, :], in_=gains_d[:, 0:D].partition_broadcast(128)[:, 0, :]),
            sem_g0, writes=[B_gb0])
        for s in range(1, NS):
            load_x(0, s)
        dma(SP, lambda e: e.dma_start(out=gT[:], in_=gT_d), sem_gT, writes=[B_gT])
        qT = {}
        for k in range(3):
            qT[k] = loadT(0, k)
        for _ in range(2):
            issue_load()
        dma(SP, lambda e: e.dma_start(out=xhb[0:2, :], in_=xh_d), sem_setup, reads=[B_ring[1]], writes=[B_xhb])
        dma(SP, lambda e: e.dma_start(out=gb[:, 1:3, :].rearrange("p a d -> p (a d)"),
                                      in_=gains_d[:, D:3 * D].partition_broadcast(128)[:, 0, :]),
            sem_setup, writes=[B_gb])
        dma(SP, lambda e: e.dma_start(out=convw[:], in_=convw_d), sem_setup2, writes=[B_convw])
        dma(SP, lambda e: e.dma_start(out=wsT32[:].rearrange("p a d -> p (a d)"), in_=wsT_d), sem_setup2,
            writes=[B_wsT32])
        dma(SP, lambda e: e.dma_start(out=bt[:].rearrange("p a d -> p (a d)"),
                                      in_=bsp_d.partition_broadcast(128)[:, 0, :]), sem_setup2, writes=[B_bt])
        for b_ in (B_xhb, B_gb):
            b_.writer = (sem_setup, sem_setup.count)
        B_ring[1].readers[sem_setup] = sem_setup.count
        for b_ in (B_wsT32, B_bt, B_convw):
            b_.writer = (sem_setup2, sem_setup2.count)

        HK = "h"
        P0A(0, 0)
        P0A(0, 1)
        prepB2(0, 0)
        P0A(0, 2)
        prepB2(0, 1)
        P0A(0, 3)
        prepB2(0, 2)
        prepB2(0, 3)
        Rmm(0)
        for k in range(KC):
            C2(0, k, qT[k])
            if k + 3 < KC:
                qT[k + 3] = loadT(0, k + 3)
        for _ in range(RING - 2):
            issue_load()
        op(POOL, lambda e: e.affine_select(out=WT[:], in_=wsT32[:], pattern=[[0, 8], [1, 128]],
                                           compare_op=ALU.is_ge, fill=0.0, base=0, channel_multiplier=-1),
           reads=[B_wsT32], writes=[B_WT])
        for s in range(NS):
            load_x(1, s)

        P1(0)
        prepA(HK, xhb[:], B_xhb)
        prepB(HK, xhb[:], B_xhb)
        prepC(HK, hTh[:], B_hTh)
        for i in range(NT):
            for j in range(KC):
                P3(i, j)
                if i >= 1 and j == 1:
                    for s in range(NS):
                        store_y(i - 1, s)
                    if i + 1 < NT:
                        for s in range(NS):
                            load_x(i + 1, s)
                if i + 1 < NT:
                    if 4 <= j <= 7:
                        prepB2(i + 1, j - 4)
                    if 3 <= j <= 6:
                        P0A(i + 1, j - 3)
                    if j == 6:
                        qT = {0: loadT(i + 1, 0)}
                    if j == 7:
                        qT[1] = loadT(i + 1, 1)
            for g in range(KC):
                P2(i, g)
                if i + 1 < NT:
                    if g == 0:
                        Rmm(i + 1)
                    if g + 2 < KC:
                        qT[g + 2] = loadT(i + 1, g + 2)
                    C2(i + 1, g, qT[g])
            for j in range(KC):
                P4(i, j)
            if i + 1 < NT:
                P1(i + 1)
            P5(i)
        for s in range(NS):
            store_y(NT - 1, s)
        fin = []
        for b in range(2):
            for s in range(NS):
                fin.append((sem_st[b][s], sem_st[b][s].count))

        with nc.Block() as block:
            @block.sync
            def _(e):
                emit(SP, e)
                for s_, v in fin:
                    e.wait_ge(s_.h, v)

            @block.tensor
            def _(e):
                emit(PE, e)

            @block.scalar
            def _(e):
                emit(ACT, e)

            @block.vector
            def _(e):
                emit(DVE, e)

            @block.gpsimd
            def _(e):
                emit(POOL, e)
    return nc


def _blocks(w, col0, ncols):
    return w[:, col0:col0 + ncols].reshape(KC, 128, ncols).transpose(1, 0, 2)


def build_stream(w_in, w_a, w_b, w_out):
    st = np.empty((NSLOT, 128, SLOT_ELEMS), dtype=np.float32)
    U, V, ZA, XB, CB, BB, ZB, GA, GB = [q * D for q in range(9)]
    for h in range(2):
        st[S_WV[h]] = _blocks(w_in, V + h * 512, 512).reshape(128, SLOT_ELEMS)
        st[S_WO[h]] = _blocks(w_out, h * 512, 512).reshape(128, SLOT_ELEMS)
    for j in range(KC):
        c = j * 128
        st[S_P3[j]] = np.stack([_blocks(w_in, XB + c, 128), _blocks(w_in, CB + c, 128),
                                _blocks(w_in, BB + c, 128), _blocks(w_in, ZB + c, 128)], axis=1).reshape(128, SLOT_ELEMS)
        st[S_P4[j]] = np.stack([_blocks(w_in, GA + c, 128), _blocks(w_in, GB + c, 128),
                                _blocks(w_b, c, 128), _blocks(w_a, c, 128)], axis=1).reshape(128, SLOT_ELEMS)
    for q in range(4):
        g0, g1 = 2 * q, 2 * q + 1
        st[S_P2[q]] = np.stack([_blocks(w_in, U + g0 * 128, 128), _blocks(w_in, ZA + g0 * 128, 128),
                                _blocks(w_in, U + g1 * 128, 128), _blocks(w_in, ZA + g1 * 128, 128)],
                               axis=1).reshape(128, SLOT_ELEMS)
    return st


_NC_CACHE = {}


def kernel(x, norm_g, w_in, v_norm_g, w_spatial, b_spatial, conv_w, w_branch_a, w_branch_b, w_out, final_norm_g):
    x = np.asarray(x, dtype=np.float32)
    B, S, _ = x.shape
    assert (B, S) == (4, 8192)
    w_in = np.asarray(w_in, np.float32)[0]
    wst = build_stream(w_in, np.asarray(w_branch_a, np.float32)[0], np.asarray(w_branch_b, np.float32)[0],
                       np.asarray(w_out, np.float32)[0])
    gains = np.concatenate([np.asarray(norm_g, np.float32)[0], np.asarray(v_norm_g, np.float32)[0],
                            np.asarray(final_norm_g, np.float32)]).reshape(1, 3 * D)
    bsp = np.asarray(b_spatial, np.float32)[0].reshape(1, 8 * 128)
    convw = np.ascontiguousarray(np.asarray(conv_w, np.float32)[0].reshape(3, KC, 128).transpose(2, 1, 0)).reshape(128, 24)
    wsT = np.ascontiguousarray(np.asarray(w_spatial, np.float32)[0].transpose(2, 0, 1)).reshape(128, 8 * 128)

    gTh = np.ascontiguousarray(np.asarray(norm_g, np.float32)[0].reshape(KC, 128).T)
    in_maps = []
    for c in range(NCORES):
        b, half = c // 2, c % 2
        xc = np.ascontiguousarray(x[b, half * TOK:(half + 1) * TOK, :])
        if half == 0:
            xh = np.zeros((2, D), np.float32)
        else:
            xh = np.ascontiguousarray(x[b, TOK - 2:TOK, :])
        xT = np.ascontiguousarray(xc.T)
        in_maps.append({"x": xc, "xh": xh, "xT": xT, "gT": gTh, "wst": wst, "gains": gains, "bsp": bsp, "convw": convw, "wsT": wsT})

    if "nc" not in _NC_CACHE:
        _NC_CACHE["nc"] = build_program()
    nc = _NC_CACHE["nc"]
    res = run_bass_kernel_spmd(nc, in_maps, core_ids=list(range(NCORES)))
    out = np.empty((B, S, D), np.float32)
    for c in range(NCORES):
        b, half = c // 2, c % 2
        out[b, half * TOK:(half + 1) * TOK, :] = res.results[c]["y"]
    return out
```

```python
import numpy as np
from contextlib import ExitStack

import concourse.bass as bass
import concourse.mybir as mybir
from concourse.bass_utils import run_bass_kernel_spmd

F32 = mybir.dt.float32
BF16 = mybir.dt.bfloat16
AF = mybir.ActivationFunctionType
ALU = mybir.AluOpType

D = 1024
NCORES = 8
TOK = 4096
T = 512
NT = TOK // T
NS = 4
KC = 8
EPS = 1e-6
RING = 4
NSLOT = 24
SLOT_ELEMS = 4096
WB_SPREAD = 4

S_WV = [0, 1]
S_P3 = list(range(2, 10))
S_P2 = list(range(10, 14))
S_P4 = list(range(14, 22))
S_WO = [22, 23]


class Sem:
    def __init__(self, h, name):
        self.h = h
        self.name = name
        self.count = 0


class Buf:
    __slots__ = ("name", "writer", "readers")

    def __init__(self, name):
        self.name = name
        self.writer = None
        self.readers = {}


class Eng:
    def __init__(self, name, sem, self_sync=True):
        self.name = name
        self.sem = sem
        self.ops = []
        self.waited = {}
        self.self_sync = self_sync


def _collect_waits(eng, reads, writes):
    deps = {}

    def add(tok):
        if tok is None:
            return
        s, v = tok
        if deps.get(s, 0) < v:
            deps[s] = v

    for b in reads:
        add(b.writer)
    for b in writes:
        add(b.writer)
        for s, v in b.readers.items():
            add((s, v))
    waits = []
    for s, v in deps.items():
        if s is eng.sem and not eng.self_sync:
            continue
        if eng.waited.get(s, 0) < v:
            eng.waited[s] = v
            waits.append((s, v))
    return waits


def _commit(tok, reads, writes):
    s, v = tok
    for b in reads:
        if b.readers.get(s, 0) < v:
            b.readers[s] = v
    for b in writes:
        b.writer = tok
        b.readers = {}


def op(eng, fn, reads=(), writes=()):
    waits = _collect_waits(eng, reads, writes)
    eng.sem.count += 1
    tok = (eng.sem, eng.sem.count)
    eng.ops.append((waits, fn, eng.sem, 1))
    _commit(tok, reads, writes)
    return tok


def dma(eng, fn, dsem, reads=(), writes=()):
    waits = _collect_waits(eng, reads, writes)
    dsem.count += 16
    tok = (dsem, dsem.count)
    eng.ops.append((waits, fn, dsem, 16))
    _commit(tok, reads, writes)
    return tok


def emit(eng, e):
    for waits, fn, sem, inc in eng.ops:
        for s, v in waits:
            e.wait_ge(s.h, v)
        ins = fn(e)
        ins.then_inc(sem.h, inc)


class Rot:
    def __init__(self, items):
        self.items = items
        self.i = 0

    def next(self):
        it = self.items[self.i % len(self.items)]
        self.i += 1
        return it


def build_program():
    nc = bass.Bass("TRN2", target_bir_lowering=False)
    x_d = nc.dram_tensor("x", [TOK, D], F32, kind="ExternalInput").ap()
    xh_d = nc.dram_tensor("xh", [2, D], F32, kind="ExternalInput").ap()
    wst_d = nc.dram_tensor("wst", [NSLOT, 128, SLOT_ELEMS], F32, kind="ExternalInput").ap()
    gains_d = nc.dram_tensor("gains", [1, 3 * D], F32, kind="ExternalInput").ap()
    bsp_d = nc.dram_tensor("bsp", [1, 8 * 128], F32, kind="ExternalInput").ap()
    convw_d = nc.dram_tensor("convw", [128, 24], F32, kind="ExternalInput").ap()
    wsT_d = nc.dram_tensor("wsT", [128, 8 * 128], F32, kind="ExternalInput").ap()
    y_d = nc.dram_tensor("y", [TOK, D], F32, kind="ExternalOutput").ap()
    wbf_d = nc.dram_tensor("wbf", [NSLOT, 128, SLOT_ELEMS], BF16).ap()

    with ExitStack() as es:
        def sb(name, shape, dt):
            return es.enter_context(nc.sbuf_tensor("sb_" + name, shape, dt))

        def new_sem(name):
            return Sem(es.enter_context(nc.semaphore(name)), name)

        xbuf = sb("xbuf", [128, 2, NS, D], F32)
        xsb = sb("xsb", [128, 3, D], BF16)
        hT = sb("hT", [128, 2, KC, T], BF16)
        hTh = sb("hTh", [128, KC, 128], BF16)
        xhb = sb("xhb", [128, D], F32)
        gvb = sb("gvb", [128, 2, D], F32)
        junk = sb("junk", [128, D], BF16)
        vpp = sb("vpp", [128, NS, D], BF16)
        yA = sb("yA", [128, KC, T], BF16)
        yB = sb("yB", [128, KC, T], BF16)
        mg = sb("mg", [128, KC, T], BF16)
        NTMP = 19
        tmp = sb("tmp", [128, NTMP, T], F32)
        hcb = sb("hcb", [128, 2, T + 4], F32)
        gb = sb("gb", [128, 3, D], F32)
        bt = sb("bt", [128, 8, 128], F32)
        wsT32 = sb("wsT32", [128, 8, 128], F32)
        WT = sb("WT", [128, 8, 128], BF16)
        ident = sb("ident", [128, 128], BF16)
        ones = sb("ones", [128, 128], F32)
        convw = sb("convw", [128, 24], F32)
        halo = sb("halo", [128, 8, 2], F32)
        tiny = sb("tiny", [128, 3, 16], F32)
        mhalf = sb("mhalf", [128, 1], F32)
        hxc = sb("hxc", [128, 2, 4], F32)
        ring = sb("ring", [128, RING, SLOT_ELEMS], BF16)

        ps = es.enter_context(nc.psum_tensor("ps_main", [128, 7, 512], F32))
        psT = es.enter_context(nc.psum_tensor("ps_tr", [128, D], BF16))

        PE = Eng("pe", new_sem("s_pe"), self_sync=False)
        ACT = Eng("act", new_sem("s_act"))
        DVE = Eng("dve", new_sem("s_dve"))
        POOL = Eng("pool", new_sem("s_pool"))
        SP = Eng("sp", new_sem("s_sp"))
        sem_setup = new_sem("d_setup")
        sem_setup2 = new_sem("d_setup2")
        sem_g0 = new_sem("d_g0")
        sem_ring = [new_sem(f"d_ring{r}") for r in range(RING)]
        sem_xl = [[new_sem(f"d_xl{b}{s}") for s in range(NS)] for b in range(2)]
        sem_st = [[new_sem(f"d_st{b}{s}") for s in range(NS)] for b in range(2)]
        sem_wb = [new_sem(f"d_wb{r}") for r in range(RING)]
        sem_ringc = [new_sem(f"d_ringc{r}") for r in range(RING)]

        B_x = [[Buf(f"x{b}{s}") for s in range(NS)] for b in range(2)]
        B_xsb = [Buf(f"xsb{b}") for b in range(3)]
        B_hT = [[Buf(f"hT{b}{s}") for s in range(NS)] for b in range(2)]
        B_hTh = Buf("hTh")
        B_xhb = Buf("xhb")
        B_gvb = [Buf(f"gvb{b}") for b in range(2)]
        B_vpp = [Buf(f"vpp{s}") for s in range(NS)]
        B_yA = [Buf(f"yA{j}") for j in range(KC)]
        B_yB = [Buf(f"yB{j}") for j in range(KC)]
        B_mg = [Buf(f"mg{j}") for j in range(KC)]
        B_bank = [Buf(f"bank{b}") for b in range(7)]
        B_psT = Buf("psT")
        B_ring = [Buf(f"ring{r}") for r in range(RING)]
        B_wbf = [Buf(f"wbf{k}") for k in range(NSLOT)]
        B_gb = Buf("gb")
        B_gb0 = Buf("gb0")
        B_bt = Buf("bt")
        B_wsT32 = Buf("wsT32")
        B_WT = Buf("WT")
        B_ident = Buf("ident")
        B_ones = Buf("ones")
        B_convw = Buf("convw")
        B_halo = [Buf(f"halo{j}") for j in range(KC)]
        B_mhalf = Buf("mhalf")
        B_junk = Buf("junk")

        tmp_rot = Rot([(tmp[:, q, :], Buf(f"tmp{q}")) for q in range(NTMP)])
        hc_rot = Rot([(hcb[:, q, :], Buf(f"hc{q}")) for q in range(2)])
        hxc_rot = Rot([(hxc[:, q, :], Buf(f"hxc{q}")) for q in range(2)])
        tiny_rot = Rot([((tiny[:, 0, q:q + 1], tiny[:, 1, q:q + 1], tiny[:, 2, q:q + 1]),
                         (Buf(f"ss{q}"), Buf(f"ms{q}"), Buf(f"r{q}"))) for q in range(16)])

        bank_ptr = [0]

        def alloc_bank():
            b = bank_ptr[0] % 7
            bank_ptr[0] += 1
            return b

        def alloc_pair():
            while (bank_ptr[0] % 7) % 2 == 1 or (bank_ptr[0] % 7) == 6:
                bank_ptr[0] += 1
            b = bank_ptr[0] % 7
            bank_ptr[0] += 2
            return b

        stream = []
        stream += [(0, s) for s in S_WV]
        for i in range(NT):
            stream += [(i, s) for s in S_P3 + S_P2 + S_P4]
            if i + 1 < NT:
                stream += [(i + 1, s) for s in S_WV]
            stream += [(i, s) for s in S_WO]
        stream_pos = {ts: n for n, ts in enumerate(stream)}
        next_load = [0]

        pending_wb = []

        def flush_wb(upto):
            while pending_wb and pending_wb[0][0] <= upto:
                _, r, slot = pending_wb.pop(0)
                dma(SP, lambda e, r=r, slot=slot: e.dma_start(out=wbf_d[slot], in_=ring[:, r, :]),
                    sem_wb[r], reads=[B_ring[r]], writes=[B_wbf[slot]])

        def issue_load():
            n = next_load[0]
            flush_wb(n - 2)
            if n >= len(stream):
                return
            next_load[0] += 1
            ti, slot = stream[n]
            r = n % RING
            wbt = slot % WB_SPREAD
            if ti <= wbt:
                extra = []
                dma(POOL, lambda e, r=r, slot=slot: e.dma_start(out=ring[:, r, :], in_=wst_d[slot]),
                    sem_ringc[r], reads=extra, writes=[B_ring[r]])
                if ti == wbt:
                    pending_wb.append((n, r, slot))
            else:
                dma(SP, lambda e, r=r, slot=slot: e.dma_start(out=ring[:, r, :], in_=wbf_d[slot]),
                    sem_ring[r], reads=[B_wbf[slot]], writes=[B_ring[r]])

        def ring_of(i, slot):
            return stream_pos[(i, slot)] % RING

        def slot_done(i, slot):
            n = stream_pos[(i, slot)]
            assert next_load[0] == n + RING or next_load[0] >= len(stream), (next_load[0], n)
            issue_load()

        def rms_scale(src_ap, src_buf, width):
            (ss, ms, r), (b_ss, b_ms, b_r) = tiny_rot.next()
            op(ACT, lambda e: e.activation(out=junk[:, 0:width], in_=src_ap, func=AF.Square, accum_out=ss),
               reads=[src_buf], writes=[b_ss, B_junk])
            op(DVE, lambda e: e.tensor_scalar(out=ms, in0=ss, scalar1=1.0 / width, scalar2=EPS,
                                              op0=ALU.mult, op1=ALU.add),
               reads=[b_ss], writes=[b_ms])
            op(POOL, lambda e: e.tensor_tensor(out=r, in0=ms, in1=mhalf[:, 0:1], op=ALU.pow),
               reads=[b_ms, B_mhalf], writes=[b_r])
            return r, b_r

        def mm_group(out_ap, pairs, reads, writes):
            n = len(pairs)

            def fn(e):
                ins = None
                for q, (l, r_) in enumerate(pairs):
                    ins = e.matmul(out_ap, lhsT=l, rhs=r_, start=(q == 0), stop=(q == n - 1))
                return ins
            return op(PE, fn, reads=reads, writes=writes)

        def blk(r, q):
            return ring[:, r, q * 1024:(q + 1) * 1024].rearrange("p (k c) -> p k c", k=KC)

        def wide(r):
            return ring[:, r, :].rearrange("p (k c) -> p k c", k=KC)

        p0_state = {}
        xs_ctr = [0]

        def prepA(key, x_ap, x_buf):
            p0_state[key] = rms_scale(x_ap, x_buf, D)

        def prepB(key, x_ap, x_buf):
            r, b_r = p0_state[key]
            q = xs_ctr[0] % 3
            xs_ctr[0] += 1
            p0_state[key] = q
            op(DVE, lambda e: e.scalar_tensor_tensor(out=xsb[:, q, :], in0=x_ap, scalar=r, in1=gb[:, 0, :],
                                                     op0=ALU.mult, op1=ALU.mult),
               reads=[x_buf, b_r, B_gb0], writes=[B_xsb[q]])

        def prepC(key, hT_out_ap, hT_buf):
            q = p0_state.pop(key)

            def tr(e):
                ins = None
                for k in range(KC):
                    ins = e.transpose(psT[:, k * 128:(k + 1) * 128], xsb[:, q, k * 128:(k + 1) * 128], ident[:])
                return ins
            op(PE, tr, reads=[B_xsb[q], B_ident], writes=[B_psT])
            op(ACT, lambda e: e.activation(out=hT_out_ap, in_=psT[:].rearrange("p (k t) -> p k t", k=KC),
                                           func=AF.Copy),
               reads=[B_psT], writes=[hT_buf])

        def P0A(i, s):
            prepA((i, s), xbuf[:, i % 2, s, :], B_x[i % 2][s])

        def P0B(i, s):
            prepB((i, s), xbuf[:, i % 2, s, :], B_x[i % 2][s])

        def P0C(i, s):
            prepC((i, s), hT[:, i % 2, :, s * 128:(s + 1) * 128], B_hT[i % 2][s])

        def load_x(i, s):
            b = i % 2
            row0 = i * T + s * 128
            dma(SP, lambda e: e.dma_start(out=xbuf[:, b, s, :], in_=x_d[row0:row0 + 128, :]),
                sem_xl[b][s], reads=[], writes=[B_x[b][s]])

        def store_y(i, s):
            b = i % 2
            row0 = i * T + s * 128
            dma(SP, lambda e: e.dma_start(out=y_d[row0:row0 + 128, :], in_=xbuf[:, b, s, :]),
                sem_st[b][s], reads=[B_x[b][s]], writes=[])

        def P1(i):
            b = i % 2
            r0, r1 = ring_of(i, S_WV[0]), ring_of(i, S_WV[1])
            for s in range(NS):
                pb = alloc_pair()
                for h, rr in enumerate((r0, r1)):
                    w = wide(rr)
                    pairs = [(hT[:, b, k, s * 128:(s + 1) * 128], w[:, k, :]) for k in range(KC)]
                    mm_group(ps[:, pb + h, :], pairs, reads=[B_hT[b][s], B_ring[rr]], writes=[B_bank[pb + h]])
                    if s == NS - 1:
                        slot_done(i, S_WV[h])
                gq = s % 2
                gv_ap = gvb[:, gq, :]
                pair_ap = ps[:, pb:pb + 2, :].rearrange("p a b -> p (a b)")
                op(ACT, lambda e, gv_ap=gv_ap, pair_ap=pair_ap: e.activation(out=gv_ap, in_=pair_ap, func=AF.Gelu),
                   reads=[B_bank[pb], B_bank[pb + 1]], writes=[B_gvb[gq]])
                r, b_r = rms_scale(gv_ap, B_gvb[gq], D)
                op(DVE, lambda e, gv_ap=gv_ap, r=r, s=s: e.scalar_tensor_tensor(
                    out=vpp[:, s, :], in0=gv_ap, scalar=r, in1=gb[:, 1, :], op0=ALU.mult, op1=ALU.mult),
                   reads=[B_gvb[gq], b_r, B_gb], writes=[B_vpp[s]])

        def P3(i, j):
            b = i % 2
            rr = ring_of(i, S_P3[j])
            hTb = [B_hT[b][s] for s in range(NS)]

            def proj(q, N=T, rhs_fn=None):
                bk = alloc_bank()
                w = blk(rr, q)
                pairs = [(w[:, k, :], hT[:, b, k, :]) for k in range(KC)]
                mm_group(ps[:, bk, :], pairs, reads=hTb + [B_ring[rr]], writes=[B_bank[bk]])
                return bk

            hc_ap, hc_buf = hc_rot.next()
            bX = proj(0)
            if i == 0:
                bH = alloc_bank()
                wx, wc = blk(rr, 0), blk(rr, 1)
                mm_group(ps[:, bH, 0:2], [(wx[:, k, :], hTh[:, k, 0:2]) for k in range(KC)],
                         reads=[B_hTh, B_ring[rr]], writes=[B_bank[bH]])
            bC = proj(1)
            if i == 0:
                mm_group(ps[:, bH, 2:4], [(wc[:, k, :], hTh[:, k, 0:2]) for k in range(KC)],
                         reads=[B_hTh, B_ring[rr]], writes=[B_bank[bH]])
            bB = proj(2)
            bZ = proj(3)
            slot_done(i, S_P3[j])

            xs_ap, xs_buf = tmp_rot.next()
            op(ACT, lambda e: e.activation(out=xs_ap, in_=ps[:, bX, :], func=AF.Copy),
               reads=[B_bank[bX]], writes=[xs_buf])
            if i == 0:
                hx_ap, hx_buf = hxc_rot.next()
                op(ACT, lambda e: e.activation(out=hx_ap, in_=ps[:, bH, 0:4], func=AF.Copy),
                   reads=[B_bank[bH]], writes=[hx_buf])
                op(DVE, lambda e: e.tensor_tensor(out=halo[:, j, :], in0=hx_ap[:, 0:2], in1=hx_ap[:, 2:4],
                                                  op=ALU.mult),
                   reads=[hx_buf], writes=[B_halo[j]])
            op(ACT, lambda e: e.activation(out=hc_ap[:, 0:2], in_=halo[:, j, :], func=AF.Copy),
               reads=[B_halo[j]], writes=[hc_buf])
            op(DVE, lambda e: e.tensor_tensor(out=hc_ap[:, 2:T + 2], in0=xs_ap, in1=ps[:, bC, :], op=ALU.mult),
               reads=[xs_buf, B_bank[bC]], writes=[hc_buf])
            tz_ap, tz_buf = tmp_rot.next()
            op(ACT, lambda e: e.activation(out=tz_ap, in_=ps[:, bZ, :], func=AF.Tanh, scale=0.5),
               reads=[B_bank[bZ]], writes=[tz_buf])
            acc_ap, acc_buf = tmp_rot.next()
            op(ACT, lambda e: e.activation(out=acc_ap, in_=hc_ap[:, 0:T], func=AF.Copy,
                                           scale=convw[:, 3 * j:3 * j + 1]),
               reads=[hc_buf, B_convw], writes=[acc_buf])
            a_ap, a_buf = tmp_rot.next()
            op(DVE, lambda e: e.scalar_tensor_tensor(out=a_ap, in0=tz_ap, scalar=1.0, in1=ps[:, bZ, :],
                                                     op0=ALU.add, op1=ALU.mult),
               reads=[tz_buf, B_bank[bZ]], writes=[a_buf])
            for kk in (1, 2):
                op(DVE, lambda e, kk=kk: e.scalar_tensor_tensor(
                    out=acc_ap, in0=hc_ap[:, kk:kk + T], scalar=convw[:, 3 * j + kk:3 * j + kk + 1], in1=acc_ap,
                    op0=ALU.mult, op1=ALU.add),
                   reads=[hc_buf, acc_buf, B_convw], writes=[acc_buf])
            op(ACT, lambda e: e.activation(out=halo[:, j, :], in_=hc_ap[:, T:T + 2], func=AF.Copy),
               reads=[hc_buf], writes=[B_halo[j]])
            op(DVE, lambda e: e.tensor_tensor(out=acc_ap, in0=acc_ap, in1=ps[:, bB, :], op=ALU.mult),
               reads=[acc_buf, B_bank[bB]], writes=[acc_buf])
            op(POOL, lambda e: e.tensor_tensor(out=yB[:, j, :], in0=acc_ap, in1=a_ap, op=ALU.mult),
               reads=[acc_buf, a_buf], writes=[B_yB[j]])

        def P2(i, g):
            b = i % 2
            rr = ring_of(i, S_P2[g // 2])
            q0 = (g % 2) * 2
            hTb = [B_hT[b][s] for s in range(NS)]
            bS = alloc_bank()

            def sp(e):
                ins = None
                for s in range(NS):
                    ins = e.matmul(ps[:, bS, s * 128:(s + 1) * 128], lhsT=vpp[:, s, g * 128:(g + 1) * 128],
                                   rhs=WT[:, g, :], start=True, stop=True)
                return ins
            op(PE, sp, reads=B_vpp + [B_WT], writes=[B_bank[bS]])
            bU = alloc_bank()
            wu = blk(rr, q0)
            mm_group(ps[:, bU, :], [(wu[:, k, :], hT[:, b, k, :]) for k in range(KC)],
                     reads=hTb + [B_ring[rr]], writes=[B_bank[bU]])
            bZ = alloc_bank()
            wz = blk(rr, q0 + 1)
            mm_group(ps[:, bZ, :], [(wz[:, k, :], hT[:, b, k, :]) for k in range(KC)],
                     reads=hTb + [B_ring[rr]], writes=[B_bank[bZ]])
            if g % 2 == 1:
                slot_done(i, S_P2[g // 2])

            gu_ap, gu_buf = tmp_rot.next()
            op(ACT, lambda e: e.activation(out=gu_ap, in_=ps[:, bU, :], func=AF.Gelu),
               reads=[B_bank[bU]], writes=[gu_buf])
            tz_ap, tz_buf = tmp_rot.next()
            op(ACT, lambda e: e.activation(out=tz_ap, in_=ps[:, bZ, :], func=AF.Tanh, scale=0.5),
               reads=[B_bank[bZ]], writes=[tz_buf])
            m_ap, m_buf = tmp_rot.next()
            op(DVE, lambda e: e.tensor_tensor(
                out=m_ap.rearrange("p (a t) -> p a t", a=NS),
                in0=ps[:, bS, :].rearrange("p (a t) -> p a t", a=NS),
                in1=bt[:, g:g + 1, :].to_broadcast([128, NS, 128]), op=ALU.add),
               reads=[B_bank[bS], B_bt], writes=[m_buf])
            op(DVE, lambda e: e.tensor_tensor(out=m_ap, in0=m_ap, in1=gu_ap, op=ALU.mult),
               reads=[m_buf, gu_buf], writes=[m_buf])
            a_ap, a_buf = tmp_rot.next()
            op(DVE, lambda e: e.scalar_tensor_tensor(out=a_ap, in0=tz_ap, scalar=1.0, in1=ps[:, bZ, :],
                                                     op0=ALU.add, op1=ALU.mult),
               reads=[tz_buf, B_bank[bZ]], writes=[a_buf])
            op(POOL, lambda e: e.tensor_tensor(out=yA[:, g, :], in0=m_ap, in1=a_ap, op=ALU.mult),
               reads=[m_buf, a_buf], writes=[B_yA[g]])

        def P4(i, j):
            b = i % 2
            rr = ring_of(i, S_P4[j])
            hTb = [B_hT[b][s] for s in range(NS)]
            bGA = alloc_bank()
            w = blk(rr, 0)
            mm_group(ps[:, bGA, :], [(w[:, k, :], hT[:, b, k, :]) for k in range(KC)],
                     reads=hTb + [B_ring[rr]], writes=[B_bank[bGA]])
            bGB = alloc_bank()
            w = blk(rr, 1)
            mm_group(ps[:, bGB, :], [(w[:, k, :], hT[:, b, k, :]) for k in range(KC)],
                     reads=hTb + [B_ring[rr]], writes=[B_bank[bGB]])
            bYB = alloc_bank()
            w = blk(rr, 2)
            mm_group(ps[:, bYB, :], [(w[:, k, :], yB[:, k, :]) for k in range(KC)],
                     reads=B_yB + [B_ring[rr]], writes=[B_bank[bYB]])
            bYA = alloc_bank()
            w = blk(rr, 3)
            mm_group(ps[:, bYA, :], [(w[:, k, :], yA[:, k, :]) for k in range(KC)],
                     reads=B_yA + [B_ring[rr]], writes=[B_bank[bYA]])
            slot_done(i, S_P4[j])

            tA_ap, tA_buf = tmp_rot.next()
            op(ACT, lambda e: e.activation(out=tA_ap, in_=ps[:, bGA, :], func=AF.Tanh, scale=0.5),
               reads=[B_bank[bGA]], writes=[tA_buf])
            tB_ap, tB_buf = tmp_rot.next()
            op(ACT, lambda e: e.activation(out=tB_ap, in_=ps[:, bGB, :], func=AF.Tanh, scale=0.5),
               reads=[B_bank[bGB]], writes=[tB_buf])
            m2_ap, m2_buf = tmp_rot.next()
            op(DVE, lambda e: e.scalar_tensor_tensor(out=m2_ap, in0=tB_ap, scalar=1.0, in1=ps[:, bYB, :],
                                                     op0=ALU.add, op1=ALU.mult),
               reads=[tB_buf, B_bank[bYB]], writes=[m2_buf])
            m1_ap, m1_buf = tmp_rot.next()
            op(DVE, lambda e: e.scalar_tensor_tensor(out=m1_ap, in0=tA_ap, scalar=1.0, in1=ps[:, bYA, :],
                                                     op0=ALU.add, op1=ALU.mult),
               reads=[tA_buf, B_bank[bYA]], writes=[m1_buf])
            op(POOL, lambda e: e.tensor_tensor(out=mg[:, j, :], in0=m1_ap, in1=m2_ap, op=ALU.add),
               reads=[m1_buf, m2_buf], writes=[B_mg[j]])

        def P5(i):
            b = i % 2
            r0, r1 = ring_of(i, S_WO[0]), ring_of(i, S_WO[1])
            for s in range(NS):
                pb = alloc_pair()
                for h, rr in enumerate((r0, r1)):
                    w = wide(rr)
                    pairs = [(mg[:, k, s * 128:(s + 1) * 128], w[:, k, :]) for k in range(KC)]
                    mm_group(ps[:, pb + h, :], pairs, reads=B_mg + [B_ring[rr]], writes=[B_bank[pb + h]])
                    if s == NS - 1:
                        slot_done(i, S_WO[h])
                x_ap = xbuf[:, b, s, :]
                pair_ap = ps[:, pb:pb + 2, :].rearrange("p a b -> p (a b)")
                op(DVE, lambda e, x_ap=x_ap, pair_ap=pair_ap: e.scalar_tensor_tensor(
                    out=x_ap, in0=pair_ap, scalar=0.25, in1=x_ap, op0=ALU.mult, op1=ALU.add),
                   reads=[B_bank[pb], B_bank[pb + 1], B_x[b][s]], writes=[B_x[b][s]])
                r, b_r = rms_scale(x_ap, B_x[b][s], D)
                op(DVE, lambda e, x_ap=x_ap, r=r: e.scalar_tensor_tensor(
                    out=x_ap, in0=x_ap, scalar=r, in1=gb[:, 2, :], op0=ALU.mult, op1=ALU.mult),
                   reads=[B_x[b][s], b_r, B_gb], writes=[B_x[b][s]])

        load_x(0, 0)
        op(POOL, lambda e: e.memset(mhalf[:], -0.5), writes=[B_mhalf])
        op(ACT, lambda e: e.activation(out=junk[:, 0:1], in_=mhalf[:, 0:1], func=AF.Square),
           reads=[B_mhalf], writes=[B_junk])
        op(POOL, lambda e: e.memset(ones[:], 1.0), writes=[B_ones])
        op(POOL, lambda e: e.memset(xhb[:], 0.0), writes=[B_xhb])
        op(POOL, lambda e: e.affine_select(out=ident[:], in_=ones[:], pattern=[[1, 128]], compare_op=ALU.is_equal,
                                           fill=0.0, base=0, channel_multiplier=-1),
           reads=[B_ones], writes=[B_ident])
        dma(SP, lambda e: e.dma_start(out=gb[:, 0, :], in_=gains_d[:, 0:D].partition_broadcast(128)[:, 0, :]),
            sem_g0, writes=[B_gb0])
        for s in range(1, NS):
            load_x(0, s)
        for _ in range(2):
            issue_load()
        dma(SP, lambda e: e.dma_start(out=xhb[0:2, :], in_=xh_d), sem_setup, reads=[B_ring[1]], writes=[B_xhb])
        dma(SP, lambda e: e.dma_start(out=gb[:, 1:3, :].rearrange("p a d -> p (a d)"),
                                      in_=gains_d[:, D:3 * D].partition_broadcast(128)[:, 0, :]),
            sem_setup, writes=[B_gb])
        dma(SP, lambda e: e.dma_start(out=convw[:], in_=convw_d), sem_setup2, writes=[B_convw])
        dma(SP, lambda e: e.dma_start(out=wsT32[:].rearrange("p a d -> p (a d)"), in_=wsT_d), sem_setup2,
            writes=[B_wsT32])
        dma(SP, lambda e: e.dma_start(out=bt[:].rearrange("p a d -> p (a d)"),
                                      in_=bsp_d.partition_broadcast(128)[:, 0, :]), sem_setup2, writes=[B_bt])
        for b_ in (B_xhb, B_gb):
            b_.writer = (sem_setup, sem_setup.count)
        B_ring[1].readers[sem_setup] = sem_setup.count
        for b_ in (B_wsT32, B_bt, B_convw):
            b_.writer = (sem_setup2, sem_setup2.count)

        HK = "h"
        stA = {HK: lambda: prepA(HK, xhb[:], B_xhb)}
        stB = {HK: lambda: prepB(HK, xhb[:], B_xhb)}
        stC = {HK: lambda: prepC(HK, hTh[:], B_hTh)}
        for s in range(NS):
            stA[s] = lambda s=s: P0A(0, s)
            stB[s] = lambda s=s: P0B(0, s)
            stC[s] = lambda s=s: P0C(0, s)
        order = [0, 1, 2, 3]
        stA[order[0]]()
        stB[order[0]]()
        for n in range(1, len(order)):
            stA[order[n]]()
            stC[order[n - 1]]()
            stB[order[n]]()
        stC[order[-1]]()
        for _ in range(RING - 2):
            issue_load()
        op(POOL, lambda e: e.affine_select(out=WT[:], in_=wsT32[:], pattern=[[0, 8], [1, 128]],
                                           compare_op=ALU.is_ge, fill=0.0, base=0, channel_multiplier=-1),
           reads=[B_wsT32], writes=[B_WT])

        P1(0)
        stA[HK]()
        stB[HK]()
        stC[HK]()
        for i in range(NT):
            for j in range(KC):
                P3(i, j)
                if j == 1:
                    if i >= 1:
                        for s in range(NS):
                            store_y(i - 1, s)
                    if i + 1 < NT:
                        for s in range(NS):
                            load_x(i + 1, s)
                if i + 1 < NT:
                    if 6 <= j <= 7:
                        P0C(i + 1, j - 6)
                    if 4 <= j <= 7:
                        P0B(i + 1, j - 4)
                    if 3 <= j <= 6:
                        P0A(i + 1, j - 3)
            for g in range(KC):
                P2(i, g)
                if i + 1 < NT and g <= 1:
                    P0C(i + 1, 2 + g)
            for j in range(KC):
                P4(i, j)
            if i + 1 < NT:
                P1(i + 1)
            P5(i)
        for s in range(NS):
            store_y(NT - 1, s)
        fin = []
        for b in range(2):
            for s in range(NS):
                fin.append((sem_st[b][s], sem_st[b][s].count))

        with nc.Block() as block:
            @block.sync
            def _(e):
                emit(SP, e)
                for s_, v in fin:
                    e.wait_ge(s_.h, v)

            @block.tensor
            def _(e):
                emit(PE, e)

            @block.scalar
            def _(e):
                emit(ACT, e)

            @block.vector
            def _(e):
                emit(DVE, e)

            @block.gpsimd
            def _(e):
                emit(POOL, e)
    return nc


def _blocks(w, col0, ncols):
    return w[:, col0:col0 + ncols].reshape(KC, 128, ncols).transpose(1, 0, 2)


def build_stream(w_in, w_a, w_b, w_out):
    st = np.empty((NSLOT, 128, SLOT_ELEMS), dtype=np.float32)
    U, V, ZA, XB, CB, BB, ZB, GA, GB = [q * D for q in range(9)]
    for h in range(2):
        st[S_WV[h]] = _blocks(w_in, V + h * 512, 512).reshape(128, SLOT_ELEMS)
        st[S_WO[h]] = _blocks(w_out, h * 512, 512).reshape(128, SLOT_ELEMS)
    for j in range(KC):
        c = j * 128
        st[S_P3[j]] = np.stack([_blocks(w_in, XB + c, 128), _blocks(w_in, CB + c, 128),
                                _blocks(w_in, BB + c, 128), _blocks(w_in, ZB + c, 128)], axis=1).reshape(128, SLOT_ELEMS)
        st[S_P4[j]] = np.stack([_blocks(w_in, GA + c, 128), _blocks(w_in, GB + c, 128),
                                _blocks(w_b, c, 128), _blocks(w_a, c, 128)], axis=1).reshape(128, SLOT_ELEMS)
    for q in range(4):
        g0, g1 = 2 * q, 2 * q + 1
        st[S_P2[q]] = np.stack([_blocks(w_in, U + g0 * 128, 128), _blocks(w_in, ZA + g0 * 128, 128),
                                _blocks(w_in, U + g1 * 128, 128), _blocks(w_in, ZA + g1 * 128, 128)],
                               axis=1).reshape(128, SLOT_ELEMS)
    return st


_NC_CACHE = {}


def kernel(x, norm_g, w_in, v_norm_g, w_spatial, b_spatial, conv_w, w_branch_a, w_branch_b, w_out, final_norm_g):
    x = np.asarray(x, dtype=np.float32)
    B, S, _ = x.shape
    assert (B, S) == (4, 8192)
    w_in = np.asarray(w_in, np.float32)[0]
    wst = build_stream(w_in, np.asarray(w_branch_a, np.float32)[0], np.asarray(w_branch_b, np.float32)[0],
                       np.asarray(w_out, np.float32)[0])
    gains = np.concatenate([np.asarray(norm_g, np.float32)[0], np.asarray(v_norm_g, np.float32)[0],
                            np.asarray(final_norm_g, np.float32)]).reshape(1, 3 * D)
    bsp = np.asarray(b_spatial, np.float32)[0].reshape(1, 8 * 128)
    convw = np.ascontiguousarray(np.asarray(conv_w, np.float32)[0].reshape(3, KC, 128).transpose(2, 1, 0)).reshape(128, 24)
    wsT = np.ascontiguousarray(np.asarray(w_spatial, np.float32)[0].transpose(2, 0, 1)).reshape(128, 8 * 128)

    in_maps = []
    for c in range(NCORES):
        b, half = c // 2, c % 2
        xc = np.ascontiguousarray(x[b, half * TOK:(half + 1) * TOK, :])
        if half == 0:
            xh = np.zeros((2, D), np.float32)
        else:
            xh = np.ascontiguousarray(x[b, TOK - 2:TOK, :])
        in_maps.append({"x": xc, "xh": xh, "wst": wst, "gains": gains, "bsp": bsp, "convw": convw, "wsT": wsT})

    if "nc" not in _NC_CACHE:
        _NC_CACHE["nc"] = build_program()
    nc = _NC_CACHE["nc"]
    res = run_bass_kernel_spmd(nc, in_maps, core_ids=list(range(NCORES)))
    out = np.empty((B, S, D), np.float32)
    for c in range(NCORES):
        b, half = c // 2, c % 2
        out[b, half * TOK:(half + 1) * TOK, :] = res.results[c]["y"]
    return out
```

```python
import numpy as np
from contextlib import ExitStack

import concourse.bass as bass
import concourse.mybir as mybir
from concourse.bass_utils import run_bass_kernel_spmd

F32 = mybir.dt.float32
BF16 = mybir.dt.bfloat16
AF = mybir.ActivationFunctionType
ALU = mybir.AluOpType

D = 1024
NCORES = 8
TOK = 4096
T = 512
NT = TOK // T
NS = 4
KC = 8
EPS = 1e-6
RING = 4
NSLOT = 24
SLOT_ELEMS = 4096
WB_SPREAD = 4

S_WV = [0, 1]
S_P3 = list(range(2, 10))
S_P2 = list(range(10, 14))
S_P4 = list(range(14, 22))
S_WO = [22, 23]


class Sem:
    def __init__(self, h, name):
        self.h = h
        self.name = name
        self.count = 0


class Buf:
    __slots__ = ("name", "writer", "readers")

    def __init__(self, name):
        self.name = name
        self.writer = None
        self.readers = {}


class Eng:
    def __init__(self, name, sem, self_sync=True):
        self.name = name
        self.sem = sem
        self.ops = []
        self.waited = {}
        self.self_sync = self_sync


def _collect_waits(eng, reads, writes):
    deps = {}

    def add(tok):
        if tok is None:
            return
        s, v = tok
        if deps.get(s, 0) < v:
            deps[s] = v

    for b in reads:
        add(b.writer)
    for b in writes:
        add(b.writer)
        for s, v in b.readers.items():
            add((s, v))
    waits = []
    for s, v in deps.items():
        if s is eng.sem and not eng.self_sync:
            continue
        if eng.waited.get(s, 0) < v:
            eng.waited[s] = v
            waits.append((s, v))
    return waits


def _commit(tok, reads, writes):
    s, v = tok
    for b in reads:
        if b.readers.get(s, 0) < v:
            b.readers[s] = v
    for b in writes:
        b.writer = tok
        b.readers = {}


def op(eng, fn, reads=(), writes=()):
    waits = _collect_waits(eng, reads, writes)
    eng.sem.count += 1
    tok = (eng.sem, eng.sem.count)
    eng.ops.append((waits, fn, eng.sem, 1))
    _commit(tok, reads, writes)
    return tok


def dma(eng, fn, dsem, reads=(), writes=()):
    waits = _collect_waits(eng, reads, writes)
    dsem.count += 16
    tok = (dsem, dsem.count)
    eng.ops.append((waits, fn, dsem, 16))
    _commit(tok, reads, writes)
    return tok


def emit(eng, e):
    for waits, fn, sem, inc in eng.ops:
        for s, v in waits:
            e.wait_ge(s.h, v)
        ins = fn(e)
        ins.then_inc(sem.h, inc)


class Rot:
    def __init__(self, items):
        self.items = items
        self.i = 0

    def next(self):
        it = self.items[self.i % len(self.items)]
        self.i += 1
        return it


def build_program():
    nc = bass.Bass("TRN2", target_bir_lowering=False)
    x_d = nc.dram_tensor("x", [TOK, D], F32, kind="ExternalInput").ap()
    xh_d = nc.dram_tensor("xh", [2, D], F32, kind="ExternalInput").ap()
    wst_d = nc.dram_tensor("wst", [NSLOT, 128, SLOT_ELEMS], F32, kind="ExternalInput").ap()
    gains_d = nc.dram_tensor("gains", [1, 3 * D], F32, kind="ExternalInput").ap()
    bsp_d = nc.dram_tensor("bsp", [1, 8 * 128], F32, kind="ExternalInput").ap()
    convw_d = nc.dram_tensor("convw", [128, 24], F32, kind="ExternalInput").ap()
    wsT_d = nc.dram_tensor("wsT", [128, 8 * 128], F32, kind="ExternalInput").ap()
    y_d = nc.dram_tensor("y", [TOK, D], F32, kind="ExternalOutput").ap()
    wbf_d = nc.dram_tensor("wbf", [NSLOT, 128, SLOT_ELEMS], BF16).ap()

    with ExitStack() as es:
        def sb(name, shape, dt):
            return es.enter_context(nc.sbuf_tensor("sb_" + name, shape, dt))

        def new_sem(name):
            return Sem(es.enter_context(nc.semaphore(name)), name)

        xbuf = sb("xbuf", [128, 2, NS, D], F32)
        xsb = sb("xsb", [128, 3, D], BF16)
        hT = sb("hT", [128, 2, KC, T], BF16)
        hTh = sb("hTh", [128, KC, 128], BF16)
        xhb = sb("xhb", [128, D], F32)
        gvb = sb("gvb", [128, 2, D], F32)
        junk = sb("junk", [128, D], BF16)
        vpp = sb("vpp", [128, NS, D], BF16)
        yA = sb("yA", [128, KC, T], BF16)
        yB = sb("yB", [128, KC, T], BF16)
        mg = sb("mg", [128, KC, T], BF16)
        NTMP = 19
        tmp = sb("tmp", [128, NTMP, T], F32)
        hcb = sb("hcb", [128, 2, T + 4], F32)
        gb = sb("gb", [128, 3, D], F32)
        bt = sb("bt", [128, 8, 128], F32)
        wsT32 = sb("wsT32", [128, 8, 128], F32)
        WT = sb("WT", [128, 8, 128], BF16)
        ident = sb("ident", [128, 128], BF16)
        ones = sb("ones", [128, 128], F32)
        convw = sb("convw", [128, 24], F32)
        halo = sb("halo", [128, 8, 2], F32)
        tiny = sb("tiny", [128, 3, 16], F32)
        mhalf = sb("mhalf", [128, 1], F32)
        hxc = sb("hxc", [128, 2, 4], F32)
        ring = sb("ring", [128, RING, SLOT_ELEMS], BF16)

        ps = es.enter_context(nc.psum_tensor("ps_main", [128, 7, 512], F32))
        psT = es.enter_context(nc.psum_tensor("ps_tr", [128, D], BF16))

        PE = Eng("pe", new_sem("s_pe"), self_sync=False)
        ACT = Eng("act", new_sem("s_act"))
        DVE = Eng("dve", new_sem("s_dve"))
        POOL = Eng("pool", new_sem("s_pool"))
        SP = Eng("sp", new_sem("s_sp"))
        sem_setup = new_sem("d_setup")
        sem_setup2 = new_sem("d_setup2")
        sem_g0 = new_sem("d_g0")
        sem_ring = [new_sem(f"d_ring{r}") for r in range(RING)]
        sem_xl = [[new_sem(f"d_xl{b}{s}") for s in range(NS)] for b in range(2)]
        sem_st = [[new_sem(f"d_st{b}{s}") for s in range(NS)] for b in range(2)]
        sem_wb = [new_sem(f"d_wb{r}") for r in range(RING)]
        sem_ringc = [new_sem(f"d_ringc{r}") for r in range(RING)]

        B_x = [[Buf(f"x{b}{s}") for s in range(NS)] for b in range(2)]
        B_xsb = [Buf(f"xsb{b}") for b in range(3)]
        B_hT = [[Buf(f"hT{b}{s}") for s in range(NS)] for b in range(2)]
        B_hTh = Buf("hTh")
        B_xhb = Buf("xhb")
        B_gvb = [Buf(f"gvb{b}") for b in range(2)]
        B_vpp = [Buf(f"vpp{s}") for s in range(NS)]
        B_yA = [Buf(f"yA{j}") for j in range(KC)]
        B_yB = [Buf(f"yB{j}") for j in range(KC)]
        B_mg = [Buf(f"mg{j}") for j in range(KC)]
        B_bank = [Buf(f"bank{b}") for b in range(7)]
        B_psT = Buf("psT")
        B_ring = [Buf(f"ring{r}") for r in range(RING)]
        B_wbf = [Buf(f"wbf{k}") for k in range(NSLOT)]
        B_gb = Buf("gb")
        B_gb0 = Buf("gb0")
        B_bt = Buf("bt")
        B_wsT32 = Buf("wsT32")
        B_WT = Buf("WT")
        B_ident = Buf("ident")
        B_ones = Buf("ones")
        B_convw = Buf("convw")
        B_halo = [Buf(f"halo{j}") for j in range(KC)]
        B_mhalf = Buf("mhalf")
        B_junk = Buf("junk")

        tmp_rot = Rot([(tmp[:, q, :], Buf(f"tmp{q}")) for q in range(NTMP)])
        hc_rot = Rot([(hcb[:, q, :], Buf(f"hc{q}")) for q in range(2)])
        hxc_rot = Rot([(hxc[:, q, :], Buf(f"hxc{q}")) for q in range(2)])
        tiny_rot = Rot([((tiny[:, 0, q:q + 1], tiny[:, 1, q:q + 1], tiny[:, 2, q:q + 1]),
                         (Buf(f"ss{q}"), Buf(f"ms{q}"), Buf(f"r{q}"))) for q in range(16)])

        bank_ptr = [0]

        def alloc_bank():
            b = bank_ptr[0] % 7
            bank_ptr[0] += 1
            return b

        def alloc_pair():
            while (bank_ptr[0] % 7) % 2 == 1 or (bank_ptr[0] % 7) == 6:
                bank_ptr[0] += 1
            b = bank_ptr[0] % 7
            bank_ptr[0] += 2
            return b

        stream = []
        for i in range(NT):
            if i == 0:
                stream += [(0, s) for s in S_P3 + S_WV + S_P2 + S_P4]
            else:
                stream += [(i, s) for s in S_P3 + S_P2 + S_P4]
            if i + 1 < NT:
                stream += [(i + 1, s) for s in S_WV]
            stream += [(i, s) for s in S_WO]
        stream_pos = {ts: n for n, ts in enumerate(stream)}
        next_load = [0]

        pending_wb = []

        def flush_wb(upto):
            while pending_wb and pending_wb[0][0] <= upto:
                _, r, slot = pending_wb.pop(0)
                dma(SP, lambda e, r=r, slot=slot: e.dma_start(out=wbf_d[slot], in_=ring[:, r, :]),
                    sem_wb[r], reads=[B_ring[r]], writes=[B_wbf[slot]])

        def issue_load():
            n = next_load[0]
            flush_wb(n - 2)
            if n >= len(stream):
                return
            next_load[0] += 1
            ti, slot = stream[n]
            r = n % RING
            wbt = slot % WB_SPREAD
            if ti <= wbt:
                extra = []
                dma(POOL, lambda e, r=r, slot=slot: e.dma_start(out=ring[:, r, :], in_=wst_d[slot]),
                    sem_ringc[r], reads=extra, writes=[B_ring[r]])
                if ti == wbt:
                    pending_wb.append((n, r, slot))
            else:
                dma(SP, lambda e, r=r, slot=slot: e.dma_start(out=ring[:, r, :], in_=wbf_d[slot]),
                    sem_ring[r], reads=[B_wbf[slot]], writes=[B_ring[r]])

        def ring_of(i, slot):
            return stream_pos[(i, slot)] % RING

        def slot_done(i, slot):
            n = stream_pos[(i, slot)]
            assert next_load[0] == n + RING or next_load[0] >= len(stream), (next_load[0], n)
            issue_load()

        def rms_scale(src_ap, src_buf, width):
            (ss, ms, r), (b_ss, b_ms, b_r) = tiny_rot.next()
            op(ACT, lambda e: e.activation(out=junk[:, 0:width], in_=src_ap, func=AF.Square, accum_out=ss),
               reads=[src_buf], writes=[b_ss, B_junk])
            op(DVE, lambda e: e.tensor_scalar(out=ms, in0=ss, scalar1=1.0 / width, scalar2=EPS,
                                              op0=ALU.mult, op1=ALU.add),
               reads=[b_ss], writes=[b_ms])
            op(POOL, lambda e: e.tensor_tensor(out=r, in0=ms, in1=mhalf[:, 0:1], op=ALU.pow),
               reads=[b_ms, B_mhalf], writes=[b_r])
            return r, b_r

        def mm_group(out_ap, pairs, reads, writes):
            n = len(pairs)

            def fn(e):
                ins = None
                for q, (l, r_) in enumerate(pairs):
                    ins = e.matmul(out_ap, lhsT=l, rhs=r_, start=(q == 0), stop=(q == n - 1))
                return ins
            return op(PE, fn, reads=reads, writes=writes)

        def blk(r, q):
            return ring[:, r, q * 1024:(q + 1) * 1024].rearrange("p (k c) -> p k c", k=KC)

        def wide(r):
            return ring[:, r, :].rearrange("p (k c) -> p k c", k=KC)

        p0_state = {}
        xs_ctr = [0]

        def prepA(key, x_ap, x_buf):
            p0_state[key] = rms_scale(x_ap, x_buf, D)

        def prepB(key, x_ap, x_buf):
            r, b_r = p0_state[key]
            q = xs_ctr[0] % 3
            xs_ctr[0] += 1
            p0_state[key] = q
            op(DVE, lambda e: e.scalar_tensor_tensor(out=xsb[:, q, :], in0=x_ap, scalar=r, in1=gb[:, 0, :],
                                                     op0=ALU.mult, op1=ALU.mult),
               reads=[x_buf, b_r, B_gb0], writes=[B_xsb[q]])

        def prepC(key, hT_out_ap, hT_buf):
            q = p0_state.pop(key)

            def tr(e):
                ins = None
                for k in range(KC):
                    ins = e.transpose(psT[:, k * 128:(k + 1) * 128], xsb[:, q, k * 128:(k + 1) * 128], ident[:])
                return ins
            op(PE, tr, reads=[B_xsb[q], B_ident], writes=[B_psT])
            op(ACT, lambda e: e.activation(out=hT_out_ap, in_=psT[:].rearrange("p (k t) -> p k t", k=KC),
                                           func=AF.Copy),
               reads=[B_psT], writes=[hT_buf])

        def P0A(i, s):
            prepA((i, s), xbuf[:, i % 2, s, :], B_x[i % 2][s])

        def P0B(i, s):
            prepB((i, s), xbuf[:, i % 2, s, :], B_x[i % 2][s])

        def P0C(i, s):
            prepC((i, s), hT[:, i % 2, :, s * 128:(s + 1) * 128], B_hT[i % 2][s])

        def load_x(i, s):
            b = i % 2
            row0 = i * T + s * 128
            dma(SP, lambda e: e.dma_start(out=xbuf[:, b, s, :], in_=x_d[row0:row0 + 128, :]),
                sem_xl[b][s], reads=[], writes=[B_x[b][s]])

        def store_y(i, s):
            b = i % 2
            row0 = i * T + s * 128
            dma(SP, lambda e: e.dma_start(out=y_d[row0:row0 + 128, :], in_=xbuf[:, b, s, :]),
                sem_st[b][s], reads=[B_x[b][s]], writes=[])

        def P1(i):
            b = i % 2
            r0, r1 = ring_of(i, S_WV[0]), ring_of(i, S_WV[1])
            for s in range(NS):
                pb = alloc_pair()
                for h, rr in enumerate((r0, r1)):
                    w = wide(rr)
                    pairs = [(hT[:, b, k, s * 128:(s + 1) * 128], w[:, k, :]) for k in range(KC)]
                    mm_group(ps[:, pb + h, :], pairs, reads=[B_hT[b][s], B_ring[rr]], writes=[B_bank[pb + h]])
                    if s == NS - 1:
                        slot_done(i, S_WV[h])
                gq = s % 2
                gv_ap = gvb[:, gq, :]
                pair_ap = ps[:, pb:pb + 2, :].rearrange("p a b -> p (a b)")
                op(ACT, lambda e, gv_ap=gv_ap, pair_ap=pair_ap: e.activation(out=gv_ap, in_=pair_ap, func=AF.Gelu),
                   reads=[B_bank[pb], B_bank[pb + 1]], writes=[B_gvb[gq]])
                r, b_r = rms_scale(gv_ap, B_gvb[gq], D)
                op(DVE, lambda e, gv_ap=gv_ap, r=r, s=s: e.scalar_tensor_tensor(
                    out=vpp[:, s, :], in0=gv_ap, scalar=r, in1=gb[:, 1, :], op0=ALU.mult, op1=ALU.mult),
                   reads=[B_gvb[gq], b_r, B_gb], writes=[B_vpp[s]])

        def P3(i, j):
            b = i % 2
            rr = ring_of(i, S_P3[j])
            hTb = [B_hT[b][s] for s in range(NS)]

            def proj(q, N=T, rhs_fn=None):
                bk = alloc_bank()
                w = blk(rr, q)
                pairs = [(w[:, k, :], hT[:, b, k, :]) for k in range(KC)]
                mm_group(ps[:, bk, :], pairs, reads=hTb + [B_ring[rr]], writes=[B_bank[bk]])
                return bk

            hc_ap, hc_buf = hc_rot.next()
            bX = proj(0)
            bC = proj(1)
            bB = proj(2)
            bZ = proj(3)
            if i == 0:
                bH = alloc_bank()
                wx, wc = blk(rr, 0), blk(rr, 1)
                mm_group(ps[:, bH, 0:2], [(wx[:, k, :], hTh[:, k, 0:2]) for k in range(KC)],
                         reads=[B_hTh, B_ring[rr]], writes=[B_bank[bH]])
                mm_group(ps[:, bH, 2:4], [(wc[:, k, :], hTh[:, k, 0:2]) for k in range(KC)],
                         reads=[B_hTh, B_ring[rr]], writes=[B_bank[bH]])
            slot_done(i, S_P3[j])

            xs_ap, xs_buf = tmp_rot.next()
            op(ACT, lambda e: e.activation(out=xs_ap, in_=ps[:, bX, :], func=AF.Copy),
               reads=[B_bank[bX]], writes=[xs_buf])
            if i == 0:
                op(DVE, lambda e: e.tensor_tensor(out=hc_ap[:, 2:T + 2], in0=xs_ap, in1=ps[:, bC, :], op=ALU.mult),
                   reads=[xs_buf, B_bank[bC]], writes=[hc_buf])
                hx_ap, hx_buf = hxc_rot.next()
                op(ACT, lambda e: e.activation(out=hx_ap, in_=ps[:, bH, 0:4], func=AF.Copy),
                   reads=[B_bank[bH]], writes=[hx_buf])
                op(DVE, lambda e: e.tensor_tensor(out=halo[:, j, :], in0=hx_ap[:, 0:2], in1=hx_ap[:, 2:4],
                                                  op=ALU.mult),
                   reads=[hx_buf], writes=[B_halo[j]])
                op(ACT, lambda e: e.activation(out=hc_ap[:, 0:2], in_=halo[:, j, :], func=AF.Copy),
                   reads=[B_halo[j]], writes=[hc_buf])
            else:
                op(ACT, lambda e: e.activation(out=hc_ap[:, 0:2], in_=halo[:, j, :], func=AF.Copy),
                   reads=[B_halo[j]], writes=[hc_buf])
                op(DVE, lambda e: e.tensor_tensor(out=hc_ap[:, 2:T + 2], in0=xs_ap, in1=ps[:, bC, :], op=ALU.mult),
                   reads=[xs_buf, B_bank[bC]], writes=[hc_buf])
            tz_ap, tz_buf = tmp_rot.next()
            op(ACT, lambda e: e.activation(out=tz_ap, in_=ps[:, bZ, :], func=AF.Tanh, scale=0.5),
               reads=[B_bank[bZ]], writes=[tz_buf])
            acc_ap, acc_buf = tmp_rot.next()
            op(ACT, lambda e: e.activation(out=acc_ap, in_=hc_ap[:, 0:T], func=AF.Copy,
                                           scale=convw[:, 3 * j:3 * j + 1]),
               reads=[hc_buf, B_convw], writes=[acc_buf])
            a_ap, a_buf = tmp_rot.next()
            op(DVE, lambda e: e.scalar_tensor_tensor(out=a_ap, in0=tz_ap, scalar=1.0, in1=ps[:, bZ, :],
                                                     op0=ALU.add, op1=ALU.mult),
               reads=[tz_buf, B_bank[bZ]], writes=[a_buf])
            for kk in (1, 2):
                op(DVE, lambda e, kk=kk: e.scalar_tensor_tensor(
                    out=acc_ap, in0=hc_ap[:, kk:kk + T], scalar=convw[:, 3 * j + kk:3 * j + kk + 1], in1=acc_ap,
                    op0=ALU.mult, op1=ALU.add),
                   reads=[hc_buf, acc_buf, B_convw], writes=[acc_buf])
            op(ACT, lambda e: e.activation(out=halo[:, j, :], in_=hc_ap[:, T:T + 2], func=AF.Copy),
               reads=[hc_buf], writes=[B_halo[j]])
            op(DVE, lambda e: e.tensor_tensor(out=acc_ap, in0=acc_ap, in1=ps[:, bB, :], op=ALU.mult),
               reads=[acc_buf, B_bank[bB]], writes=[acc_buf])
            op(POOL, lambda e: e.tensor_tensor(out=yB[:, j, :], in0=acc_ap, in1=a_ap, op=ALU.mult),
               reads=[acc_buf, a_buf], writes=[B_yB[j]])

        def P2(i, g):
            b = i % 2
            rr = ring_of(i, S_P2[g // 2])
            q0 = (g % 2) * 2
            hTb = [B_hT[b][s] for s in range(NS)]
            bS = alloc_bank()

            def sp(e):
                ins = None
                for s in range(NS):
                    ins = e.matmul(ps[:, bS, s * 128:(s + 1) * 128], lhsT=vpp[:, s, g * 128:(g + 1) * 128],
                                   rhs=WT[:, g, :], start=True, stop=True)
                return ins
            op(PE, sp, reads=B_vpp + [B_WT], writes=[B_bank[bS]])
            bU = alloc_bank()
            wu = blk(rr, q0)
            mm_group(ps[:, bU, :], [(wu[:, k, :], hT[:, b, k, :]) for k in range(KC)],
                     reads=hTb + [B_ring[rr]], writes=[B_bank[bU]])
            bZ = alloc_bank()
            wz = blk(rr, q0 + 1)
            mm_group(ps[:, bZ, :], [(wz[:, k, :], hT[:, b, k, :]) for k in range(KC)],
                     reads=hTb + [B_ring[rr]], writes=[B_bank[bZ]])
            if g % 2 == 1:
                slot_done(i, S_P2[g // 2])

            gu_ap, gu_buf = tmp_rot.next()
            op(ACT, lambda e: e.activation(out=gu_ap, in_=ps[:, bU, :], func=AF.Gelu),
               reads=[B_bank[bU]], writes=[gu_buf])
            tz_ap, tz_buf = tmp_rot.next()
            op(ACT, lambda e: e.activation(out=tz_ap, in_=ps[:, bZ, :], func=AF.Tanh, scale=0.5),
               reads=[B_bank[bZ]], writes=[tz_buf])
            m_ap, m_buf = tmp_rot.next()
            op(DVE, lambda e: e.tensor_tensor(
                out=m_ap.rearrange("p (a t) -> p a t", a=NS),
                in0=ps[:, bS, :].rearrange("p (a t) -> p a t", a=NS),
                in1=bt[:, g:g + 1, :].to_broadcast([128, NS, 128]), op=ALU.add),
               reads=[B_bank[bS], B_bt], writes=[m_buf])
            op(DVE, lambda e: e.tensor_tensor(out=m_ap, in0=m_ap, in1=gu_ap, op=ALU.mult),
               reads=[m_buf, gu_buf], writes=[m_buf])
            a_ap, a_buf = tmp_rot.next()
            op(DVE, lambda e: e.scalar_tensor_tensor(out=a_ap, in0=tz_ap, scalar=1.0, in1=ps[:, bZ, :],
                                                     op0=ALU.add, op1=ALU.mult),
               reads=[tz_buf, B_bank[bZ]], writes=[a_buf])
            op(POOL, lambda e: e.tensor_tensor(out=yA[:, g, :], in0=m_ap, in1=a_ap, op=ALU.mult),
               reads=[m_buf, a_buf], writes=[B_yA[g]])

        def P4(i, j):
            b = i % 2
            rr = ring_of(i, S_P4[j])
            hTb = [B_hT[b][s] for s in range(NS)]
            bGA = alloc_bank()
            w = blk(rr, 0)
            mm_group(ps[:, bGA, :], [(w[:, k, :], hT[:, b, k, :]) for k in range(KC)],
                     reads=hTb + [B_ring[rr]], writes=[B_bank[bGA]])
            bGB = alloc_bank()
            w = blk(rr, 1)
            mm_group(ps[:, bGB, :], [(w[:, k, :], hT[:, b, k, :]) for k in range(KC)],
                     reads=hTb + [B_ring[rr]], writes=[B_bank[bGB]])
            bYB = alloc_bank()
            w = blk(rr, 2)
            mm_group(ps[:, bYB, :], [(w[:, k, :], yB[:, k, :]) for k in range(KC)],
                     reads=B_yB + [B_ring[rr]], writes=[B_bank[bYB]])
            bYA = alloc_bank()
            w = blk(rr, 3)
            mm_group(ps[:, bYA, :], [(w[:, k, :], yA[:, k, :]) for k in range(KC)],
                     reads=B_yA + [B_ring[rr]], writes=[B_bank[bYA]])
            slot_done(i, S_P4[j])

            tA_ap, tA_buf = tmp_rot.next()
            op(ACT, lambda e: e.activation(out=tA_ap, in_=ps[:, bGA, :], func=AF.Tanh, scale=0.5),
               reads=[B_bank[bGA]], writes=[tA_buf])
            tB_ap, tB_buf = tmp_rot.next()
            op(ACT, lambda e: e.activation(out=tB_ap, in_=ps[:, bGB, :], func=AF.Tanh, scale=0.5),
               reads=[B_bank[bGB]], writes=[tB_buf])
            m2_ap, m2_buf = tmp_rot.next()
            op(DVE, lambda e: e.scalar_tensor_tensor(out=m2_ap, in0=tB_ap, scalar=1.0, in1=ps[:, bYB, :],
                                                     op0=ALU.add, op1=ALU.mult),
               reads=[tB_buf, B_bank[bYB]], writes=[m2_buf])
            m1_ap, m1_buf = tmp_rot.next()
            op(DVE, lambda e: e.scalar_tensor_tensor(out=m1_ap, in0=tA_ap, scalar=1.0, in1=ps[:, bYA, :],
                                                     op0=ALU.add, op1=ALU.mult),
               reads=[tA_buf, B_bank[bYA]], writes=[m1_buf])
            op(POOL, lambda e: e.tensor_tensor(out=mg[:, j, :], in0=m1_ap, in1=m2_ap, op=ALU.add),
               reads=[m1_buf, m2_buf], writes=[B_mg[j]])

        def P5(i):
            b = i % 2
            r0, r1 = ring_of(i, S_WO[0]), ring_of(i, S_WO[1])
            for s in range(NS):
                pb = alloc_pair()
                for h, rr in enumerate((r0, r1)):
                    w = wide(rr)
                    pairs = [(mg[:, k, s * 128:(s + 1) * 128], w[:, k, :]) for k in range(KC)]
                    mm_group(ps[:, pb + h, :], pairs, reads=B_mg + [B_ring[rr]], writes=[B_bank[pb + h]])
                    if s == NS - 1:
                        slot_done(i, S_WO[h])
                x_ap = xbuf[:, b, s, :]
                pair_ap = ps[:, pb:pb + 2, :].rearrange("p a b -> p (a b)")
                op(DVE, lambda e, x_ap=x_ap, pair_ap=pair_ap: e.scalar_tensor_tensor(
                    out=x_ap, in0=pair_ap, scalar=0.25, in1=x_ap, op0=ALU.mult, op1=ALU.add),
                   reads=[B_bank[pb], B_bank[pb + 1], B_x[b][s]], writes=[B_x[b][s]])
                r, b_r = rms_scale(x_ap, B_x[b][s], D)
                op(DVE, lambda e, x_ap=x_ap, r=r: e.scalar_tensor_tensor(
                    out=x_ap, in0=x_ap, scalar=r, in1=gb[:, 2, :], op0=ALU.mult, op1=ALU.mult),
                   reads=[B_x[b][s], b_r, B_gb], writes=[B_x[b][s]])

        load_x(0, 0)
        op(POOL, lambda e: e.memset(mhalf[:], -0.5), writes=[B_mhalf])
        op(ACT, lambda e: e.activation(out=junk[:, 0:1], in_=mhalf[:, 0:1], func=AF.Square),
           reads=[B_mhalf], writes=[B_junk])
        op(POOL, lambda e: e.memset(ones[:], 1.0), writes=[B_ones])
        op(POOL, lambda e: e.memset(xhb[:], 0.0), writes=[B_xhb])
        op(POOL, lambda e: e.affine_select(out=ident[:], in_=ones[:], pattern=[[1, 128]], compare_op=ALU.is_equal,
                                           fill=0.0, base=0, channel_multiplier=-1),
           reads=[B_ones], writes=[B_ident])
        dma(SP, lambda e: e.dma_start(out=gb[:, 0, :], in_=gains_d[:, 0:D].partition_broadcast(128)[:, 0, :]),
            sem_g0, writes=[B_gb0])
        for s in range(1, NS):
            load_x(0, s)
        for _ in range(2):
            issue_load()
        dma(SP, lambda e: e.dma_start(out=xhb[0:2, :], in_=xh_d), sem_setup, writes=[B_xhb])
        dma(SP, lambda e: e.dma_start(out=convw[:], in_=convw_d), sem_setup, writes=[B_convw])
        dma(SP, lambda e: e.dma_start(out=gb[:, 1:3, :].rearrange("p a d -> p (a d)"),
                                      in_=gains_d[:, D:3 * D].partition_broadcast(128)[:, 0, :]),
            sem_setup2, reads=[B_ring[1]], writes=[B_gb])
        dma(SP, lambda e: e.dma_start(out=wsT32[:].rearrange("p a d -> p (a d)"), in_=wsT_d), sem_setup2,
            writes=[B_wsT32])
        dma(SP, lambda e: e.dma_start(out=bt[:].rearrange("p a d -> p (a d)"),
                                      in_=bsp_d.partition_broadcast(128)[:, 0, :]), sem_setup2, writes=[B_bt])
        for b_ in (B_xhb, B_convw):
            b_.writer = (sem_setup, sem_setup.count)
        for b_ in (B_gb, B_wsT32, B_bt):
            b_.writer = (sem_setup2, sem_setup2.count)
        B_ring[1].readers[sem_setup2] = sem_setup2.count

        HK = "h"
        stA = {HK: lambda: prepA(HK, xhb[:], B_xhb)}
        stB = {HK: lambda: prepB(HK, xhb[:], B_xhb)}
        stC = {HK: lambda: prepC(HK, hTh[:], B_hTh)}
        for s in range(NS):
            stA[s] = lambda s=s: P0A(0, s)
            stB[s] = lambda s=s: P0B(0, s)
            stC[s] = lambda s=s: P0C(0, s)
        order = [0, 1, 2, 3, HK]
        stA[order[0]]()
        stB[order[0]]()
        for n in range(1, len(order)):
            stA[order[n]]()
            stC[order[n - 1]]()
            stB[order[n]]()
        stC[order[-1]]()
        for _ in range(RING - 2):
            issue_load()
        op(POOL, lambda e: e.affine_select(out=WT[:], in_=wsT32[:], pattern=[[0, 8], [1, 128]],
                                           compare_op=ALU.is_ge, fill=0.0, base=0, channel_multiplier=-1),
           reads=[B_wsT32], writes=[B_WT])

        for i in range(NT):
            for j in range(KC):
                P3(i, j)
                if j == 1:
                    if i >= 1:
                        for s in range(NS):
                            store_y(i - 1, s)
                    if i + 1 < NT:
                        for s in range(NS):
                            load_x(i + 1, s)
                if i + 1 < NT:
                    if 6 <= j <= 7:
                        P0C(i + 1, j - 6)
                    if 4 <= j <= 7:
                        P0B(i + 1, j - 4)
                    if 3 <= j <= 6:
                        P0A(i + 1, j - 3)
            if i == 0:
                P1(0)
            for g in range(KC):
                P2(i, g)
                if i + 1 < NT and g <= 1:
                    P0C(i + 1, 2 + g)
            for j in range(KC):
                P4(i, j)
            if i + 1 < NT:
                P1(i + 1)
            P5(i)
        for s in range(NS):
            store_y(NT - 1, s)
        fin = []
        for b in range(2):
            for s in range(NS):
                fin.append((sem_st[b][s], sem_st[b][s].count))

        with nc.Block() as block:
            @block.sync
            def _(e):
                emit(SP, e)
                for s_, v in fin:
                    e.wait_ge(s_.h, v)

            @block.tensor
            def _(e):
                emit(PE, e)

            @block.scalar
            def _(e):
                emit(ACT, e)

            @block.vector
            def _(e):
                emit(DVE, e)

            @block.gpsimd
            def _(e):
                emit(POOL, e)
    return nc


def _blocks(w, col0, ncols):
    return w[:, col0:col0 + ncols].reshape(KC, 128, ncols).transpose(1, 0, 2)


def build_stream(w_in, w_a, w_b, w_out):
    st = np.empty((NSLOT, 128, SLOT_ELEMS), dtype=np.float32)
    U, V, ZA, XB, CB, BB, ZB, GA, GB = [q * D for q in range(9)]
    for h in range(2):
        st[S_WV[h]] = _blocks(w_in, V + h * 512, 512).reshape(128, SLOT_ELEMS)
        st[S_WO[h]] = _blocks(w_out, h * 512, 512).reshape(128, SLOT_ELEMS)
    for j in range(KC):
        c = j * 128
        st[S_P3[j]] = np.stack([_blocks(w_in, XB + c, 128), _blocks(w_in, CB + c, 128),
                                _blocks(w_in, BB + c, 128), _blocks(w_in, ZB + c, 128)], axis=1).reshape(128, SLOT_ELEMS)
        st[S_P4[j]] = np.stack([_blocks(w_in, GA + c, 128), _blocks(w_in, GB + c, 128),
                                _blocks(w_b, c, 128), _blocks(w_a, c, 128)], axis=1).reshape(128, SLOT_ELEMS)
    for q in range(4):
        g0, g1 = 2 * q, 2 * q + 1
        st[S_P2[q]] = np.stack([_blocks(w_in, U + g0 * 128, 128), _blocks(w_in, ZA + g0 * 128, 128),
                                _blocks(w_in, U + g1 * 128, 128), _blocks(w_in, ZA + g1 * 128, 128)],
                               axis=1).reshape(128, SLOT_ELEMS)
    return st


_NC_CACHE = {}


def kernel(x, norm_g, w_in, v_norm_g, w_spatial, b_spatial, conv_w, w_branch_a, w_branch_b, w_out, final_norm_g):
    x = np.asarray(x, dtype=np.float32)
    B, S, _ = x.shape
    assert (B, S) == (4, 8192)
    w_in = np.asarray(w_in, np.float32)[0]
    wst = build_stream(w_in, np.asarray(w_branch_a, np.float32)[0], np.asarray(w_branch_b, np.float32)[0],
                       np.asarray(w_out, np.float32)[0])
    gains = np.concatenate([np.asarray(norm_g, np.float32)[0], np.asarray(v_norm_g, np.float32)[0],
                            np.asarray(final_norm_g, np.float32)]).reshape(1, 3 * D)
    bsp = np.asarray(b_spatial, np.float32)[0].reshape(1, 8 * 128)
    convw = np.ascontiguousarray(np.asarray(conv_w, np.float32)[0].reshape(3, KC, 128).transpose(2, 1, 0)).reshape(128, 24)
    wsT = np.ascontiguousarray(np.asarray(w_spatial, np.float32)[0].transpose(2, 0, 1)).reshape(128, 8 * 128)

    in_maps = []
    for c in range(NCORES):
        b, half = c // 2, c % 2
        xc = np.ascontiguousarray(x[b, half * TOK:(half + 1) * TOK, :])
        if half == 0:
            xh = np.zeros((2, D), np.float32)
        else:
            xh = np.ascontiguousarray(x[b, TOK - 2:TOK, :])
        in_maps.append({"x": xc, "xh": xh, "wst": wst, "gains": gains, "bsp": bsp, "convw": convw, "wsT": wsT})

    if "nc" not in _NC_CACHE:
        _NC_CACHE["nc"] = build_program()
    nc = _NC_CACHE["nc"]
    res = run_bass_kernel_spmd(nc, in_maps, core_ids=list(range(NCORES)))
    out = np.empty((B, S, D), np.float32)
    for c in range(NCORES):
        b, half = c // 2, c % 2
        out[b, half * TOK:(half + 1) * TOK, :] = res.results[c]["y"]
    return out
```

```python
import numpy as np
from contextlib import ExitStack

import concourse.bass as bass
import concourse.mybir as mybir
from concourse.bass_utils import run_bass_kernel_spmd

F32 = mybir.dt.float32
BF16 = mybir.dt.bfloat16
AF = mybir.ActivationFunctionType
ALU = mybir.AluOpType

D = 1024
NCORES = 8
TOK = 4096
T = 512
NT = TOK // T
NS = 4
KC = 8
EPS = 1e-6
RING = 4
NSLOT = 24
SLOT_ELEMS = 4096
WB_SPREAD = 4

S_WV = [0, 1]
S_P3 = list(range(2, 10))
S_P2 = list(range(10, 14))
S_P4 = list(range(14, 22))
S_WO = [22, 23]


class Sem:
    def __init__(self, h, name):
        self.h = h
        self.name = name
        self.count = 0


class Buf:
    __slots__ = ("name", "writer", "readers")

    def __init__(self, name):
        self.name = name
        self.writer = None
        self.readers = {}


class Eng:
    def __init__(self, name, sem, self_sync=True):
        self.name = name
        self.sem = sem
        self.ops = []
        self.waited = {}
        self.self_sync = self_sync


def _collect_waits(eng, reads, writes):
    deps = {}

    def add(tok):
        if tok is None:
            return
        s, v = tok
        if deps.get(s, 0) < v:
            deps[s] = v

    for b in reads:
        add(b.writer)
    for b in writes:
        add(b.writer)
        for s, v in b.readers.items():
            add((s, v))
    waits = []
    for s, v in deps.items():
        if s is eng.sem and not eng.self_sync:
            continue
        if eng.waited.get(s, 0) < v:
            eng.waited[s] = v
            waits.append((s, v))
    return waits


def _commit(tok, reads, writes):
    s, v = tok
    for b in reads:
        if b.readers.get(s, 0) < v:
            b.readers[s] = v
    for b in writes:
        b.writer = tok
        b.readers = {}


def op(eng, fn, reads=(), writes=()):
    waits = _collect_waits(eng, reads, writes)
    eng.sem.count += 1
    tok = (eng.sem, eng.sem.count)
    eng.ops.append((waits, fn, eng.sem, 1))
    _commit(tok, reads, writes)
    return tok


def dma(eng, fn, dsem, reads=(), writes=()):
    waits = _collect_waits(eng, reads, writes)
    dsem.count += 16
    tok = (dsem, dsem.count)
    eng.ops.append((waits, fn, dsem, 16))
    _commit(tok, reads, writes)
    return tok


def emit(eng, e):
    for waits, fn, sem, inc in eng.ops:
        for s, v in waits:
            e.wait_ge(s.h, v)
        ins = fn(e)
        ins.then_inc(sem.h, inc)


class Rot:
    def __init__(self, items):
        self.items = items
        self.i = 0

    def next(self):
        it = self.items[self.i % len(self.items)]
        self.i += 1
        return it


def build_program():
    nc = bass.Bass("TRN2", target_bir_lowering=False)
    x_d = nc.dram_tensor("x", [TOK, D], F32, kind="ExternalInput").ap()
    xh_d = nc.dram_tensor("xh", [2, D], F32, kind="ExternalInput").ap()
    wst_d = nc.dram_tensor("wst", [NSLOT, 128, SLOT_ELEMS], F32, kind="ExternalInput").ap()
    gains_d = nc.dram_tensor("gains", [1, 3 * D], F32, kind="ExternalInput").ap()
    bsp_d = nc.dram_tensor("bsp", [1, 8 * 128], F32, kind="ExternalInput").ap()
    convw_d = nc.dram_tensor("convw", [128, 24], F32, kind="ExternalInput").ap()
    wsT_d = nc.dram_tensor("wsT", [128, 8 * 128], F32, kind="ExternalInput").ap()
    y_d = nc.dram_tensor("y", [TOK, D], F32, kind="ExternalOutput").ap()
    wbf_d = nc.dram_tensor("wbf", [NSLOT, 128, SLOT_ELEMS], BF16).ap()

    with ExitStack() as es:
        def sb(name, shape, dt):
            return es.enter_context(nc.sbuf_tensor("sb_" + name, shape, dt))

        def new_sem(name):
            return Sem(es.enter_context(nc.semaphore(name)), name)

        xbuf = sb("xbuf", [128, 2, NS, D], F32)
        xsb = sb("xsb", [128, 3, D], BF16)
        hT = sb("hT", [128, 2, KC, T], BF16)
        hTh = sb("hTh", [128, KC, 128], BF16)
        xhb = sb("xhb", [128, D], F32)
        gvb = sb("gvb", [128, 2, D], F32)
        junk = sb("junk", [128, D], BF16)
        vpp = sb("vpp", [128, NS, D], BF16)
        yA = sb("yA", [128, KC, T], BF16)
        yB = sb("yB", [128, KC, T], BF16)
        mg = sb("mg", [128, KC, T], BF16)
        NTMP = 19
        tmp = sb("tmp", [128, NTMP, T], F32)
        hcb = sb("hcb", [128, 2, T + 4], F32)
        gb = sb("gb", [128, 3, D], F32)
        bt = sb("bt", [128, 8, 128], F32)
        wsT32 = sb("wsT32", [128, 8, 128], F32)
        WT = sb("WT", [128, 8, 128], BF16)
        ident = sb("ident", [128, 128], BF16)
        ones = sb("ones", [128, 128], F32)
        convw = sb("convw", [128, 24], F32)
        halo = sb("halo", [128, 8, 2], F32)
        tiny = sb("tiny", [128, 3, 16], F32)
        mhalf = sb("mhalf", [128, 1], F32)
        hxc = sb("hxc", [128, 2, 4], F32)
        ring = sb("ring", [128, RING, SLOT_ELEMS], BF16)

        ps = es.enter_context(nc.psum_tensor("ps_main", [128, 7, 512], F32))
        psT = es.enter_context(nc.psum_tensor("ps_tr", [128, D], BF16))

        PE = Eng("pe", new_sem("s_pe"), self_sync=False)
        ACT = Eng("act", new_sem("s_act"))
        DVE = Eng("dve", new_sem("s_dve"))
        POOL = Eng("pool", new_sem("s_pool"))
        SP = Eng("sp", new_sem("s_sp"))
        sem_setup = new_sem("d_setup")
        sem_setup2 = new_sem("d_setup2")
        sem_g0 = new_sem("d_g0")
        sem_ring = [new_sem(f"d_ring{r}") for r in range(RING)]
        sem_xl = [[new_sem(f"d_xl{b}{s}") for s in range(NS)] for b in range(2)]
        sem_st = [[new_sem(f"d_st{b}{s}") for s in range(NS)] for b in range(2)]
        sem_wb = [new_sem(f"d_wb{r}") for r in range(RING)]
        sem_ringc = [new_sem(f"d_ringc{r}") for r in range(RING)]

        B_x = [[Buf(f"x{b}{s}") for s in range(NS)] for b in range(2)]
        B_xsb = [Buf(f"xsb{b}") for b in range(3)]
        B_hT = [[Buf(f"hT{b}{s}") for s in range(NS)] for b in range(2)]
        B_hTh = Buf("hTh")
        B_xhb = Buf("xhb")
        B_gvb = [Buf(f"gvb{b}") for b in range(2)]
        B_vpp = [Buf(f"vpp{s}") for s in range(NS)]
        B_yA = [Buf(f"yA{j}") for j in range(KC)]
        B_yB = [Buf(f"yB{j}") for j in range(KC)]
        B_mg = [Buf(f"mg{j}") for j in range(KC)]
        B_bank = [Buf(f"bank{b}") for b in range(7)]
        B_psT = Buf("psT")
        B_ring = [Buf(f"ring{r}") for r in range(RING)]
        B_wbf = [Buf(f"wbf{k}") for k in range(NSLOT)]
        B_gb = Buf("gb")
        B_gb0 = Buf("gb0")
        B_bt = Buf("bt")
        B_wsT32 = Buf("wsT32")
        B_WT = Buf("WT")
        B_ident = Buf("ident")
        B_ones = Buf("ones")
        B_convw = Buf("convw")
        B_halo = [Buf(f"halo{j}") for j in range(KC)]
        B_mhalf = Buf("mhalf")
        B_junk = Buf("junk")

        tmp_rot = Rot([(tmp[:, q, :], Buf(f"tmp{q}")) for q in range(NTMP)])
        hc_rot = Rot([(hcb[:, q, :], Buf(f"hc{q}")) for q in range(2)])
        hxc_rot = Rot([(hxc[:, q, :], Buf(f"hxc{q}")) for q in range(2)])
        tiny_rot = Rot([((tiny[:, 0, q:q + 1], tiny[:, 1, q:q + 1], tiny[:, 2, q:q + 1]),
                         (Buf(f"ss{q}"), Buf(f"ms{q}"), Buf(f"r{q}"))) for q in range(16)])

        bank_ptr = [0]

        def alloc_bank():
            b = bank_ptr[0] % 7
            bank_ptr[0] += 1
            return b

        def alloc_pair():
            while (bank_ptr[0] % 7) % 2 == 1 or (bank_ptr[0] % 7) == 6:
                bank_ptr[0] += 1
            b = bank_ptr[0] % 7
            bank_ptr[0] += 2
            return b

        stream = []
        stream += [(0, s) for s in S_WV]
        for i in range(NT):
            stream += [(i, s) for s in S_P3 + S_P2 + S_P4]
            if i + 1 < NT:
                stream += [(i + 1, s) for s in S_WV]
            stream += [(i, s) for s in S_WO]
        stream_pos = {ts: n for n, ts in enumerate(stream)}
        next_load = [0]

        pending_wb = []

        def flush_wb(upto):
            while pending_wb and pending_wb[0][0] <= upto:
                _, r, slot = pending_wb.pop(0)
                dma(SP, lambda e, r=r, slot=slot: e.dma_start(out=wbf_d[slot], in_=ring[:, r, :]),
                    sem_wb[r], reads=[B_ring[r]], writes=[B_wbf[slot]])

        def issue_load():
            n = next_load[0]
            flush_wb(n - 2)
            if n >= len(stream):
                return
            next_load[0] += 1
            ti, slot = stream[n]
            r = n % RING
            wbt = slot % WB_SPREAD
            if ti <= wbt:
                extra = []
                dma(POOL, lambda e, r=r, slot=slot: e.dma_start(out=ring[:, r, :], in_=wst_d[slot]),
                    sem_ringc[r], reads=extra, writes=[B_ring[r]])
                if ti == wbt:
                    pending_wb.append((n, r, slot))
            else:
                dma(SP, lambda e, r=r, slot=slot: e.dma_start(out=ring[:, r, :], in_=wbf_d[slot]),
                    sem_ring[r], reads=[B_wbf[slot]], writes=[B_ring[r]])

        def ring_of(i, slot):
            return stream_pos[(i, slot)] % RING

        def slot_done(i, slot):
            n = stream_pos[(i, slot)]
            assert next_load[0] == n + RING or next_load[0] >= len(stream), (next_load[0], n)
            issue_load()

        def rms_scale(src_ap, src_buf, width):
            (ss, ms, r), (b_ss, b_ms, b_r) = tiny_rot.next()
            op(ACT, lambda e: e.activation(out=junk[:, 0:width], in_=src_ap, func=AF.Square, accum_out=ss),
               reads=[src_buf], writes=[b_ss, B_junk])
            op(DVE, lambda e: e.tensor_scalar(out=ms, in0=ss, scalar1=1.0 / width, scalar2=EPS,
                                              op0=ALU.mult, op1=ALU.add),
               reads=[b_ss], writes=[b_ms])
            op(POOL, lambda e: e.tensor_tensor(out=r, in0=ms, in1=mhalf[:, 0:1], op=ALU.pow),
               reads=[b_ms, B_mhalf], writes=[b_r])
            return r, b_r

        def mm_group(out_ap, pairs, reads, writes):
            n = len(pairs)

            def fn(e):
                ins = None
                for q, (l, r_) in enumerate(pairs):
                    ins = e.matmul(out_ap, lhsT=l, rhs=r_, start=(q == 0), stop=(q == n - 1))
                return ins
            return op(PE, fn, reads=reads, writes=writes)

        def blk(r, q):
            return ring[:, r, q * 1024:(q + 1) * 1024].rearrange("p (k c) -> p k c", k=KC)

        def wide(r):
            return ring[:, r, :].rearrange("p (k c) -> p k c", k=KC)

        p0_state = {}
        xs_ctr = [0]

        def prepA(key, x_ap, x_buf):
            p0_state[key] = rms_scale(x_ap, x_buf, D)

        def prepB(key, x_ap, x_buf):
            r, b_r = p0_state[key]
            q = xs_ctr[0] % 3
            xs_ctr[0] += 1
            p0_state[key] = q
            op(DVE, lambda e: e.scalar_tensor_tensor(out=xsb[:, q, :], in0=x_ap, scalar=r, in1=gb[:, 0, :],
                                                     op0=ALU.mult, op1=ALU.mult),
               reads=[x_buf, b_r, B_gb0], writes=[B_xsb[q]])

        def prepC(key, hT_out_ap, hT_buf):
            q = p0_state.pop(key)

            def tr(e):
                ins = None
                for k in range(KC):
                    ins = e.transpose(psT[:, k * 128:(k + 1) * 128], xsb[:, q, k * 128:(k + 1) * 128], ident[:])
                return ins
            op(PE, tr, reads=[B_xsb[q], B_ident], writes=[B_psT])
            op(ACT, lambda e: e.activation(out=hT_out_ap, in_=psT[:].rearrange("p (k t) -> p k t", k=KC),
                                           func=AF.Copy),
               reads=[B_psT], writes=[hT_buf])

        def P0A(i, s):
            prepA((i, s), xbuf[:, i % 2, s, :], B_x[i % 2][s])

        def P0B(i, s):
            prepB((i, s), xbuf[:, i % 2, s, :], B_x[i % 2][s])

        def P0C(i, s):
            prepC((i, s), hT[:, i % 2, :, s * 128:(s + 1) * 128], B_hT[i % 2][s])

        def load_x(i, s):
            b = i % 2
            row0 = i * T + s * 128
            dma(SP, lambda e: e.dma_start(out=xbuf[:, b, s, :], in_=x_d[row0:row0 + 128, :]),
                sem_xl[b][s], reads=[], writes=[B_x[b][s]])

        def store_y(i, s):
            b = i % 2
            row0 = i * T + s * 128
            dma(SP, lambda e: e.dma_start(out=y_d[row0:row0 + 128, :], in_=xbuf[:, b, s, :]),
                sem_st[b][s], reads=[B_x[b][s]], writes=[])

        def P1(i):
            b = i % 2
            r0, r1 = ring_of(i, S_WV[0]), ring_of(i, S_WV[1])
            for s in range(NS):
                pb = alloc_pair()
                for h, rr in enumerate((r0, r1)):
                    w = wide(rr)
                    pairs = [(hT[:, b, k, s * 128:(s + 1) * 128], w[:, k, :]) for k in range(KC)]
                    mm_group(ps[:, pb + h, :], pairs, reads=[B_hT[b][s], B_ring[rr]], writes=[B_bank[pb + h]])
                    if s == NS - 1:
                        slot_done(i, S_WV[h])
                gq = s % 2
                gv_ap = gvb[:, gq, :]
                pair_ap = ps[:, pb:pb + 2, :].rearrange("p a b -> p (a b)")
                op(ACT, lambda e, gv_ap=gv_ap, pair_ap=pair_ap: e.activation(out=gv_ap, in_=pair_ap, func=AF.Gelu),
                   reads=[B_bank[pb], B_bank[pb + 1]], writes=[B_gvb[gq]])
                r, b_r = rms_scale(gv_ap, B_gvb[gq], D)
                op(DVE, lambda e, gv_ap=gv_ap, r=r, s=s: e.scalar_tensor_tensor(
                    out=vpp[:, s, :], in0=gv_ap, scalar=r, in1=gb[:, 1, :], op0=ALU.mult, op1=ALU.mult),
                   reads=[B_gvb[gq], b_r, B_gb], writes=[B_vpp[s]])

        def P3(i, j):
            b = i % 2
            rr = ring_of(i, S_P3[j])
            hTb = [B_hT[b][s] for s in range(NS)]

            def proj(q, N=T, rhs_fn=None):
                bk = alloc_bank()
                w = blk(rr, q)
                pairs = [(w[:, k, :], hT[:, b, k, :]) for k in range(KC)]
                mm_group(ps[:, bk, :], pairs, reads=hTb + [B_ring[rr]], writes=[B_bank[bk]])
                return bk

            hc_ap, hc_buf = hc_rot.next()
            bX = proj(0)
            if i == 0:
                bH = alloc_bank()
                wx, wc = blk(rr, 0), blk(rr, 1)
                mm_group(ps[:, bH, 0:2], [(wx[:, k, :], hTh[:, k, 0:2]) for k in range(KC)],
                         reads=[B_hTh, B_ring[rr]], writes=[B_bank[bH]])
            bC = proj(1)
            if i == 0:
                mm_group(ps[:, bH, 2:4], [(wc[:, k, :], hTh[:, k, 0:2]) for k in range(KC)],
                         reads=[B_hTh, B_ring[rr]], writes=[B_bank[bH]])
            bB = proj(2)
            bZ = proj(3)
            slot_done(i, S_P3[j])

            xs_ap, xs_buf = tmp_rot.next()
            op(ACT, lambda e: e.activation(out=xs_ap, in_=ps[:, bX, :], func=AF.Copy),
               reads=[B_bank[bX]], writes=[xs_buf])
            if i == 0:
                hx_ap, hx_buf = hxc_rot.next()
                op(ACT, lambda e: e.activation(out=hx_ap, in_=ps[:, bH, 0:4], func=AF.Copy),
                   reads=[B_bank[bH]], writes=[hx_buf])
                op(DVE, lambda e: e.tensor_tensor(out=halo[:, j, :], in0=hx_ap[:, 0:2], in1=hx_ap[:, 2:4],
                                                  op=ALU.mult),
                   reads=[hx_buf], writes=[B_halo[j]])
            op(ACT, lambda e: e.activation(out=hc_ap[:, 0:2], in_=halo[:, j, :], func=AF.Copy),
               reads=[B_halo[j]], writes=[hc_buf])
            op(DVE, lambda e: e.tensor_tensor(out=hc_ap[:, 2:T + 2], in0=xs_ap, in1=ps[:, bC, :], op=ALU.mult),
               reads=[xs_buf, B_bank[bC]], writes=[hc_buf])
            tz_ap, tz_buf = tmp_rot.next()
            op(ACT, lambda e: e.activation(out=tz_ap, in_=ps[:, bZ, :], func=AF.Tanh, scale=0.5),
               reads=[B_bank[bZ]], writes=[tz_buf])
            acc_ap, acc_buf = tmp_rot.next()
            op(ACT, lambda e: e.activation(out=acc_ap, in_=hc_ap[:, 0:T], func=AF.Copy,
                                           scale=convw[:, 3 * j:3 * j + 1]),
               reads=[hc_buf, B_convw], writes=[acc_buf])
            a_ap, a_buf = tmp_rot.next()
            op(DVE, lambda e: e.scalar_tensor_tensor(out=a_ap, in0=tz_ap, scalar=1.0, in1=ps[:, bZ, :],
                                                     op0=ALU.add, op1=ALU.mult),
               reads=[tz_buf, B_bank[bZ]], writes=[a_buf])
            for kk in (1, 2):
                op(DVE, lambda e, kk=kk: e.scalar_tensor_tensor(
                    out=acc_ap, in0=hc_ap[:, kk:kk + T], scalar=convw[:, 3 * j + kk:3 * j + kk + 1], in1=acc_ap,
                    op0=ALU.mult, op1=ALU.add),
                   reads=[hc_buf, acc_buf, B_convw], writes=[acc_buf])
            op(ACT, lambda e: e.activation(out=halo[:, j, :], in_=hc_ap[:, T:T + 2], func=AF.Copy),
               reads=[hc_buf], writes=[B_halo[j]])
            op(DVE, lambda e: e.tensor_tensor(out=acc_ap, in0=acc_ap, in1=ps[:, bB, :], op=ALU.mult),
               reads=[acc_buf, B_bank[bB]], writes=[acc_buf])
            op(POOL, lambda e: e.tensor_tensor(out=yB[:, j, :], in0=acc_ap, in1=a_ap, op=ALU.mult),
               reads=[acc_buf, a_buf], writes=[B_yB[j]])

        def P2(i, g):
            b = i % 2
            rr = ring_of(i, S_P2[g // 2])
            q0 = (g % 2) * 2
            hTb = [B_hT[b][s] for s in range(NS)]
            bS = alloc_bank()

            def sp(e):
                ins = None
                for s in range(NS):
                    ins = e.matmul(ps[:, bS, s * 128:(s + 1) * 128], lhsT=vpp[:, s, g * 128:(g + 1) * 128],
                                   rhs=WT[:, g, :], start=True, stop=True)
                return ins
            op(PE, sp, reads=B_vpp + [B_WT], writes=[B_bank[bS]])
            bU = alloc_bank()
            wu = blk(rr, q0)
            mm_group(ps[:, bU, :], [(wu[:, k, :], hT[:, b, k, :]) for k in range(KC)],
                     reads=hTb + [B_ring[rr]], writes=[B_bank[bU]])
            bZ = alloc_bank()
            wz = blk(rr, q0 + 1)
            mm_group(ps[:, bZ, :], [(wz[:, k, :], hT[:, b, k, :]) for k in range(KC)],
                     reads=hTb + [B_ring[rr]], writes=[B_bank[bZ]])
            if g % 2 == 1:
                slot_done(i, S_P2[g // 2])

            gu_ap, gu_buf = tmp_rot.next()
            op(ACT, lambda e: e.activation(out=gu_ap, in_=ps[:, bU, :], func=AF.Gelu),
               reads=[B_bank[bU]], writes=[gu_buf])
            tz_ap, tz_buf = tmp_rot.next()
            op(ACT, lambda e: e.activation(out=tz_ap, in_=ps[:, bZ, :], func=AF.Tanh, scale=0.5),
               reads=[B_bank[bZ]], writes=[tz_buf])
            m_ap, m_buf = tmp_rot.next()
            op(DVE, lambda e: e.tensor_tensor(
                out=m_ap.rearrange("p (a t) -> p a t", a=NS),
                in0=ps[:, bS, :].rearrange("p (a t) -> p a t", a=NS),
                in1=bt[:, g:g + 1, :].to_broadcast([128, NS, 128]), op=ALU.add),
               reads=[B_bank[bS], B_bt], writes=[m_buf])
            op(DVE, lambda e: e.tensor_tensor(out=m_ap, in0=m_ap, in1=gu_ap, op=ALU.mult),
               reads=[m_buf, gu_buf], writes=[m_buf])
            a_ap, a_buf = tmp_rot.next()
            op(DVE, lambda e: e.scalar_tensor_tensor(out=a_ap, in0=tz_ap, scalar=1.0, in1=ps[:, bZ, :],
                                                     op0=ALU.add, op1=ALU.mult),
               reads=[tz_buf, B_bank[bZ]], writes=[a_buf])
            op(POOL, lambda e: e.tensor_tensor(out=yA[:, g, :], in0=m_ap, in1=a_ap, op=ALU.mult),
               reads=[m_buf, a_buf], writes=[B_yA[g]])

        def P4(i, j):
            b = i % 2
            rr = ring_of(i, S_P4[j])
            hTb = [B_hT[b][s] for s in range(NS)]
            bGA = alloc_bank()
            w = blk(rr, 0)
            mm_group(ps[:, bGA, :], [(w[:, k, :], hT[:, b, k, :]) for k in range(KC)],
                     reads=hTb + [B_ring[rr]], writes=[B_bank[bGA]])
            bGB = alloc_bank()
            w = blk(rr, 1)
            mm_group(ps[:, bGB, :], [(w[:, k, :], hT[:, b, k, :]) for k in range(KC)],
                     reads=hTb + [B_ring[rr]], writes=[B_bank[bGB]])
            bYB = alloc_bank()
            w = blk(rr, 2)
            mm_group(ps[:, bYB, :], [(w[:, k, :], yB[:, k, :]) for k in range(KC)],
                     reads=B_yB + [B_ring[rr]], writes=[B_bank[bYB]])
            bYA = alloc_bank()
            w = blk(rr, 3)
            mm_group(ps[:, bYA, :], [(w[:, k, :], yA[:, k, :]) for k in range(KC)],
                     reads=B_yA + [B_ring[rr]], writes=[B_bank[bYA]])
            slot_done(i, S_P4[j])

            tA_ap, tA_buf = tmp_rot.next()
            op(ACT, lambda e: e.activation(out=tA_ap, in_=ps[:, bGA, :], func=AF.Tanh, scale=0.5),
               reads=[B_bank[bGA]], writes=[tA_buf])
            tB_ap, tB_buf = tmp_rot.next()
            op(ACT, lambda e: e.activation(out=tB_ap, in_=ps[:, bGB, :], func=AF.Tanh, scale=0.5),
               reads=[B_bank[bGB]], writes=[tB_buf])
            m2_ap, m2_buf = tmp_rot.next()
            op(DVE, lambda e: e.scalar_tensor_tensor(out=m2_ap, in0=tB_ap, scalar=1.0, in1=ps[:, bYB, :],
                                                     op0=ALU.add, op1=ALU.mult),
               reads=[tB_buf, B_bank[bYB]], writes=[m2_buf])
            m1_ap, m1_buf = tmp_rot.next()
            op(DVE, lambda e: e.scalar_tensor_tensor(out=m1_ap, in0=tA_ap, scalar=1.0, in1=ps[:, bYA, :],
                                                     op0=ALU.add, op1=ALU.mult),
               reads=[tA_buf, B_bank[bYA]], writes=[m1_buf])
            op(POOL, lambda e: e.tensor_tensor(out=mg[:, j, :], in0=m1_ap, in1=m2_ap, op=ALU.add),
               reads=[m1_buf, m2_buf], writes=[B_mg[j]])

        def P5(i):
            b = i % 2
            r0, r1 = ring_of(i, S_WO[0]), ring_of(i, S_WO[1])
            for s in range(NS):
                pb = alloc_pair()
                for h, rr in enumerate((r0, r1)):
                    w = wide(rr)
                    pairs = [(mg[:, k, s * 128:(s + 1) * 128], w[:, k, :]) for k in range(KC)]
                    mm_group(ps[:, pb + h, :], pairs, reads=B_mg + [B_ring[rr]], writes=[B_bank[pb + h]])
                    if s == NS - 1:
                        slot_done(i, S_WO[h])
                x_ap = xbuf[:, b, s, :]
                pair_ap = ps[:, pb:pb + 2, :].rearrange("p a b -> p (a b)")
                op(DVE, lambda e, x_ap=x_ap, pair_ap=pair_ap: e.scalar_tensor_tensor(
                    out=x_ap, in0=pair_ap, scalar=0.25, in1=x_ap, op0=ALU.mult, op1=ALU.add),
                   reads=[B_bank[pb], B_bank[pb + 1], B_x[b][s]], writes=[B_x[b][s]])
                r, b_r = rms_scale(x_ap, B_x[b][s], D)
                op(DVE, lambda e, x_ap=x_ap, r=r: e.scalar_tensor_tensor(
                    out=x_ap, in0=x_ap, scalar=r, in1=gb[:, 2, :], op0=ALU.mult, op1=ALU.mult),
                   reads=[B_x[b][s], b_r, B_gb], writes=[B_x[b][s]])

        load_x(0, 0)
        op(POOL, lambda e: e.memset(mhalf[:], -0.5), writes=[B_mhalf])
        op(ACT, lambda e: e.activation(out=junk[:, 0:1], in_=mhalf[:, 0:1], func=AF.Square),
           reads=[B_mhalf], writes=[B_junk])
        op(POOL, lambda e: e.memset(ones[:], 1.0), writes=[B_ones])
        op(POOL, lambda e: e.memset(xhb[:], 0.0), writes=[B_xhb])
        op(POOL, lambda e: e.affine_select(out=ident[:], in_=ones[:], pattern=[[1, 128]], compare_op=ALU.is_equal,
                                           fill=0.0, base=0, channel_multiplier=-1),
           reads=[B_ones], writes=[B_ident])
        dma(SP, lambda e: e.dma_start(out=gb[:, 0, :], in_=gains_d[:, 0:D].partition_broadcast(128)[:, 0, :]),
            sem_g0, writes=[B_gb0])
        for s in range(1, NS):
            load_x(0, s)
        HK = "h"
        stA = {HK: lambda: prepA(HK, xhb[:], B_xhb)}
        stB = {HK: lambda: prepB(HK, xhb[:], B_xhb)}
        stC = {HK: lambda: prepC(HK, hTh[:], B_hTh)}
        for s in range(NS):
            stA[s] = lambda s=s: P0A(0, s)
            stB[s] = lambda s=s: P0B(0, s)
            stC[s] = lambda s=s: P0C(0, s)
        order = [0, 1, 2, 3, HK]
        stA[order[0]]()
        for _ in range(2):
            issue_load()
        dma(SP, lambda e: e.dma_start(out=xhb[0:2, :], in_=xh_d), sem_setup, reads=[B_ring[1]], writes=[B_xhb])
        dma(SP, lambda e: e.dma_start(out=gb[:, 1:3, :].rearrange("p a d -> p (a d)"),
                                      in_=gains_d[:, D:3 * D].partition_broadcast(128)[:, 0, :]),
            sem_setup, writes=[B_gb])
        dma(SP, lambda e: e.dma_start(out=convw[:], in_=convw_d), sem_setup2, writes=[B_convw])
        dma(SP, lambda e: e.dma_start(out=wsT32[:].rearrange("p a d -> p (a d)"), in_=wsT_d), sem_setup2,
            writes=[B_wsT32])
        dma(SP, lambda e: e.dma_start(out=bt[:].rearrange("p a d -> p (a d)"),
                                      in_=bsp_d.partition_broadcast(128)[:, 0, :]), sem_setup2, writes=[B_bt])
        for b_ in (B_xhb, B_gb):
            b_.writer = (sem_setup, sem_setup.count)
        B_ring[1].readers[sem_setup] = sem_setup.count
        for b_ in (B_wsT32, B_bt, B_convw):
            b_.writer = (sem_setup2, sem_setup2.count)

        stB[order[0]]()
        for n in range(1, len(order)):
            stA[order[n]]()
            stC[order[n - 1]]()
            stB[order[n]]()
        stC[order[-1]]()
        for _ in range(RING - 2):
            issue_load()
        op(POOL, lambda e: e.affine_select(out=WT[:], in_=wsT32[:], pattern=[[0, 8], [1, 128]],
                                           compare_op=ALU.is_ge, fill=0.0, base=0, channel_multiplier=-1),
           reads=[B_wsT32], writes=[B_WT])
        for s in range(NS):
            load_x(1, s)

        P1(0)
        for i in range(NT):
            for j in range(KC):
                P3(i, j)
                if i >= 1 and j == 1:
                    for s in range(NS):
                        store_y(i - 1, s)
                    if i + 1 < NT:
                        for s in range(NS):
                            load_x(i + 1, s)
                if i + 1 < NT:
                    if 6 <= j <= 7:
                        P0C(i + 1, j - 6)
                    if 4 <= j <= 7:
                        P0B(i + 1, j - 4)
                    if 3 <= j <= 6:
                        P0A(i + 1, j - 3)
            for g in range(KC):
                P2(i, g)
                if i + 1 < NT and g <= 1:
                    P0C(i + 1, 2 + g)
            for j in range(KC):
                P4(i, j)
            if i + 1 < NT:
                P1(i + 1)
            P5(i)
        for s in range(NS):
            store_y(NT - 1, s)
        fin = []
        for b in range(2):
            for s in range(NS):
                fin.append((sem_st[b][s], sem_st[b][s].count))

        with nc.Block() as block:
            @block.sync
            def _(e):
                emit(SP, e)
                for s_, v in fin:
                    e.wait_ge(s_.h, v)

            @block.tensor
            def _(e):
                emit(PE, e)

            @block.scalar
            def _(e):
                emit(ACT, e)

            @block.vector
            def _(e):
                emit(DVE, e)

            @block.gpsimd
            def _(e):
                emit(POOL, e)
    return nc


def _blocks(w, col0, ncols):
    return w[:, col0:col0 + ncols].reshape(KC, 128, ncols).transpose(1, 0, 2)


def build_stream(w_in, w_a, w_b, w_out):
    st = np.empty((NSLOT, 128, SLOT_ELEMS), dtype=np.float32)
    U, V, ZA, XB, CB, BB, ZB, GA, GB = [q * D for q in range(9)]
    for h in range(2):
        st[S_WV[h]] = _blocks(w_in, V + h * 512, 512).reshape(128, SLOT_ELEMS)
        st[S_WO[h]] = _blocks(w_out, h * 512, 512).reshape(128, SLOT_ELEMS)
    for j in range(KC):
        c = j * 128
        st[S_P3[j]] = np.stack([_blocks(w_in, XB + c, 128), _blocks(w_in, CB + c, 128),
                                _blocks(w_in, BB + c, 128), _blocks(w_in, ZB + c, 128)], axis=1).reshape(128, SLOT_ELEMS)
        st[S_P4[j]] = np.stack([_blocks(w_in, GA + c, 128), _blocks(w_in, GB + c, 128),
                                _blocks(w_b, c, 128), _blocks(w_a, c, 128)], axis=1).reshape(128, SLOT_ELEMS)
    for q in range(4):
        g0, g1 = 2 * q, 2 * q + 1
        st[S_P2[q]] = np.stack([_blocks(w_in, U + g0 * 128, 128), _blocks(w_in, ZA + g0 * 128, 128),
                                _blocks(w_in, U + g1 * 128, 128), _blocks(w_in, ZA + g1 * 128, 128)],
                               axis=1).reshape(128, SLOT_ELEMS)
    return st


_NC_CACHE = {}


def kernel(x, norm_g, w_in, v_norm_g, w_spatial, b_spatial, conv_w, w_branch_a, w_branch_b, w_out, final_norm_g):
    x = np.asarray(x, dtype=np.float32)
    B, S, _ = x.shape
    assert (B, S) == (4, 8192)
    w_in = np.asarray(w_in, np.float32)[0]
    wst = build_stream(w_in, np.asarray(w_branch_a, np.float32)[0], np.asarray(w_branch_b, np.float32)[0],
                       np.asarray(w_out, np.float32)[0])
    gains = np.concatenate([np.asarray(norm_g, np.float32)[0], np.asarray(v_norm_g, np.float32)[0],
                            np.asarray(final_norm_g, np.float32)]).reshape(1, 3 * D)
    bsp = np.asarray(b_spatial, np.float32)[0].reshape(1, 8 * 128)
    convw = np.ascontiguousarray(np.asarray(conv_w, np.float32)[0].reshape(3, KC, 128).transpose(2, 1, 0)).reshape(128, 24)
    wsT = np.ascontiguousarray(np.asarray(w_spatial, np.float32)[0].transpose(2, 0, 1)).reshape(128, 8 * 128)

    in_maps = []
    for c in range(NCORES):
        b, half = c // 2, c % 2
        xc = np.ascontiguousarray(x[b, half * TOK:(half + 1) * TOK, :])
        if half == 0:
            xh = np.zeros((2, D), np.float32)
        else:
            xh = np.ascontiguousarray(x[b, TOK - 2:TOK, :])
        in_maps.append({"x": xc, "xh": xh, "wst": wst, "gains": gains, "bsp": bsp, "convw": convw, "wsT": wsT})

    if "nc" not in _NC_CACHE:
        _NC_CACHE["nc"] = build_program()
    nc = _NC_CACHE["nc"]
    res = run_bass_kernel_spmd(nc, in_maps, core_ids=list(range(NCORES)))
    out = np.empty((B, S, D), np.float32)
    for c in range(NCORES):
        b, half = c // 2, c % 2
        out[b, half * TOK:(half + 1) * TOK, :] = res.results[c]["y"]
    return out
```

```python
import numpy as np
from contextlib import ExitStack

import concourse.bass as bass
import concourse.mybir as mybir
from concourse.bass_utils import run_bass_kernel_spmd

F32 = mybir.dt.float32
BF16 = mybir.dt.bfloat16
AF = mybir.ActivationFunctionType
ALU = mybir.AluOpType

D = 1024
NCORES = 8
TOK = 4096
T = 512
NT = TOK // T
NS = 4
KC = 8
EPS = 1e-6
RING = 4
NSLOT = 24
SLOT_ELEMS = 4096
WB_SPREAD = 4
WBT = {0: 2, 1: 3, 2: 1, 3: 0, 4: 0, 5: 2, 6: 3, 7: 0, 8: 2, 9: 3, 10: 0, 11: 2, 12: 3, 13: 1, 14: 4, 15: 0,
       16: 2, 17: 3, 18: 0, 19: 2, 20: 3, 21: 1, 22: 4, 23: 4}

S_WV = [0, 1]
S_P3 = list(range(2, 10))
S_P2 = list(range(10, 14))
S_P4 = list(range(14, 22))
S_WO = [22, 23]


class Sem:
    def __init__(self, h, name):
        self.h = h
        self.name = name
        self.count = 0


class Buf:
    __slots__ = ("name", "writer", "readers")

    def __init__(self, name):
        self.name = name
        self.writer = None
        self.readers = {}


class Eng:
    def __init__(self, name, sem, self_sync=True):
        self.name = name
        self.sem = sem
        self.ops = []
        self.waited = {}
        self.self_sync = self_sync


def _collect_waits(eng, reads, writes):
    deps = {}

    def add(tok):
        if tok is None:
            return
        s, v = tok
        if deps.get(s, 0) < v:
            deps[s] = v

    for b in reads:
        add(b.writer)
    for b in writes:
        add(b.writer)
        for s, v in b.readers.items():
            add((s, v))
    waits = []
    for s, v in deps.items():
        if s is eng.sem and not eng.self_sync:
            continue
        if eng.waited.get(s, 0) < v:
            eng.waited[s] = v
            waits.append((s, v))
    return waits


def _commit(tok, reads, writes):
    s, v = tok
    for b in reads:
        if b.readers.get(s, 0) < v:
            b.readers[s] = v
    for b in writes:
        b.writer = tok
        b.readers = {}


def op(eng, fn, reads=(), writes=()):
    waits = _collect_waits(eng, reads, writes)
    eng.sem.count += 1
    tok = (eng.sem, eng.sem.count)
    eng.ops.append((waits, fn, eng.sem, 1))
    _commit(tok, reads, writes)
    return tok


def dma(eng, fn, dsem, reads=(), writes=()):
    waits = _collect_waits(eng, reads, writes)
    dsem.count += 16
    tok = (dsem, dsem.count)
    eng.ops.append((waits, fn, dsem, 16))
    _commit(tok, reads, writes)
    return tok


def emit(eng, e):
    for waits, fn, sem, inc in eng.ops:
        for s, v in waits:
            e.wait_ge(s.h, v)
        ins = fn(e)
        ins.then_inc(sem.h, inc)


class Rot:
    def __init__(self, items):
        self.items = items
        self.i = 0

    def next(self):
        it = self.items[self.i % len(self.items)]
        self.i += 1
        return it


def build_program():
    nc = bass.Bass("TRN2", target_bir_lowering=False)
    x_d = nc.dram_tensor("x", [TOK, D], F32, kind="ExternalInput").ap()
    xh_d = nc.dram_tensor("xh", [2, D], F32, kind="ExternalInput").ap()
    wst_d = nc.dram_tensor("wst", [NSLOT, 128, SLOT_ELEMS], F32, kind="ExternalInput").ap()
    gains_d = nc.dram_tensor("gains", [1, 3 * D], F32, kind="ExternalInput").ap()
    bsp_d = nc.dram_tensor("bsp", [1, 8 * 128], F32, kind="ExternalInput").ap()
    convw_d = nc.dram_tensor("convw", [128, 24], F32, kind="ExternalInput").ap()
    wsT_d = nc.dram_tensor("wsT", [128, 8 * 128], F32, kind="ExternalInput").ap()
    y_d = nc.dram_tensor("y", [TOK, D], F32, kind="ExternalOutput").ap()
    wbf_d = nc.dram_tensor("wbf", [NSLOT, 128, SLOT_ELEMS], BF16).ap()

    with ExitStack() as es:
        def sb(name, shape, dt):
            return es.enter_context(nc.sbuf_tensor("sb_" + name, shape, dt))

        def new_sem(name):
            return Sem(es.enter_context(nc.semaphore(name)), name)

        xbuf = sb("xbuf", [128, 2, NS, D], F32)
        xsb = sb("xsb", [128, 3, D], BF16)
        hT = sb("hT", [128, 2, KC, T], BF16)
        hTh = sb("hTh", [128, KC, 128], BF16)
        xhb = sb("xhb", [128, D], F32)
        gvb = sb("gvb", [128, 2, D], F32)
        junk = sb("junk", [128, D], BF16)
        vpp = sb("vpp", [128, NS, D], BF16)
        yA = sb("yA", [128, KC, T], BF16)
        yB = sb("yB", [128, KC, T], BF16)
        mg = sb("mg", [128, KC, T], BF16)
        NTMP = 19
        tmp = sb("tmp", [128, NTMP, T], F32)
        hcb = sb("hcb", [128, 2, T + 4], F32)
        gb = sb("gb", [128, 3, D], F32)
        bt = sb("bt", [128, 8, 128], F32)
        wsT32 = sb("wsT32", [128, 8, 128], F32)
        WT = sb("WT", [128, 8, 128], BF16)
        ident = sb("ident", [128, 128], BF16)
        ones = sb("ones", [128, 128], F32)
        convw = sb("convw", [128, 24], F32)
        halo = sb("halo", [128, 8, 2], F32)
        tiny = sb("tiny", [128, 3, 16], F32)
        mhalf = sb("mhalf", [128, 1], F32)
        hxc = sb("hxc", [128, 2, 4], F32)
        ring = sb("ring", [128, RING, SLOT_ELEMS], BF16)

        ps = es.enter_context(nc.psum_tensor("ps_main", [128, 7, 512], F32))
        psT = es.enter_context(nc.psum_tensor("ps_tr", [128, D], BF16))

        PE = Eng("pe", new_sem("s_pe"), self_sync=False)
        ACT = Eng("act", new_sem("s_act"))
        DVE = Eng("dve", new_sem("s_dve"))
        POOL = Eng("pool", new_sem("s_pool"))
        SP = Eng("sp", new_sem("s_sp"))
        sem_setup = new_sem("d_setup")
        sem_setup2 = new_sem("d_setup2")
        sem_g0 = new_sem("d_g0")
        sem_ring = [new_sem(f"d_ring{r}") for r in range(RING)]
        sem_xl = [[new_sem(f"d_xl{b}{s}") for s in range(NS)] for b in range(2)]
        sem_st = [[new_sem(f"d_st{b}{s}") for s in range(NS)] for b in range(2)]
        sem_wb = [new_sem(f"d_wb{r}") for r in range(RING)]
        sem_ringc = [new_sem(f"d_ringc{r}") for r in range(RING)]

        B_x = [[Buf(f"x{b}{s}") for s in range(NS)] for b in range(2)]
        B_xsb = [Buf(f"xsb{b}") for b in range(3)]
        B_hT = [[Buf(f"hT{b}{s}") for s in range(NS)] for b in range(2)]
        B_hTh = Buf("hTh")
        B_xhb = Buf("xhb")
        B_gvb = [Buf(f"gvb{b}") for b in range(2)]
        B_vpp = [Buf(f"vpp{s}") for s in range(NS)]
        B_yA = [Buf(f"yA{j}") for j in range(KC)]
        B_yB = [Buf(f"yB{j}") for j in range(KC)]
        B_mg = [Buf(f"mg{j}") for j in range(KC)]
        B_bank = [Buf(f"bank{b}") for b in range(7)]
        B_psT = Buf("psT")
        B_ring = [Buf(f"ring{r}") for r in range(RING)]
        B_wbf = [Buf(f"wbf{k}") for k in range(NSLOT)]
        B_gb = Buf("gb")
        B_gb0 = Buf("gb0")
        B_bt = Buf("bt")
        B_wsT32 = Buf("wsT32")
        B_WT = Buf("WT")
        B_ident = Buf("ident")
        B_ones = Buf("ones")
        B_convw = Buf("convw")
        B_halo = [Buf(f"halo{j}") for j in range(KC)]
        B_mhalf = Buf("mhalf")
        B_junk = Buf("junk")

        tmp_rot = Rot([(tmp[:, q, :], Buf(f"tmp{q}")) for q in range(NTMP)])
        hc_rot = Rot([(hcb[:, q, :], Buf(f"hc{q}")) for q in range(2)])
        hxc_rot = Rot([(hxc[:, q, :], Buf(f"hxc{q}")) for q in range(2)])
        tiny_rot = Rot([((tiny[:, 0, q:q + 1], tiny[:, 1, q:q + 1], tiny[:, 2, q:q + 1]),
                         (Buf(f"ss{q}"), Buf(f"ms{q}"), Buf(f"r{q}"))) for q in range(16)])

        bank_ptr = [0]

        def alloc_bank():
            b = bank_ptr[0] % 7
            bank_ptr[0] += 1
            return b

        def alloc_pair():
            while (bank_ptr[0] % 7) % 2 == 1 or (bank_ptr[0] % 7) == 6:
                bank_ptr[0] += 1
            b = bank_ptr[0] % 7
            bank_ptr[0] += 2
            return b

        stream = []
        stream += [(0, s) for s in S_WV]
        for i in range(NT):
            stream += [(i, s) for s in S_P3 + S_P2 + S_P4]
            if i + 1 < NT:
                stream += [(i + 1, s) for s in S_WV]
            stream += [(i, s) for s in S_WO]
        stream_pos = {ts: n for n, ts in enumerate(stream)}
        next_load = [0]

        pending_wb = []

        def flush_wb(upto):
            while pending_wb and pending_wb[0][0] <= upto:
                _, r, slot = pending_wb.pop(0)
                dma(SP, lambda e, r=r, slot=slot: e.dma_start(out=wbf_d[slot], in_=ring[:, r, :]),
                    sem_wb[r], reads=[B_ring[r]], writes=[B_wbf[slot]])

        def issue_load():
            n = next_load[0]
            flush_wb(n - 2)
            if n >= len(stream):
                return
            next_load[0] += 1
            ti, slot = stream[n]
            r = n % RING
            wbt = WBT[slot]
            if ti <= wbt:
                extra = []
                dma(POOL, lambda e, r=r, slot=slot: e.dma_start(out=ring[:, r, :], in_=wst_d[slot]),
                    sem_ringc[r], reads=extra, writes=[B_ring[r]])
                if ti == wbt:
                    pending_wb.append((n, r, slot))
            else:
                dma(SP, lambda e, r=r, slot=slot: e.dma_start(out=ring[:, r, :], in_=wbf_d[slot]),
                    sem_ring[r], reads=[B_wbf[slot]], writes=[B_ring[r]])

        def ring_of(i, slot):
            return stream_pos[(i, slot)] % RING

        def slot_done(i, slot):
            n = stream_pos[(i, slot)]
            assert next_load[0] == n + RING or next_load[0] >= len(stream), (next_load[0], n)
            issue_load()

        def rms_scale(src_ap, src_buf, width):
            (ss, ms, r), (b_ss, b_ms, b_r) = tiny_rot.next()
            op(ACT, lambda e: e.activation(out=junk[:, 0:width], in_=src_ap, func=AF.Square, accum_out=ss),
               reads=[src_buf], writes=[b_ss, B_junk])
            op(DVE, lambda e: e.tensor_scalar(out=ms, in0=ss, scalar1=1.0 / width, scalar2=EPS,
                                              op0=ALU.mult, op1=ALU.add),
               reads=[b_ss], writes=[b_ms])
            op(POOL, lambda e: e.tensor_tensor(out=r, in0=ms, in1=mhalf[:, 0:1], op=ALU.pow),
               reads=[b_ms, B_mhalf], writes=[b_r])
            return r, b_r

        def mm_group(out_ap, pairs, reads, writes):
            n = len(pairs)

            def fn(e):
                ins = None
                for q, (l, r_) in enumerate(pairs):
                    ins = e.matmul(out_ap, lhsT=l, rhs=r_, start=(q == 0), stop=(q == n - 1))
                return ins
            return op(PE, fn, reads=reads, writes=writes)

        def blk(r, q):
            return ring[:, r, q * 1024:(q + 1) * 1024].rearrange("p (k c) -> p k c", k=KC)

        def wide(r):
            return ring[:, r, :].rearrange("p (k c) -> p k c", k=KC)

        p0_state = {}
        xs_ctr = [0]

        def prepA(key, x_ap, x_buf):
            p0_state[key] = rms_scale(x_ap, x_buf, D)

        def prepB(key, x_ap, x_buf):
            r, b_r = p0_state[key]
            q = xs_ctr[0] % 3
            xs_ctr[0] += 1
            p0_state[key] = q
            op(DVE, lambda e: e.scalar_tensor_tensor(out=xsb[:, q, :], in0=x_ap, scalar=r, in1=gb[:, 0, :],
                                                     op0=ALU.mult, op1=ALU.mult),
               reads=[x_buf, b_r, B_gb0], writes=[B_xsb[q]])

        def prepC(key, hT_out_ap, hT_buf):
            q = p0_state.pop(key)

            def tr(e):
                ins = None
                for k in range(KC):
                    ins = e.transpose(psT[:, k * 128:(k + 1) * 128], xsb[:, q, k * 128:(k + 1) * 128], ident[:])
                return ins
            op(PE, tr, reads=[B_xsb[q], B_ident], writes=[B_psT])
            op(ACT, lambda e: e.activation(out=hT_out_ap, in_=psT[:].rearrange("p (k t) -> p k t", k=KC),
                                           func=AF.Copy),
               reads=[B_psT], writes=[hT_buf])

        def P0A(i, s):
            prepA((i, s), xbuf[:, i % 2, s, :], B_x[i % 2][s])

        def P0B(i, s):
            prepB((i, s), xbuf[:, i % 2, s, :], B_x[i % 2][s])

        def P0C(i, s):
            prepC((i, s), hT[:, i % 2, :, s * 128:(s + 1) * 128], B_hT[i % 2][s])

        def load_x(i, s):
            b = i % 2
            row0 = i * T + s * 128
            dma(SP, lambda e: e.dma_start(out=xbuf[:, b, s, :], in_=x_d[row0:row0 + 128, :]),
                sem_xl[b][s], reads=[], writes=[B_x[b][s]])

        def store_y(i, s):
            b = i % 2
            row0 = i * T + s * 128
            dma(SP, lambda e: e.dma_start(out=y_d[row0:row0 + 128, :], in_=xbuf[:, b, s, :]),
                sem_st[b][s], reads=[B_x[b][s]], writes=[])

        def P1(i):
            b = i % 2
            r0, r1 = ring_of(i, S_WV[0]), ring_of(i, S_WV[1])
            for s in range(NS):
                pb = alloc_pair()
                for h, rr in enumerate((r0, r1)):
                    w = wide(rr)
                    pairs = [(hT[:, b, k, s * 128:(s + 1) * 128], w[:, k, :]) for k in range(KC)]
                    mm_group(ps[:, pb + h, :], pairs, reads=[B_hT[b][s], B_ring[rr]], writes=[B_bank[pb + h]])
                    if s == NS - 1:
                        slot_done(i, S_WV[h])
                gq = s % 2
                gv_ap = gvb[:, gq, :]
                pair_ap = ps[:, pb:pb + 2, :].rearrange("p a b -> p (a b)")
                op(ACT, lambda e, gv_ap=gv_ap, pair_ap=pair_ap: e.activation(out=gv_ap, in_=pair_ap, func=AF.Gelu),
                   reads=[B_bank[pb], B_bank[pb + 1]], writes=[B_gvb[gq]])
                r, b_r = rms_scale(gv_ap, B_gvb[gq], D)
                op(DVE, lambda e, gv_ap=gv_ap, r=r, s=s: e.scalar_tensor_tensor(
                    out=vpp[:, s, :], in0=gv_ap, scalar=r, in1=gb[:, 1, :], op0=ALU.mult, op1=ALU.mult),
                   reads=[B_gvb[gq], b_r, B_gb], writes=[B_vpp[s]])

        def P3(i, j):
            b = i % 2
            rr = ring_of(i, S_P3[j])
            hTb = [B_hT[b][s] for s in range(NS)]

            def proj(q, N=T, rhs_fn=None):
                bk = alloc_bank()
                w = blk(rr, q)
                pairs = [(w[:, k, :], hT[:, b, k, :]) for k in range(KC)]
                mm_group(ps[:, bk, :], pairs, reads=hTb + [B_ring[rr]], writes=[B_bank[bk]])
                return bk

            hc_ap, hc_buf = hc_rot.next()
            bX = proj(0)
            if i == 0:
                bH = alloc_bank()
                wx, wc = blk(rr, 0), blk(rr, 1)
                mm_group(ps[:, bH, 0:2], [(wx[:, k, :], hTh[:, k, 0:2]) for k in range(KC)],
                         reads=[B_hTh, B_ring[rr]], writes=[B_bank[bH]])
            bC = proj(1)
            if i == 0:
                mm_group(ps[:, bH, 2:4], [(wc[:, k, :], hTh[:, k, 0:2]) for k in range(KC)],
                         reads=[B_hTh, B_ring[rr]], writes=[B_bank[bH]])
            bB = proj(2)
            bZ = proj(3)
            slot_done(i, S_P3[j])

            xs_ap, xs_buf = tmp_rot.next()
            op(ACT, lambda e: e.activation(out=xs_ap, in_=ps[:, bX, :], func=AF.Copy),
               reads=[B_bank[bX]], writes=[xs_buf])
            if i == 0:
                hx_ap, hx_buf = hxc_rot.next()
                op(ACT, lambda e: e.activation(out=hx_ap, in_=ps[:, bH, 0:4], func=AF.Copy),
                   reads=[B_bank[bH]], writes=[hx_buf])
                op(DVE, lambda e: e.tensor_tensor(out=halo[:, j, :], in0=hx_ap[:, 0:2], in1=hx_ap[:, 2:4],
                                                  op=ALU.mult),
                   reads=[hx_buf], writes=[B_halo[j]])
            op(ACT, lambda e: e.activation(out=hc_ap[:, 0:2], in_=halo[:, j, :], func=AF.Copy),
               reads=[B_halo[j]], writes=[hc_buf])
            op(DVE, lambda e: e.tensor_tensor(out=hc_ap[:, 2:T + 2], in0=xs_ap, in1=ps[:, bC, :], op=ALU.mult),
               reads=[xs_buf, B_bank[bC]], writes=[hc_buf])
            tz_ap, tz_buf = tmp_rot.next()
            op(ACT, lambda e: e.activation(out=tz_ap, in_=ps[:, bZ, :], func=AF.Tanh, scale=0.5),
               reads=[B_bank[bZ]], writes=[tz_buf])
            acc_ap, acc_buf = tmp_rot.next()
            op(ACT, lambda e: e.activation(out=acc_ap, in_=hc_ap[:, 0:T], func=AF.Copy,
                                           scale=convw[:, 3 * j:3 * j + 1]),
               reads=[hc_buf, B_convw], writes=[acc_buf])
            a_ap, a_buf = tmp_rot.next()
            op(DVE, lambda e: e.scalar_tensor_tensor(out=a_ap, in0=tz_ap, scalar=1.0, in1=ps[:, bZ, :],
                                                     op0=ALU.add, op1=ALU.mult),
               reads=[tz_buf, B_bank[bZ]], writes=[a_buf])
            for kk in (1, 2):
                op(DVE, lambda e, kk=kk: e.scalar_tensor_tensor(
                    out=acc_ap, in0=hc_ap[:, kk:kk + T], scalar=convw[:, 3 * j + kk:3 * j + kk + 1], in1=acc_ap,
                    op0=ALU.mult, op1=ALU.add),
                   reads=[hc_buf, acc_buf, B_convw], writes=[acc_buf])
            op(ACT, lambda e: e.activation(out=halo[:, j, :], in_=hc_ap[:, T:T + 2], func=AF.Copy),
               reads=[hc_buf], writes=[B_halo[j]])
            op(DVE, lambda e: e.tensor_tensor(out=acc_ap, in0=acc_ap, in1=ps[:, bB, :], op=ALU.mult),
               reads=[acc_buf, B_bank[bB]], writes=[acc_buf])
            op(POOL, lambda e: e.tensor_tensor(out=yB[:, j, :], in0=acc_ap, in1=a_ap, op=ALU.mult),
               reads=[acc_buf, a_buf], writes=[B_yB[j]])

        def P2(i, g):
            b = i % 2
            rr = ring_of(i, S_P2[g // 2])
            q0 = (g % 2) * 2
            hTb = [B_hT[b][s] for s in range(NS)]
            bS = alloc_bank()

            def sp(e):
                ins = None
                for s in range(NS):
                    ins = e.matmul(ps[:, bS, s * 128:(s + 1) * 128], lhsT=vpp[:, s, g * 128:(g + 1) * 128],
                                   rhs=WT[:, g, :], start=True, stop=True)
                return ins
            op(PE, sp, reads=B_vpp + [B_WT], writes=[B_bank[bS]])
            bU = alloc_bank()
            wu = blk(rr, q0)
            mm_group(ps[:, bU, :], [(wu[:, k, :], hT[:, b, k, :]) for k in range(KC)],
                     reads=hTb + [B_ring[rr]], writes=[B_bank[bU]])
            bZ = alloc_bank()
            wz = blk(rr, q0 + 1)
            mm_group(ps[:, bZ, :], [(wz[:, k, :], hT[:, b, k, :]) for k in range(KC)],
                     reads=hTb + [B_ring[rr]], writes=[B_bank[bZ]])
            if g % 2 == 1:
                slot_done(i, S_P2[g // 2])

            gu_ap, gu_buf = tmp_rot.next()
            op(ACT, lambda e: e.activation(out=gu_ap, in_=ps[:, bU, :], func=AF.Gelu),
               reads=[B_bank[bU]], writes=[gu_buf])
            tz_ap, tz_buf = tmp_rot.next()
            op(ACT, lambda e: e.activation(out=tz_ap, in_=ps[:, bZ, :], func=AF.Tanh, scale=0.5),
               reads=[B_bank[bZ]], writes=[tz_buf])
            m_ap, m_buf = tmp_rot.next()
            op(DVE, lambda e: e.tensor_tensor(
                out=m_ap.rearrange("p (a t) -> p a t", a=NS),
                in0=ps[:, bS, :].rearrange("p (a t) -> p a t", a=NS),
                in1=bt[:, g:g + 1, :].to_broadcast([128, NS, 128]), op=ALU.add),
               reads=[B_bank[bS], B_bt], writes=[m_buf])
            op(DVE, lambda e: e.tensor_tensor(out=m_ap, in0=m_ap, in1=gu_ap, op=ALU.mult),
               reads=[m_buf, gu_buf], writes=[m_buf])
            a_ap, a_buf = tmp_rot.next()
            op(DVE, lambda e: e.scalar_tensor_tensor(out=a_ap, in0=tz_ap, scalar=1.0, in1=ps[:, bZ, :],
                                                     op0=ALU.add, op1=ALU.mult),
               reads=[tz_buf, B_bank[bZ]], writes=[a_buf])
            op(POOL, lambda e: e.tensor_tensor(out=yA[:, g, :], in0=m_ap, in1=a_ap, op=ALU.mult),
               reads=[m_buf, a_buf], writes=[B_yA[g]])

        def P4(i, j):
            b = i % 2
            rr = ring_of(i, S_P4[j])
            hTb = [B_hT[b][s] for s in range(NS)]
            bGA = alloc_bank()
            w = blk(rr, 0)
            mm_group(ps[:, bGA, :], [(w[:, k, :], hT[:, b, k, :]) for k in range(KC)],
                     reads=hTb + [B_ring[rr]], writes=[B_bank[bGA]])
            bGB = alloc_bank()
            w = blk(rr, 1)
            mm_group(ps[:, bGB, :], [(w[:, k, :], hT[:, b, k, :]) for k in range(KC)],
                     reads=hTb + [B_ring[rr]], writes=[B_bank[bGB]])
            bYB = alloc_bank()
            w = blk(rr, 2)
            mm_group(ps[:, bYB, :], [(w[:, k, :], yB[:, k, :]) for k in range(KC)],
                     reads=B_yB + [B_ring[rr]], writes=[B_bank[bYB]])
            bYA = alloc_bank()
            w = blk(rr, 3)
            mm_group(ps[:, bYA, :], [(w[:, k, :], yA[:, k, :]) for k in range(KC)],
                     reads=B_yA + [B_ring[rr]], writes=[B_bank[bYA]])
            slot_done(i, S_P4[j])

            tA_ap, tA_buf = tmp_rot.next()
            op(ACT, lambda e: e.activation(out=tA_ap, in_=ps[:, bGA, :], func=AF.Tanh, scale=0.5),
               reads=[B_bank[bGA]], writes=[tA_buf])
            tB_ap, tB_buf = tmp_rot.next()
            op(ACT, lambda e: e.activation(out=tB_ap, in_=ps[:, bGB, :], func=AF.Tanh, scale=0.5),
               reads=[B_bank[bGB]], writes=[tB_buf])
            m2_ap, m2_buf = tmp_rot.next()
            op(DVE, lambda e: e.scalar_tensor_tensor(out=m2_ap, in0=tB_ap, scalar=1.0, in1=ps[:, bYB, :],
                                                     op0=ALU.add, op1=ALU.mult),
               reads=[tB_buf, B_bank[bYB]], writes=[m2_buf])
            m1_ap, m1_buf = tmp_rot.next()
            op(DVE, lambda e: e.scalar_tensor_tensor(out=m1_ap, in0=tA_ap, scalar=1.0, in1=ps[:, bYA, :],
                                                     op0=ALU.add, op1=ALU.mult),
               reads=[tA_buf, B_bank[bYA]], writes=[m1_buf])
            op(POOL, lambda e: e.tensor_tensor(out=mg[:, j, :], in0=m1_ap, in1=m2_ap, op=ALU.add),
               reads=[m1_buf, m2_buf], writes=[B_mg[j]])

        def P5(i):
            b = i % 2
            r0, r1 = ring_of(i, S_WO[0]), ring_of(i, S_WO[1])
            for s in range(NS):
                pb = alloc_pair()
                for h, rr in enumerate((r0, r1)):
                    w = wide(rr)
                    pairs = [(mg[:, k, s * 128:(s + 1) * 128], w[:, k, :]) for k in range(KC)]
                    mm_group(ps[:, pb + h, :], pairs, reads=B_mg + [B_ring[rr]], writes=[B_bank[pb + h]])
                    if s == NS - 1:
                        slot_done(i, S_WO[h])
                x_ap = xbuf[:, b, s, :]
                pair_ap = ps[:, pb:pb + 2, :].rearrange("p a b -> p (a b)")
                op(DVE, lambda e, x_ap=x_ap, pair_ap=pair_ap: e.scalar_tensor_tensor(
                    out=x_ap, in0=pair_ap, scalar=0.25, in1=x_ap, op0=ALU.mult, op1=ALU.add),
                   reads=[B_bank[pb], B_bank[pb + 1], B_x[b][s]], writes=[B_x[b][s]])
                r, b_r = rms_scale(x_ap, B_x[b][s], D)
                op(DVE, lambda e, x_ap=x_ap, r=r: e.scalar_tensor_tensor(
                    out=x_ap, in0=x_ap, scalar=r, in1=gb[:, 2, :], op0=ALU.mult, op1=ALU.mult),
                   reads=[B_x[b][s], b_r, B_gb], writes=[B_x[b][s]])

        load_x(0, 0)
        op(POOL, lambda e: e.memset(mhalf[:], -0.5), writes=[B_mhalf])
        op(ACT, lambda e: e.activation(out=junk[:, 0:1], in_=mhalf[:, 0:1], func=AF.Square),
           reads=[B_mhalf], writes=[B_junk])
        op(POOL, lambda e: e.memset(ones[:], 1.0), writes=[B_ones])
        op(POOL, lambda e: e.memset(xhb[:], 0.0), writes=[B_xhb])
        op(POOL, lambda e: e.affine_select(out=ident[:], in_=ones[:], pattern=[[1, 128]], compare_op=ALU.is_equal,
                                           fill=0.0, base=0, channel_multiplier=-1),
           reads=[B_ones], writes=[B_ident])
        dma(SP, lambda e: e.dma_start(out=gb[:, 0, :], in_=gains_d[:, 0:D].partition_broadcast(128)[:, 0, :]),
            sem_g0, writes=[B_gb0])
        for s in range(1, NS):
            load_x(0, s)
        for _ in range(2):
            issue_load()
        dma(SP, lambda e: e.dma_start(out=xhb[0:2, :], in_=xh_d), sem_setup, reads=[B_ring[1]], writes=[B_xhb])
        dma(SP, lambda e: e.dma_start(out=gb[:, 1:3, :].rearrange("p a d -> p (a d)"),
                                      in_=gains_d[:, D:3 * D].partition_broadcast(128)[:, 0, :]),
            sem_setup, writes=[B_gb])
        dma(SP, lambda e: e.dma_start(out=convw[:], in_=convw_d), sem_setup2, writes=[B_convw])
        dma(SP, lambda e: e.dma_start(out=wsT32[:].rearrange("p a d -> p (a d)"), in_=wsT_d), sem_setup2,
            writes=[B_wsT32])
        dma(SP, lambda e: e.dma_start(out=bt[:].rearrange("p a d -> p (a d)"),
                                      in_=bsp_d.partition_broadcast(128)[:, 0, :]), sem_setup2, writes=[B_bt])
        for b_ in (B_xhb, B_gb):
            b_.writer = (sem_setup, sem_setup.count)
        B_ring[1].readers[sem_setup] = sem_setup.count
        for b_ in (B_wsT32, B_bt, B_convw):
            b_.writer = (sem_setup2, sem_setup2.count)

        HK = "h"
        stA = {HK: lambda: prepA(HK, xhb[:], B_xhb)}
        stB = {HK: lambda: prepB(HK, xhb[:], B_xhb)}
        stC = {HK: lambda: prepC(HK, hTh[:], B_hTh)}
        for s in range(NS):
            stA[s] = lambda s=s: P0A(0, s)
            stB[s] = lambda s=s: P0B(0, s)
            stC[s] = lambda s=s: P0C(0, s)
        order = [0, 1, 2, 3, HK]
        stA[order[0]]()
        stB[order[0]]()
        for n in range(1, len(order)):
            stA[order[n]]()
            stC[order[n - 1]]()
            stB[order[n]]()
        stC[order[-1]]()
        for _ in range(RING - 2):
            issue_load()
        op(POOL, lambda e: e.affine_select(out=WT[:], in_=wsT32[:], pattern=[[0, 8], [1, 128]],
                                           compare_op=ALU.is_ge, fill=0.0, base=0, channel_multiplier=-1),
           reads=[B_wsT32], writes=[B_WT])
        for s in range(NS):
            load_x(1, s)

        P1(0)
        for i in range(NT):
            for j in range(KC):
                P3(i, j)
                if i >= 1 and j == 1:
                    for s in range(NS):
                        store_y(i - 1, s)
                    if i + 1 < NT:
                        for s in range(NS):
                            load_x(i + 1, s)
                if i + 1 < NT:
                    if 6 <= j <= 7:
                        P0C(i + 1, j - 6)
                    if 4 <= j <= 7:
                        P0B(i + 1, j - 4)
                    if 3 <= j <= 6:
                        P0A(i + 1, j - 3)
            for g in range(KC):
                P2(i, g)
                if i + 1 < NT and g <= 1:
                    P0C(i + 1, 2 + g)
            for j in range(KC):
                P4(i, j)
            if i + 1 < NT:
                P1(i + 1)
            P5(i)
        for s in range(NS):
            store_y(NT - 1, s)
        fin = []
        for b in range(2):
            for s in range(NS):
                fin.append((sem_st[b][s], sem_st[b][s].count))

        with nc.Block() as block:
            @block.sync
            def _(e):
                emit(SP, e)
                for s_, v in fin:
                    e.wait_ge(s_.h, v)

            @block.tensor
            def _(e):
                emit(PE, e)

            @block.scalar
            def _(e):
                emit(ACT, e)

            @block.vector
            def _(e):
                emit(DVE, e)

            @block.gpsimd
            def _(e):
                emit(POOL, e)
    return nc


def _blocks(w, col0, ncols):
    return w[:, col0:col0 + ncols].reshape(KC, 128, ncols).transpose(1, 0, 2)


def build_stream(w_in, w_a, w_b, w_out):
    st = np.empty((NSLOT, 128, SLOT_ELEMS), dtype=np.float32)
    U, V, ZA, XB, CB, BB, ZB, GA, GB = [q * D for q in range(9)]
    for h in range(2):
        st[S_WV[h]] = _blocks(w_in, V + h * 512, 512).reshape(128, SLOT_ELEMS)
        st[S_WO[h]] = _blocks(w_out, h * 512, 512).reshape(128, SLOT_ELEMS)
    for j in range(KC):
        c = j * 128
        st[S_P3[j]] = np.stack([_blocks(w_in, XB + c, 128), _blocks(w_in, CB + c, 128),
                                _blocks(w_in, BB + c, 128), _blocks(w_in, ZB + c, 128)], axis=1).reshape(128, SLOT_ELEMS)
        st[S_P4[j]] = np.stack([_blocks(w_in, GA + c, 128), _blocks(w_in, GB + c, 128),
                                _blocks(w_b, c, 128), _blocks(w_a, c, 128)], axis=1).reshape(128, SLOT_ELEMS)
    for q in range(4):
        g0, g1 = 2 * q, 2 * q + 1
        st[S_P2[q]] = np.stack([_blocks(w_in, U + g0 * 128, 128), _blocks(w_in, ZA + g0 * 128, 128),
                                _blocks(w_in, U + g1 * 128, 128), _blocks(w_in, ZA + g1 * 128, 128)],
                               axis=1).reshape(128, SLOT_ELEMS)
    return st


_NC_CACHE = {}


def kernel(x, norm_g, w_in, v_norm_g, w_spatial, b_spatial, conv_w, w_branch_a, w_branch_b, w_out, final_norm_g):
    x = np.asarray(x, dtype=np.float32)
    B, S, _ = x.shape
    assert (B, S) == (4, 8192)
    w_in = np.asarray(w_in, np.float32)[0]
    wst = build_stream(w_in, np.asarray(w_branch_a, np.float32)[0], np.asarray(w_branch_b, np.float32)[0],
                       np.asarray(w_out, np.float32)[0])
    gains = np.concatenate([np.asarray(norm_g, np.float32)[0], np.asarray(v_norm_g, np.float32)[0],
                            np.asarray(final_norm_g, np.float32)]).reshape(1, 3 * D)
    bsp = np.asarray(b_spatial, np.float32)[0].reshape(1, 8 * 128)
    convw = np.ascontiguousarray(np.asarray(conv_w, np.float32)[0].reshape(3, KC, 128).transpose(2, 1, 0)).reshape(128, 24)
    wsT = np.ascontiguousarray(np.asarray(w_spatial, np.float32)[0].transpose(2, 0, 1)).reshape(128, 8 * 128)

    in_maps = []
    for c in range(NCORES):
        b, half = c // 2, c % 2
        xc = np.ascontiguousarray(x[b, half * TOK:(half + 1) * TOK, :])
        if half == 0:
            xh = np.zeros((2, D), np.float32)
        else:
            xh = np.ascontiguousarray(x[b, TOK - 2:TOK, :])
        in_maps.append({"x": xc, "xh": xh, "wst": wst, "gains": gains, "bsp": bsp, "convw": convw, "wsT": wsT})

    if "nc" not in _NC_CACHE:
        _NC_CACHE["nc"] = build_program()
    nc = _NC_CACHE["nc"]
    res = run_bass_kernel_spmd(nc, in_maps, core_ids=list(range(NCORES)))
    out = np.empty((B, S, D), np.float32)
    for c in range(NCORES):
        b, half = c // 2, c % 2
        out[b, half * TOK:(half + 1) * TOK, :] = res.results[c]["y"]
    return out
```

```python
import numpy as np
from contextlib import ExitStack

import concourse.bass as bass
import concourse.mybir as mybir
from concourse.bass_utils import run_bass_kernel_spmd

F32 = mybir.dt.float32
BF16 = mybir.dt.bfloat16
AF = mybir.ActivationFunctionType
ALU = mybir.AluOpType

D = 1024
NCORES = 8
TOK = 4096
T = 512
NT = TOK // T
NS = 4
KC = 8
EPS = 1e-6
RING = 4
NSLOT = 24
SLOT_ELEMS = 4096
WB_SPREAD = 4

S_WV = [0, 1]
S_P3 = list(range(2, 10))
S_P2 = list(range(10, 14))
S_P4 = list(range(14, 22))
S_WO = [22, 23]


class Sem:
    def __init__(self, h, name):
        self.h = h
        self.name = name
        self.count = 0


class Buf:
    __slots__ = ("name", "writer", "readers")

    def __init__(self, name):
        self.name = name
        self.writer = None
        self.readers = {}


class Eng:
    def __init__(self, name, sem, self_sync=True):
        self.name = name
        self.sem = sem
        self.ops = []
        self.waited = {}
        self.self_sync = self_sync


def _collect_waits(eng, reads, writes):
    deps = {}

    def add(tok):
        if tok is None:
            return
        s, v = tok
        if deps.get(s, 0) < v:
            deps[s] = v

    for b in reads:
        add(b.writer)
    for b in writes:
        add(b.writer)
        for s, v in b.readers.items():
            add((s, v))
    waits = []
    for s, v in deps.items():
        if s is eng.sem and not eng.self_sync:
            continue
        if eng.waited.get(s, 0) < v:
            eng.waited[s] = v
            waits.append((s, v))
    return waits


def _commit(tok, reads, writes):
    s, v = tok
    for b in reads:
        if b.readers.get(s, 0) < v:
            b.readers[s] = v
    for b in writes:
        b.writer = tok
        b.readers = {}


def op(eng, fn, reads=(), writes=()):
    waits = _collect_waits(eng, reads, writes)
    eng.sem.count += 1
    tok = (eng.sem, eng.sem.count)
    eng.ops.append((waits, fn, eng.sem, 1))
    _commit(tok, reads, writes)
    return tok


def dma(eng, fn, dsem, reads=(), writes=()):
    waits = _collect_waits(eng, reads, writes)
    dsem.count += 16
    tok = (dsem, dsem.count)
    eng.ops.append((waits, fn, dsem, 16))
    _commit(tok, reads, writes)
    return tok


def emit(eng, e):
    for waits, fn, sem, inc in eng.ops:
        for s, v in waits:
            e.wait_ge(s.h, v)
        ins = fn(e)
        ins.then_inc(sem.h, inc)


class Rot:
    def __init__(self, items):
        self.items = items
        self.i = 0

    def next(self):
        it = self.items[self.i % len(self.items)]
        self.i += 1
        return it


def build_program():
    nc = bass.Bass("TRN2", target_bir_lowering=False)
    x_d = nc.dram_tensor("x", [TOK, D], F32, kind="ExternalInput").ap()
    xh_d = nc.dram_tensor("xh", [2, D], F32, kind="ExternalInput").ap()
    wst_d = nc.dram_tensor("wst", [NSLOT, 128, SLOT_ELEMS], F32, kind="ExternalInput").ap()
    gains_d = nc.dram_tensor("gains", [1, 3 * D], F32, kind="ExternalInput").ap()
    bsp_d = nc.dram_tensor("bsp", [1, 8 * 128], F32, kind="ExternalInput").ap()
    convw_d = nc.dram_tensor("convw", [128, 24], F32, kind="ExternalInput").ap()
    wsT_d = nc.dram_tensor("wsT", [128, 8 * 128], F32, kind="ExternalInput").ap()
    y_d = nc.dram_tensor("y", [TOK, D], F32, kind="ExternalOutput").ap()
    wbf_d = nc.dram_tensor("wbf", [NSLOT, 128, SLOT_ELEMS], BF16).ap()

    with ExitStack() as es:
        def sb(name, shape, dt):
            return es.enter_context(nc.sbuf_tensor("sb_" + name, shape, dt))

        def new_sem(name):
            return Sem(es.enter_context(nc.semaphore(name)), name)

        xbuf = sb("xbuf", [128, 2, NS, D], F32)
        xsb = sb("xsb", [128, 3, D], BF16)
        hT = sb("hT", [128, 2, KC, T], BF16)
        hTh = sb("hTh", [128, KC, 128], BF16)
        xhb = sb("xhb", [128, D], F32)
        gvb = sb("gvb", [128, 2, D], F32)
        junk = sb("junk", [128, D], BF16)
        vpp = sb("vpp", [128, NS, D], BF16)
        yA = sb("yA", [128, KC, T], BF16)
        yB = sb("yB", [128, KC, T], BF16)
        mg = sb("mg", [128, KC, T], BF16)
        NTMP = 19
        tmp = sb("tmp", [128, NTMP, T], F32)
        hcb = sb("hcb", [128, 2, T + 4], F32)
        gb = sb("gb", [128, 3, D], F32)
        bt = sb("bt", [128, 8, 128], F32)
        wsT32 = sb("wsT32", [128, 8, 128], F32)
        WT = sb("WT", [128, 8, 128], BF16)
        ident = sb("ident", [128, 128], BF16)
        ones = sb("ones", [128, 128], F32)
        convw = sb("convw", [128, 24], F32)
        halo = sb("halo", [128, 8, 2], F32)
        tiny = sb("tiny", [128, 3, 16], F32)
        mhalf = sb("mhalf", [128, 1], F32)
        hxc = sb("hxc", [128, 2, 4], F32)
        ring = sb("ring", [128, RING, SLOT_ELEMS], BF16)

        ps = es.enter_context(nc.psum_tensor("ps_main", [128, 7, 512], F32))
        psT = es.enter_context(nc.psum_tensor("ps_tr", [128, D], BF16))

        PE = Eng("pe", new_sem("s_pe"), self_sync=False)
        ACT = Eng("act", new_sem("s_act"))
        DVE = Eng("dve", new_sem("s_dve"))
        POOL = Eng("pool", new_sem("s_pool"))
        SP = Eng("sp", new_sem("s_sp"))
        sem_setup = new_sem("d_setup")
        sem_setup2 = new_sem("d_setup2")
        sem_xh = new_sem("d_xh")
        sem_g0 = new_sem("d_g0")
        sem_ring = [new_sem(f"d_ring{r}") for r in range(RING)]
        sem_xl = [[new_sem(f"d_xl{b}{s}") for s in range(NS)] for b in range(2)]
        sem_st = [[new_sem(f"d_st{b}{s}") for s in range(NS)] for b in range(2)]
        sem_wb = [new_sem(f"d_wb{r}") for r in range(RING)]
        sem_ringc = [new_sem(f"d_ringc{r}") for r in range(RING)]

        B_x = [[Buf(f"x{b}{s}") for s in range(NS)] for b in range(2)]
        B_xsb = [Buf(f"xsb{b}") for b in range(3)]
        B_hT = [[Buf(f"hT{b}{s}") for s in range(NS)] for b in range(2)]
        B_hTh = Buf("hTh")
        B_xhb = Buf("xhb")
        B_gvb = [Buf(f"gvb{b}") for b in range(2)]
        B_vpp = [Buf(f"vpp{s}") for s in range(NS)]
        B_yA = [Buf(f"yA{j}") for j in range(KC)]
        B_yB = [Buf(f"yB{j}") for j in range(KC)]
        B_mg = [Buf(f"mg{j}") for j in range(KC)]
        B_bank = [Buf(f"bank{b}") for b in range(7)]
        B_psT = Buf("psT")
        B_ring = [Buf(f"ring{r}") for r in range(RING)]
        B_wbf = [Buf(f"wbf{k}") for k in range(NSLOT)]
        B_gb = Buf("gb")
        B_gb0 = Buf("gb0")
        B_bt = Buf("bt")
        B_wsT32 = Buf("wsT32")
        B_WT = Buf("WT")
        B_ident = Buf("ident")
        B_ones = Buf("ones")
        B_convw = Buf("convw")
        B_halo = [Buf(f"halo{j}") for j in range(KC)]
        B_mhalf = Buf("mhalf")
        B_junk = Buf("junk")

        tmp_rot = Rot([(tmp[:, q, :], Buf(f"tmp{q}")) for q in range(NTMP)])
        hc_rot = Rot([(hcb[:, q, :], Buf(f"hc{q}")) for q in range(2)])
        hxc_rot = Rot([(hxc[:, q, :], Buf(f"hxc{q}")) for q in range(2)])
        tiny_rot = Rot([((tiny[:, 0, q:q + 1], tiny[:, 1, q:q + 1], tiny[:, 2, q:q + 1]),
                         (Buf(f"ss{q}"), Buf(f"ms{q}"), Buf(f"r{q}"))) for q in range(16)])

        bank_ptr = [0]

        def alloc_bank():
            b = bank_ptr[0] % 7
            bank_ptr[0] += 1
            return b

        def alloc_pair():
            while (bank_ptr[0] % 7) % 2 == 1 or (bank_ptr[0] % 7) == 6:
                bank_ptr[0] += 1
            b = bank_ptr[0] % 7
            bank_ptr[0] += 2
            return b

        stream = []
        stream += [(0, s) for s in S_WV]
        for i in range(NT):
            stream += [(i, s) for s in S_P3 + S_P2 + S_P4]
            if i + 1 < NT:
                stream += [(i + 1, s) for s in S_WV]
            stream += [(i, s) for s in S_WO]
        stream_pos = {ts: n for n, ts in enumerate(stream)}
        next_load = [0]

        pending_wb = []

        def flush_wb(upto):
            while pending_wb and pending_wb[0][0] <= upto:
                _, r, slot = pending_wb.pop(0)
                dma(SP, lambda e, r=r, slot=slot: e.dma_start(out=wbf_d[slot], in_=ring[:, r, :]),
                    sem_wb[r], reads=[B_ring[r]], writes=[B_wbf[slot]])

        def issue_load():
            n = next_load[0]
            flush_wb(n - 2)
            if n >= len(stream):
                return
            next_load[0] += 1
            ti, slot = stream[n]
            r = n % RING
            wbt = slot % WB_SPREAD
            if ti <= wbt:
                extra = []
                dma(POOL, lambda e, r=r, slot=slot: e.dma_start(out=ring[:, r, :], in_=wst_d[slot]),
                    sem_ringc[r], reads=extra, writes=[B_ring[r]])
                if ti == wbt:
                    pending_wb.append((n, r, slot))
            else:
                dma(SP, lambda e, r=r, slot=slot: e.dma_start(out=ring[:, r, :], in_=wbf_d[slot]),
                    sem_ring[r], reads=[B_wbf[slot]], writes=[B_ring[r]])

        def ring_of(i, slot):
            return stream_pos[(i, slot)] % RING

        def slot_done(i, slot):
            n = stream_pos[(i, slot)]
            assert next_load[0] == n + RING or next_load[0] >= len(stream), (next_load[0], n)
            issue_load()

        def rms_scale(src_ap, src_buf, width):
            (ss, ms, r), (b_ss, b_ms, b_r) = tiny_rot.next()
            op(ACT, lambda e: e.activation(out=junk[:, 0:width], in_=src_ap, func=AF.Square, accum_out=ss),
               reads=[src_buf], writes=[b_ss, B_junk])
            op(DVE, lambda e: e.tensor_scalar(out=ms, in0=ss, scalar1=1.0 / width, scalar2=EPS,
                                              op0=ALU.mult, op1=ALU.add),
               reads=[b_ss], writes=[b_ms])
            op(POOL, lambda e: e.tensor_tensor(out=r, in0=ms, in1=mhalf[:, 0:1], op=ALU.pow),
               reads=[b_ms, B_mhalf], writes=[b_r])
            return r, b_r

        def mm_group(out_ap, pairs, reads, writes):
            n = len(pairs)

            def fn(e):
                ins = None
                for q, (l, r_) in enumerate(pairs):
                    ins = e.matmul(out_ap, lhsT=l, rhs=r_, start=(q == 0), stop=(q == n - 1))
                return ins
            return op(PE, fn, reads=reads, writes=writes)

        def blk(r, q):
            return ring[:, r, q * 1024:(q + 1) * 1024].rearrange("p (k c) -> p k c", k=KC)

        def wide(r):
            return ring[:, r, :].rearrange("p (k c) -> p k c", k=KC)

        p0_state = {}
        xs_ctr = [0]

        def prepA(key, x_ap, x_buf):
            p0_state[key] = rms_scale(x_ap, x_buf, D)

        def prepB(key, x_ap, x_buf):
            r, b_r = p0_state[key]
            q = xs_ctr[0] % 3
            xs_ctr[0] += 1
            p0_state[key] = q
            op(DVE, lambda e: e.scalar_tensor_tensor(out=xsb[:, q, :], in0=x_ap, scalar=r, in1=gb[:, 0, :],
                                                     op0=ALU.mult, op1=ALU.mult),
               reads=[x_buf, b_r, B_gb0], writes=[B_xsb[q]])

        def prepC(key, hT_out_ap, hT_buf):
            q = p0_state.pop(key)

            def tr(e):
                ins = None
                for k in range(KC):
                    ins = e.transpose(psT[:, k * 128:(k + 1) * 128], xsb[:, q, k * 128:(k + 1) * 128], ident[:])
                return ins
            op(PE, tr, reads=[B_xsb[q], B_ident], writes=[B_psT])
            op(ACT, lambda e: e.activation(out=hT_out_ap, in_=psT[:].rearrange("p (k t) -> p k t", k=KC),
                                           func=AF.Copy),
               reads=[B_psT], writes=[hT_buf])

        def P0A(i, s):
            prepA((i, s), xbuf[:, i % 2, s, :], B_x[i % 2][s])

        def P0B(i, s):
            prepB((i, s), xbuf[:, i % 2, s, :], B_x[i % 2][s])

        def P0C(i, s):
            prepC((i, s), hT[:, i % 2, :, s * 128:(s + 1) * 128], B_hT[i % 2][s])

        def load_x(i, s):
            b = i % 2
            row0 = i * T + s * 128
            dma(SP, lambda e: e.dma_start(out=xbuf[:, b, s, :], in_=x_d[row0:row0 + 128, :]),
                sem_xl[b][s], reads=[], writes=[B_x[b][s]])

        def store_y(i, s):
            b = i % 2
            row0 = i * T + s * 128
            dma(SP, lambda e: e.dma_start(out=y_d[row0:row0 + 128, :], in_=xbuf[:, b, s, :]),
                sem_st[b][s], reads=[B_x[b][s]], writes=[])

        def P1(i):
            b = i % 2
            r0, r1 = ring_of(i, S_WV[0]), ring_of(i, S_WV[1])
            for s in range(NS):
                pb = alloc_pair()
                for h, rr in enumerate((r0, r1)):
                    w = wide(rr)
                    pairs = [(hT[:, b, k, s * 128:(s + 1) * 128], w[:, k, :]) for k in range(KC)]
                    mm_group(ps[:, pb + h, :], pairs, reads=[B_hT[b][s], B_ring[rr]], writes=[B_bank[pb + h]])
                    if s == NS - 1:
                        slot_done(i, S_WV[h])
                gq = s % 2
                gv_ap = gvb[:, gq, :]
                pair_ap = ps[:, pb:pb + 2, :].rearrange("p a b -> p (a b)")
                op(ACT, lambda e, gv_ap=gv_ap, pair_ap=pair_ap: e.activation(out=gv_ap, in_=pair_ap, func=AF.Gelu),
                   reads=[B_bank[pb], B_bank[pb + 1]], writes=[B_gvb[gq]])
                r, b_r = rms_scale(gv_ap, B_gvb[gq], D)
                op(DVE, lambda e, gv_ap=gv_ap, r=r, s=s: e.scalar_tensor_tensor(
                    out=vpp[:, s, :], in0=gv_ap, scalar=r, in1=gb[:, 1, :], op0=ALU.mult, op1=ALU.mult),
                   reads=[B_gvb[gq], b_r, B_gb], writes=[B_vpp[s]])

        def P3(i, j):
            b = i % 2
            rr = ring_of(i, S_P3[j])
            hTb = [B_hT[b][s] for s in range(NS)]

            def proj(q, N=T, rhs_fn=None):
                bk = alloc_bank()
                w = blk(rr, q)
                pairs = [(w[:, k, :], hT[:, b, k, :]) for k in range(KC)]
                mm_group(ps[:, bk, :], pairs, reads=hTb + [B_ring[rr]], writes=[B_bank[bk]])
                return bk

            hc_ap, hc_buf = hc_rot.next()
            bX = proj(0)
            if i == 0:
                bH = alloc_bank()
                wx, wc = blk(rr, 0), blk(rr, 1)
                mm_group(ps[:, bH, 0:2], [(wx[:, k, :], hTh[:, k, 0:2]) for k in range(KC)],
                         reads=[B_hTh, B_ring[rr]], writes=[B_bank[bH]])
            bC = proj(1)
            if i == 0:
                mm_group(ps[:, bH, 2:4], [(wc[:, k, :], hTh[:, k, 0:2]) for k in range(KC)],
                         reads=[B_hTh, B_ring[rr]], writes=[B_bank[bH]])
            bB = proj(2)
            bZ = proj(3)
            slot_done(i, S_P3[j])

            xs_ap, xs_buf = tmp_rot.next()
            op(ACT, lambda e: e.activation(out=xs_ap, in_=ps[:, bX, :], func=AF.Copy),
               reads=[B_bank[bX]], writes=[xs_buf])
            if i == 0:
                hx_ap, hx_buf = hxc_rot.next()
                op(ACT, lambda e: e.activation(out=hx_ap, in_=ps[:, bH, 0:4], func=AF.Copy),
                   reads=[B_bank[bH]], writes=[hx_buf])
                op(DVE, lambda e: e.tensor_tensor(out=halo[:, j, :], in0=hx_ap[:, 0:2], in1=hx_ap[:, 2:4],
                                                  op=ALU.mult),
                   reads=[hx_buf], writes=[B_halo[j]])
            op(ACT, lambda e: e.activation(out=hc_ap[:, 0:2], in_=halo[:, j, :], func=AF.Copy),
               reads=[B_halo[j]], writes=[hc_buf])
            op(DVE, lambda e: e.tensor_tensor(out=hc_ap[:, 2:T + 2], in0=xs_ap, in1=ps[:, bC, :], op=ALU.mult),
               reads=[xs_buf, B_bank[bC]], writes=[hc_buf])
            tz_ap, tz_buf = tmp_rot.next()
            op(ACT, lambda e: e.activation(out=tz_ap, in_=ps[:, bZ, :], func=AF.Tanh, scale=0.5),
               reads=[B_bank[bZ]], writes=[tz_buf])
            acc_ap, acc_buf = tmp_rot.next()
            op(ACT, lambda e: e.activation(out=acc_ap, in_=hc_ap[:, 0:T], func=AF.Copy,
                                           scale=convw[:, 3 * j:3 * j + 1]),
               reads=[hc_buf, B_convw], writes=[acc_buf])
            a_ap, a_buf = tmp_rot.next()
            op(DVE, lambda e: e.scalar_tensor_tensor(out=a_ap, in0=tz_ap, scalar=1.0, in1=ps[:, bZ, :],
                                                     op0=ALU.add, op1=ALU.mult),
               reads=[tz_buf, B_bank[bZ]], writes=[a_buf])
            for kk in (1, 2):
                op(DVE, lambda e, kk=kk: e.scalar_tensor_tensor(
                    out=acc_ap, in0=hc_ap[:, kk:kk + T], scalar=convw[:, 3 * j + kk:3 * j + kk + 1], in1=acc_ap,
                    op0=ALU.mult, op1=ALU.add),
                   reads=[hc_buf, acc_buf, B_convw], writes=[acc_buf])
            op(ACT, lambda e: e.activation(out=halo[:, j, :], in_=hc_ap[:, T:T + 2], func=AF.Copy),
               reads=[hc_buf], writes=[B_halo[j]])
            op(DVE, lambda e: e.tensor_tensor(out=acc_ap, in0=acc_ap, in1=ps[:, bB, :], op=ALU.mult),
               reads=[acc_buf, B_bank[bB]], writes=[acc_buf])
            op(POOL, lambda e: e.tensor_tensor(out=yB[:, j, :], in0=acc_ap, in1=a_ap, op=ALU.mult),
               reads=[acc_buf, a_buf], writes=[B_yB[j]])

        def P2(i, g):
            b = i % 2
            rr = ring_of(i, S_P2[g // 2])
            q0 = (g % 2) * 2
            hTb = [B_hT[b][s] for s in range(NS)]
            bS = alloc_bank()

            def sp(e):
                ins = None
                for s in range(NS):
                    ins = e.matmul(ps[:, bS, s * 128:(s + 1) * 128], lhsT=vpp[:, s, g * 128:(g + 1) * 128],
                                   rhs=WT[:, g, :], start=True, stop=True)
                return ins
            op(PE, sp, reads=B_vpp + [B_WT], writes=[B_bank[bS]])
            bU = alloc_bank()
            wu = blk(rr, q0)
            mm_group(ps[:, bU, :], [(wu[:, k, :], hT[:, b, k, :]) for k in range(KC)],
                     reads=hTb + [B_ring[rr]], writes=[B_bank[bU]])
            bZ = alloc_bank()
            wz = blk(rr, q0 + 1)
            mm_group(ps[:, bZ, :], [(wz[:, k, :], hT[:, b, k, :]) for k in range(KC)],
                     reads=hTb + [B_ring[rr]], writes=[B_bank[bZ]])
            if g % 2 == 1:
                slot_done(i, S_P2[g // 2])

            gu_ap, gu_buf = tmp_rot.next()
            op(ACT, lambda e: e.activation(out=gu_ap, in_=ps[:, bU, :], func=AF.Gelu),
               reads=[B_bank[bU]], writes=[gu_buf])
            tz_ap, tz_buf = tmp_rot.next()
            op(ACT, lambda e: e.activation(out=tz_ap, in_=ps[:, bZ, :], func=AF.Tanh, scale=0.5),
               reads=[B_bank[bZ]], writes=[tz_buf])
            m_ap, m_buf = tmp_rot.next()
            op(DVE, lambda e: e.tensor_tensor(
                out=m_ap.rearrange("p (a t) -> p a t", a=NS),
                in0=ps[:, bS, :].rearrange("p (a t) -> p a t", a=NS),
                in1=bt[:, g:g + 1, :].to_broadcast([128, NS, 128]), op=ALU.add),
               reads=[B_bank[bS], B_bt], writes=[m_buf])
            op(DVE, lambda e: e.tensor_tensor(out=m_ap, in0=m_ap, in1=gu_ap, op=ALU.mult),
               reads=[m_buf, gu_buf], writes=[m_buf])
            a_ap, a_buf = tmp_rot.next()
            op(DVE, lambda e: e.scalar_tensor_tensor(out=a_ap, in0=tz_ap, scalar=1.0, in1=ps[:, bZ, :],
                                                     op0=ALU.add, op1=ALU.mult),
               reads=[tz_buf, B_bank[bZ]], writes=[a_buf])
            op(POOL, lambda e: e.tensor_tensor(out=yA[:, g, :], in0=m_ap, in1=a_ap, op=ALU.mult),
               reads=[m_buf, a_buf], writes=[B_yA[g]])

        def P4(i, j):
            b = i % 2
            rr = ring_of(i, S_P4[j])
            hTb = [B_hT[b][s] for s in range(NS)]
            bGA = alloc_bank()
            w = blk(rr, 0)
            mm_group(ps[:, bGA, :], [(w[:, k, :], hT[:, b, k, :]) for k in range(KC)],
                     reads=hTb + [B_ring[rr]], writes=[B_bank[bGA]])
            bGB = alloc_bank()
            w = blk(rr, 1)
            mm_group(ps[:, bGB, :], [(w[:, k, :], hT[:, b, k, :]) for k in range(KC)],
                     reads=hTb + [B_ring[rr]], writes=[B_bank[bGB]])
            bYB = alloc_bank()
            w = blk(rr, 2)
            mm_group(ps[:, bYB, :], [(w[:, k, :], yB[:, k, :]) for k in range(KC)],
                     reads=B_yB + [B_ring[rr]], writes=[B_bank[bYB]])
            bYA = alloc_bank()
            w = blk(rr, 3)
            mm_group(ps[:, bYA, :], [(w[:, k, :], yA[:, k, :]) for k in range(KC)],
                     reads=B_yA + [B_ring[rr]], writes=[B_bank[bYA]])
            slot_done(i, S_P4[j])

            tA_ap, tA_buf = tmp_rot.next()
            op(ACT, lambda e: e.activation(out=tA_ap, in_=ps[:, bGA, :], func=AF.Tanh, scale=0.5),
               reads=[B_bank[bGA]], writes=[tA_buf])
            tB_ap, tB_buf = tmp_rot.next()
            op(ACT, lambda e: e.activation(out=tB_ap, in_=ps[:, bGB, :], func=AF.Tanh, scale=0.5),
               reads=[B_bank[bGB]], writes=[tB_buf])
            m2_ap, m2_buf = tmp_rot.next()
            op(DVE, lambda e: e.scalar_tensor_tensor(out=m2_ap, in0=tB_ap, scalar=1.0, in1=ps[:, bYB, :],
                                                     op0=ALU.add, op1=ALU.mult),
               reads=[tB_buf, B_bank[bYB]], writes=[m2_buf])
            m1_ap, m1_buf = tmp_rot.next()
            op(DVE, lambda e: e.scalar_tensor_tensor(out=m1_ap, in0=tA_ap, scalar=1.0, in1=ps[:, bYA, :],
                                                     op0=ALU.add, op1=ALU.mult),
               reads=[tA_buf, B_bank[bYA]], writes=[m1_buf])
            op(POOL, lambda e: e.tensor_tensor(out=mg[:, j, :], in0=m1_ap, in1=m2_ap, op=ALU.add),
               reads=[m1_buf, m2_buf], writes=[B_mg[j]])

        def P5(i):
            b = i % 2
            r0, r1 = ring_of(i, S_WO[0]), ring_of(i, S_WO[1])

            def final_scale(s, x_ap, r, b_r):
                op(DVE, lambda e: e.scalar_tensor_tensor(
                    out=x_ap, in0=x_ap, scalar=r, in1=gb[:, 2, :], op0=ALU.mult, op1=ALU.mult),
                   reads=[B_x[b][s], b_r, B_gb], writes=[B_x[b][s]])

            pend = None
            for s in range(NS):
                pb = alloc_pair()
                for h, rr in enumerate((r0, r1)):
                    w = wide(rr)
                    pairs = [(mg[:, k, s * 128:(s + 1) * 128], w[:, k, :]) for k in range(KC)]
                    mm_group(ps[:, pb + h, :], pairs, reads=B_mg + [B_ring[rr]], writes=[B_bank[pb + h]])
                    if s == NS - 1:
                        slot_done(i, S_WO[h])
                x_ap = xbuf[:, b, s, :]
                pair_ap = ps[:, pb:pb + 2, :].rearrange("p a b -> p (a b)")
                op(DVE, lambda e, x_ap=x_ap, pair_ap=pair_ap: e.scalar_tensor_tensor(
                    out=x_ap, in0=pair_ap, scalar=0.25, in1=x_ap, op0=ALU.mult, op1=ALU.add),
                   reads=[B_bank[pb], B_bank[pb + 1], B_x[b][s]], writes=[B_x[b][s]])
                r, b_r = rms_scale(x_ap, B_x[b][s], D)
                if pend is not None:
                    final_scale(*pend)
                pend = (s, x_ap, r, b_r)
            final_scale(*pend)

        load_x(0, 0)
        op(POOL, lambda e: e.memset(mhalf[:], -0.5), writes=[B_mhalf])
        op(ACT, lambda e: e.activation(out=junk[:, 0:1], in_=mhalf[:, 0:1], func=AF.Square),
           reads=[B_mhalf], writes=[B_junk])
        op(POOL, lambda e: e.memset(ones[:], 1.0), writes=[B_ones])
        op(POOL, lambda e: e.memset(xhb[:], 0.0), writes=[B_xhb])
        op(POOL, lambda e: e.affine_select(out=ident[:], in_=ones[:], pattern=[[1, 128]], compare_op=ALU.is_equal,
                                           fill=0.0, base=0, channel_multiplier=-1),
           reads=[B_ones], writes=[B_ident])
        dma(SP, lambda e: e.dma_start(out=gb[:, 0, :], in_=gains_d[:, 0:D].partition_broadcast(128)[:, 0, :]),
            sem_g0, writes=[B_gb0])
        for s in range(1, NS):
            load_x(0, s)
        issue_load()
        dma(SP, lambda e: e.dma_start(out=xhb[0:2, :], in_=xh_d), sem_xh, writes=[B_xhb])
        HK = "h"
        stA = {HK: lambda: prepA(HK, xhb[:], B_xhb)}
        stB = {HK: lambda: prepB(HK, xhb[:], B_xhb)}
        stC = {HK: lambda: prepC(HK, hTh[:], B_hTh)}
        for s in range(NS):
            stA[s] = lambda s=s: P0A(0, s)
            stB[s] = lambda s=s: P0B(0, s)
            stC[s] = lambda s=s: P0C(0, s)
        order = [0, 1, 2, 3, HK]
        stA[order[0]]()
        issue_load()
        dma(SP, lambda e: e.dma_start(out=gb[:, 1:3, :].rearrange("p a d -> p (a d)"),
                                      in_=gains_d[:, D:3 * D].partition_broadcast(128)[:, 0, :]),
            sem_setup, reads=[B_ring[1]], writes=[B_gb])
        dma(SP, lambda e: e.dma_start(out=convw[:], in_=convw_d), sem_setup2, writes=[B_convw])
        dma(SP, lambda e: e.dma_start(out=wsT32[:].rearrange("p a d -> p (a d)"), in_=wsT_d), sem_setup2,
            writes=[B_wsT32])
        dma(SP, lambda e: e.dma_start(out=bt[:].rearrange("p a d -> p (a d)"),
                                      in_=bsp_d.partition_broadcast(128)[:, 0, :]), sem_setup2, writes=[B_bt])
        for b_ in (B_gb,):
            b_.writer = (sem_setup, sem_setup.count)
        B_ring[1].readers[sem_setup] = sem_setup.count
        for b_ in (B_wsT32, B_bt, B_convw):
            b_.writer = (sem_setup2, sem_setup2.count)

        stB[order[0]]()
        for n in range(1, len(order)):
            stA[order[n]]()
            if order[n - 1] != HK:
                stC[order[n - 1]]()
            stB[order[n]]()
        for _ in range(RING - 2):
            issue_load()
        op(POOL, lambda e: e.affine_select(out=WT[:], in_=wsT32[:], pattern=[[0, 8], [1, 128]],
                                           compare_op=ALU.is_ge, fill=0.0, base=0, channel_multiplier=-1),
           reads=[B_wsT32], writes=[B_WT])
        for s in range(NS):
            load_x(1, s)

        P1(0)
        stC[HK]()
        for i in range(NT):
            for j in range(KC):
                P3(i, j)
                if i >= 1 and j == 1:
                    for s in range(NS):
                        store_y(i - 1, s)
                    if i + 1 < NT:
                        for s in range(NS):
                            load_x(i + 1, s)
                if i + 1 < NT:
                    if 6 <= j <= 7:
                        P0C(i + 1, j - 6)
                    if 4 <= j <= 7:
                        P0B(i + 1, j - 4)
                    if 3 <= j <= 6:
                        P0A(i + 1, j - 3)
            for g in range(KC):
                P2(i, g)
                if i + 1 < NT and g <= 1:
                    P0C(i + 1, 2 + g)
            for j in range(KC):
                P4(i, j)
            if i + 1 < NT:
                P1(i + 1)
            P5(i)
        for s in range(NS):
            store_y(NT - 1, s)
        fin = []
        for b in range(2):
            for s in range(NS):
                fin.append((sem_st[b][s], sem_st[b][s].count))

        with nc.Block() as block:
            @block.sync
            def _(e):
                emit(SP, e)
                for s_, v in fin:
                    e.wait_ge(s_.h, v)

            @block.tensor
            def _(e):
                emit(PE, e)

            @block.scalar
            def _(e):
                emit(ACT, e)

            @block.vector
            def _(e):
                emit(DVE, e)

            @block.gpsimd
            def _(e):
                emit(POOL, e)
    return nc


def _blocks(w, col0, ncols):
    return w[:, col0:col0 + ncols].reshape(KC, 128, ncols).transpose(1, 0, 2)


def build_stream(w_in, w_a, w_b, w_out):
    st = np.empty((NSLOT, 128, SLOT_ELEMS), dtype=np.float32)
    U, V, ZA, XB, CB, BB, ZB, GA, GB = [q * D for q in range(9)]
    for h in range(2):
        st[S_WV[h]] = _blocks(w_in, V + h * 512, 512).reshape(128, SLOT_ELEMS)
        st[S_WO[h]] = _blocks(w_out, h * 512, 512).reshape(128, SLOT_ELEMS)
    for j in range(KC):
        c = j * 128
        st[S_P3[j]] = np.stack([_blocks(w_in, XB + c, 128), _blocks(w_in, CB + c, 128),
                                _blocks(w_in, BB + c, 128), _blocks(w_in, ZB + c, 128)], axis=1).reshape(128, SLOT_ELEMS)
        st[S_P4[j]] = np.stack([_blocks(w_in, GA + c, 128), _blocks(w_in, GB + c, 128),
                                _blocks(w_b, c, 128), _blocks(w_a, c, 128)], axis=1).reshape(128, SLOT_ELEMS)
    for q in range(4):
        g0, g1 = 2 * q, 2 * q + 1
        st[S_P2[q]] = np.stack([_blocks(w_in, U + g0 * 128, 128), _blocks(w_in, ZA + g0 * 128, 128),
                                _blocks(w_in, U + g1 * 128, 128), _blocks(w_in, ZA + g1 * 128, 128)],
                               axis=1).reshape(128, SLOT_ELEMS)
    return st


_NC_CACHE = {}


def kernel(x, norm_g, w_in, v_norm_g, w_spatial, b_spatial, conv_w, w_branch_a, w_branch_b, w_out, final_norm_g):
    x = np.asarray(x, dtype=np.float32)
    B, S, _ = x.shape
    assert (B, S) == (4, 8192)
    w_in = np.asarray(w_in, np.float32)[0]
    wst = build_stream(w_in, np.asarray(w_branch_a, np.float32)[0], np.asarray(w_branch_b, np.float32)[0],
                       np.asarray(w_out, np.float32)[0])
    gains = np.concatenate([np.asarray(norm_g, np.float32)[0], np.asarray(v_norm_g, np.float32)[0],
                            np.asarray(final_norm_g, np.float32)]).reshape(1, 3 * D)
    bsp = np.asarray(b_spatial, np.float32)[0].reshape(1, 8 * 128)
    convw = np.ascontiguousarray(np.asarray(conv_w, np.float32)[0].reshape(3, KC, 128).transpose(2, 1, 0)).reshape(128, 24)
    wsT = np.ascontiguousarray(np.asarray(w_spatial, np.float32)[0].transpose(2, 0, 1)).reshape(128, 8 * 128)

    in_maps = []
    for c in range(NCORES):
        b, half = c // 2, c % 2
        xc = np.ascontiguousarray(x[b, half * TOK:(half + 1) * TOK, :])
        if half == 0:
            xh = np.zeros((2, D), np.float32)
        else:
            xh = np.ascontiguousarray(x[b, TOK - 2:TOK, :])
        in_maps.append({"x": xc, "xh": xh, "wst": wst, "gains": gains, "bsp": bsp, "convw": convw, "wsT": wsT})

    if "nc" not in _NC_CACHE:
        _NC_CACHE["nc"] = build_program()
    nc = _NC_CACHE["nc"]
    res = run_bass_kernel_spmd(nc, in_maps, core_ids=list(range(NCORES)))
    out = np.empty((B, S, D), np.float32)
    for c in range(NCORES):
        b, half = c // 2, c % 2
        out[b, half * TOK:(half + 1) * TOK, :] = res.results[c]["y"]
    return out
```

```python
import numpy as np
from contextlib import ExitStack

import concourse.bass as bass
import concourse.mybir as mybir
from concourse.bass_utils import run_bass_kernel_spmd

F32 = mybir.dt.float32
BF16 = mybir.dt.bfloat16
AF = mybir.ActivationFunctionType
ALU = mybir.AluOpType

D = 1024
NCORES = 8
TOK = 4096
T = 512
NT = TOK // T
NS = 4
KC = 8
EPS = 1e-6
RING = 4
NSLOT = 24
SLOT_ELEMS = 4096
WB_SPREAD = 4

S_WV = [0, 1]
S_P3 = list(range(2, 10))
S_P2 = list(range(10, 14))
S_P4 = list(range(14, 22))
S_WO = [22, 23]


class Sem:
    def __init__(self, h, name):
        self.h = h
        self.name = name
        self.count = 0


class Buf:
    __slots__ = ("name", "writer", "readers")

    def __init__(self, name):
        self.name = name
        self.writer = None
        self.readers = {}


class Eng:
    def __init__(self, name, sem, self_sync=True):
        self.name = name
        self.sem = sem
        self.ops = []
        self.waited = {}
        self.self_sync = self_sync


def _collect_waits(eng, reads, writes):
    deps = {}

    def add(tok):
        if tok is None:
            return
        s, v = tok
        if deps.get(s, 0) < v:
            deps[s] = v

    for b in reads:
        add(b.writer)
    for b in writes:
        add(b.writer)
        for s, v in b.readers.items():
            add((s, v))
    waits = []
    for s, v in deps.items():
        if s is eng.sem and not eng.self_sync:
            continue
        if eng.waited.get(s, 0) < v:
            eng.waited[s] = v
            waits.append((s, v))
    return waits


def _commit(tok, reads, writes):
    s, v = tok
    for b in reads:
        if b.readers.get(s, 0) < v:
            b.readers[s] = v
    for b in writes:
        b.writer = tok
        b.readers = {}


def op(eng, fn, reads=(), writes=()):
    waits = _collect_waits(eng, reads, writes)
    eng.sem.count += 1
    tok = (eng.sem, eng.sem.count)
    eng.ops.append((waits, fn, eng.sem, 1))
    _commit(tok, reads, writes)
    return tok


def dma(eng, fn, dsem, reads=(), writes=()):
    waits = _collect_waits(eng, reads, writes)
    dsem.count += 16
    tok = (dsem, dsem.count)
    eng.ops.append((waits, fn, dsem, 16))
    _commit(tok, reads, writes)
    return tok


def emit(eng, e):
    for waits, fn, sem, inc in eng.ops:
        for s, v in waits:
            e.wait_ge(s.h, v)
        ins = fn(e)
        ins.then_inc(sem.h, inc)


class Rot:
    def __init__(self, items):
        self.items = items
        self.i = 0

    def next(self):
        it = self.items[self.i % len(self.items)]
        self.i += 1
        return it


def build_program():
    nc = bass.Bass("TRN2", target_bir_lowering=False)
    x_d = nc.dram_tensor("x", [TOK, D], F32, kind="ExternalInput").ap()
    xh_d = nc.dram_tensor("xh", [2, D], F32, kind="ExternalInput").ap()
    wst_d = nc.dram_tensor("wst", [NSLOT, 128, SLOT_ELEMS], F32, kind="ExternalInput").ap()
    gains_d = nc.dram_tensor("gains", [1, 3 * D], F32, kind="ExternalInput").ap()
    bsp_d = nc.dram_tensor("bsp", [1, 8 * 128], F32, kind="ExternalInput").ap()
    convw_d = nc.dram_tensor("convw", [128, 24], F32, kind="ExternalInput").ap()
    wsT_d = nc.dram_tensor("wsT", [128, 8 * 128], F32, kind="ExternalInput").ap()
    y_d = nc.dram_tensor("y", [TOK, D], F32, kind="ExternalOutput").ap()
    wbf_d = nc.dram_tensor("wbf", [NSLOT, 128, SLOT_ELEMS], BF16).ap()

    with ExitStack() as es:
        def sb(name, shape, dt):
            return es.enter_context(nc.sbuf_tensor("sb_" + name, shape, dt))

        def new_sem(name):
            return Sem(es.enter_context(nc.semaphore(name)), name)

        xbuf = sb("xbuf", [128, 2, NS, D], F32)
        xsb = sb("xsb", [128, 3, D], BF16)
        hT = sb("hT", [128, 2, KC, T], BF16)
        hTh = sb("hTh", [128, KC, 128], BF16)
        xhb = sb("xhb", [128, D], F32)
        gvb = sb("gvb", [128, 2, D], F32)
        junk = sb("junk", [128, D], BF16)
        vpp = sb("vpp", [128, NS, D], BF16)
        yA = sb("yA", [128, KC, T], BF16)
        yB = sb("yB", [128, KC, T], BF16)
        mg = sb("mg", [128, KC, T], BF16)
        NTMP = 19
        tmp = sb("tmp", [128, NTMP, T], F32)
        hcb = sb("hcb", [128, 2, T + 4], F32)
        gb = sb("gb", [128, 3, D], F32)
        bt = sb("bt", [128, 8, 128], F32)
        wsT32 = sb("wsT32", [128, 8, 128], F32)
        WT = sb("WT", [128, 8, 128], BF16)
        ident = sb("ident", [128, 128], BF16)
        ones = sb("ones", [128, 128], F32)
        convw = sb("convw", [128, 24], F32)
        halo = sb("halo", [128, 8, 2], F32)
        tiny = sb("tiny", [128, 3, 16], F32)
        mhalf = sb("mhalf", [128, 1], F32)
        hxc = sb("hxc", [128, 2, 4], F32)
        ring = sb("ring", [128, RING, SLOT_ELEMS], BF16)

        ps = es.enter_context(nc.psum_tensor("ps_main", [128, 7, 512], F32))
        psT = es.enter_context(nc.psum_tensor("ps_tr", [128, D], BF16))

        PE = Eng("pe", new_sem("s_pe"), self_sync=False)
        ACT = Eng("act", new_sem("s_act"))
        DVE = Eng("dve", new_sem("s_dve"))
        POOL = Eng("pool", new_sem("s_pool"))
        SP = Eng("sp", new_sem("s_sp"))
        sem_setup = new_sem("d_setup")
        sem_setup2 = new_sem("d_setup2")
        sem_g0 = new_sem("d_g0")
        sem_ring = [new_sem(f"d_ring{r}") for r in range(RING)]
        sem_xl = [[new_sem(f"d_xl{b}{s}") for s in range(NS)] for b in range(2)]
        sem_st = [[new_sem(f"d_st{b}{s}") for s in range(NS)] for b in range(2)]
        sem_wb = [new_sem(f"d_wb{r}") for r in range(RING)]
        sem_ringc = [new_sem(f"d_ringc{r}") for r in range(RING)]

        B_x = [[Buf(f"x{b}{s}") for s in range(NS)] for b in range(2)]
        B_xsb = [Buf(f"xsb{b}") for b in range(3)]
        B_hT = [[Buf(f"hT{b}{s}") for s in range(NS)] for b in range(2)]
        B_hTh = Buf("hTh")
        B_xhb = Buf("xhb")
        B_gvb = [Buf(f"gvb{b}") for b in range(2)]
        B_vpp = [Buf(f"vpp{s}") for s in range(NS)]
        B_yA = [Buf(f"yA{j}") for j in range(KC)]
        B_yB = [Buf(f"yB{j}") for j in range(KC)]
        B_mg = [Buf(f"mg{j}") for j in range(KC)]
        B_bank = [Buf(f"bank{b}") for b in range(7)]
        B_psT = Buf("psT")
        B_ring = [Buf(f"ring{r}") for r in range(RING)]
        B_wbf = [Buf(f"wbf{k}") for k in range(NSLOT)]
        B_gb = Buf("gb")
        B_gb0 = Buf("gb0")
        B_bt = Buf("bt")
        B_wsT32 = Buf("wsT32")
        B_WT = Buf("WT")
        B_ident = Buf("ident")
        B_ones = Buf("ones")
        B_convw = Buf("convw")
        B_halo = [Buf(f"halo{j}") for j in range(KC)]
        B_mhalf = Buf("mhalf")
        B_junk = Buf("junk")

        tmp_rot = Rot([(tmp[:, q, :], Buf(f"tmp{q}")) for q in range(NTMP)])
        hc_rot = Rot([(hcb[:, q, :], Buf(f"hc{q}")) for q in range(2)])
        hxc_rot = Rot([(hxc[:, q, :], Buf(f"hxc{q}")) for q in range(2)])
        tiny_rot = Rot([((tiny[:, 0, q:q + 1], tiny[:, 1, q:q + 1], tiny[:, 2, q:q + 1]),
                         (Buf(f"ss{q}"), Buf(f"ms{q}"), Buf(f"r{q}"))) for q in range(16)])

        bank_ptr = [0]

        def alloc_bank():
            b = bank_ptr[0] % 7
            bank_ptr[0] += 1
            return b

        def alloc_pair():
            while (bank_ptr[0] % 7) % 2 == 1 or (bank_ptr[0] % 7) == 6:
                bank_ptr[0] += 1
            b = bank_ptr[0] % 7
            bank_ptr[0] += 2
            return b

        stream = []
        stream += [(0, s) for s in S_WV]
        for i in range(NT):
            stream += [(i, s) for s in S_P3 + S_P2 + S_P4]
            if i + 1 < NT:
                stream += [(i + 1, s) for s in S_WV]
            stream += [(i, s) for s in S_WO]
        stream_pos = {ts: n for n, ts in enumerate(stream)}
        next_load = [0]

        pending_wb = []

        def flush_wb(upto):
            while pending_wb and pending_wb[0][0] <= upto:
                _, r, slot = pending_wb.pop(0)
                dma(SP, lambda e, r=r, slot=slot: e.dma_start(out=wbf_d[slot], in_=ring[:, r, :]),
                    sem_wb[r], reads=[B_ring[r]], writes=[B_wbf[slot]])

        def issue_load():
            n = next_load[0]
            flush_wb(n - 2)
            if n >= len(stream):
                return
            next_load[0] += 1
            ti, slot = stream[n]
            r = n % RING
            wbt = slot % WB_SPREAD
            if ti <= wbt:
                extra = []
                dma(POOL, lambda e, r=r, slot=slot: e.dma_start(out=ring[:, r, :], in_=wst_d[slot]),
                    sem_ringc[r], reads=extra, writes=[B_ring[r]])
                if ti == wbt:
                    pending_wb.append((n, r, slot))
            else:
                dma(SP, lambda e, r=r, slot=slot: e.dma_start(out=ring[:, r, :], in_=wbf_d[slot]),
                    sem_ring[r], reads=[B_wbf[slot]], writes=[B_ring[r]])

        def ring_of(i, slot):
            return stream_pos[(i, slot)] % RING

        def slot_done(i, slot):
            n = stream_pos[(i, slot)]
            assert next_load[0] == n + RING or next_load[0] >= len(stream), (next_load[0], n)
            issue_load()

        def rms_scale(src_ap, src_buf, width):
            (ss, ms, r), (b_ss, b_ms, b_r) = tiny_rot.next()
            op(ACT, lambda e: e.activation(out=junk[:, 0:width], in_=src_ap, func=AF.Square, accum_out=ss),
               reads=[src_buf], writes=[b_ss, B_junk])
            op(DVE, lambda e: e.tensor_scalar(out=ms, in0=ss, scalar1=1.0 / width, scalar2=EPS,
                                              op0=ALU.mult, op1=ALU.add),
               reads=[b_ss], writes=[b_ms])
            op(POOL, lambda e: e.tensor_tensor(out=r, in0=ms, in1=mhalf[:, 0:1], op=ALU.pow),
               reads=[b_ms, B_mhalf], writes=[b_r])
            return r, b_r

        def mm_group(out_ap, pairs, reads, writes):
            n = len(pairs)

            def fn(e):
                ins = None
                for q, (l, r_) in enumerate(pairs):
                    ins = e.matmul(out_ap, lhsT=l, rhs=r_, start=(q == 0), stop=(q == n - 1))
                return ins
            return op(PE, fn, reads=reads, writes=writes)

        def blk(r, q):
            return ring[:, r, q * 1024:(q + 1) * 1024].rearrange("p (k c) -> p k c", k=KC)

        def wide(r):
            return ring[:, r, :].rearrange("p (k c) -> p k c", k=KC)

        p0_state = {}
        xs_ctr = [0]

        def prepA(key, x_ap, x_buf):
            p0_state[key] = rms_scale(x_ap, x_buf, D)

        def prepB(key, x_ap, x_buf):
            r, b_r = p0_state[key]
            q = xs_ctr[0] % 3
            xs_ctr[0] += 1
            p0_state[key] = q
            op(DVE, lambda e: e.scalar_tensor_tensor(out=xsb[:, q, :], in0=x_ap, scalar=r, in1=gb[:, 0, :],
                                                     op0=ALU.mult, op1=ALU.mult),
               reads=[x_buf, b_r, B_gb0], writes=[B_xsb[q]])

        def prepC(key, hT_out_ap, hT_buf):
            q = p0_state.pop(key)

            def tr(e):
                ins = None
                for k in range(KC):
                    ins = e.transpose(psT[:, k * 128:(k + 1) * 128], xsb[:, q, k * 128:(k + 1) * 128], ident[:])
                return ins
            op(PE, tr, reads=[B_xsb[q], B_ident], writes=[B_psT])
            op(ACT, lambda e: e.activation(out=hT_out_ap, in_=psT[:].rearrange("p (k t) -> p k t", k=KC),
                                           func=AF.Copy),
               reads=[B_psT], writes=[hT_buf])

        def P0A(i, s):
            prepA((i, s), xbuf[:, i % 2, s, :], B_x[i % 2][s])

        def P0B(i, s):
            prepB((i, s), xbuf[:, i % 2, s, :], B_x[i % 2][s])

        def P0C(i, s):
            prepC((i, s), hT[:, i % 2, :, s * 128:(s + 1) * 128], B_hT[i % 2][s])

        def load_x(i, s):
            b = i % 2
            row0 = i * T + s * 128
            dma(SP, lambda e: e.dma_start(out=xbuf[:, b, s, :], in_=x_d[row0:row0 + 128, :]),
                sem_xl[b][s], reads=[], writes=[B_x[b][s]])

        def store_y(i, s):
            b = i % 2
            row0 = i * T + s * 128
            dma(SP, lambda e: e.dma_start(out=y_d[row0:row0 + 128, :], in_=xbuf[:, b, s, :]),
                sem_st[b][s], reads=[B_x[b][s]], writes=[])

        def P1(i):
            b = i % 2
            r0, r1 = ring_of(i, S_WV[0]), ring_of(i, S_WV[1])
            for s in range(NS):
                pb = alloc_pair()
                for h, rr in enumerate((r0, r1)):
                    w = wide(rr)
                    pairs = [(hT[:, b, k, s * 128:(s + 1) * 128], w[:, k, :]) for k in range(KC)]
                    mm_group(ps[:, pb + h, :], pairs, reads=[B_hT[b][s], B_ring[rr]], writes=[B_bank[pb + h]])
                    if s == NS - 1:
                        slot_done(i, S_WV[h])
                gq = s % 2
                gv_ap = gvb[:, gq, :]
                pair_ap = ps[:, pb:pb + 2, :].rearrange("p a b -> p (a b)")
                op(ACT, lambda e, gv_ap=gv_ap, pair_ap=pair_ap: e.activation(out=gv_ap, in_=pair_ap, func=AF.Gelu),
                   reads=[B_bank[pb], B_bank[pb + 1]], writes=[B_gvb[gq]])
                r, b_r = rms_scale(gv_ap, B_gvb[gq], D)
                op(DVE, lambda e, gv_ap=gv_ap, r=r, s=s: e.scalar_tensor_tensor(
                    out=vpp[:, s, :], in0=gv_ap, scalar=r, in1=gb[:, 1, :], op0=ALU.mult, op1=ALU.mult),
                   reads=[B_gvb[gq], b_r, B_gb], writes=[B_vpp[s]])

        def P3(i, j):
            b = i % 2
            rr = ring_of(i, S_P3[j])
            hTb = [B_hT[b][s] for s in range(NS)]

            def proj(q, N=T, rhs_fn=None):
                bk = alloc_bank()
                w = blk(rr, q)
                pairs = [(w[:, k, :], hT[:, b, k, :]) for k in range(KC)]
                mm_group(ps[:, bk, :], pairs, reads=hTb + [B_ring[rr]], writes=[B_bank[bk]])
                return bk

            hc_ap, hc_buf = hc_rot.next()
            bX = proj(0)
            if i == 0:
                bH = alloc_bank()
                wx, wc = blk(rr, 0), blk(rr, 1)
                mm_group(ps[:, bH, 0:2], [(wx[:, k, :], hTh[:, k, 0:2]) for k in range(KC)],
                         reads=[B_hTh, B_ring[rr]], writes=[B_bank[bH]])
            bC = proj(1)
            if i == 0:
                mm_group(ps[:, bH, 2:4], [(wc[:, k, :], hTh[:, k, 0:2]) for k in range(KC)],
                         reads=[B_hTh, B_ring[rr]], writes=[B_bank[bH]])
            bB = proj(2)
            bZ = proj(3)
            slot_done(i, S_P3[j])

            xs_ap, xs_buf = tmp_rot.next()
            op(ACT, lambda e: e.activation(out=xs_ap, in_=ps[:, bX, :], func=AF.Copy),
               reads=[B_bank[bX]], writes=[xs_buf])
            if i == 0:
                hx_ap, hx_buf = hxc_rot.next()
                op(ACT, lambda e: e.activation(out=hx_ap, in_=ps[:, bH, 0:4], func=AF.Copy),
                   reads=[B_bank[bH]], writes=[hx_buf])
                op(DVE, lambda e: e.tensor_tensor(out=halo[:, j, :], in0=hx_ap[:, 0:2], in1=hx_ap[:, 2:4],
                                                  op=ALU.mult),
                   reads=[hx_buf], writes=[B_halo[j]])
            op(ACT, lambda e: e.activation(out=hc_ap[:, 0:2], in_=halo[:, j, :], func=AF.Copy),
               reads=[B_halo[j]], writes=[hc_buf])
            op(DVE, lambda e: e.tensor_tensor(out=hc_ap[:, 2:T + 2], in0=xs_ap, in1=ps[:, bC, :], op=ALU.mult),
               reads=[xs_buf, B_bank[bC]], writes=[hc_buf])
            tz_ap, tz_buf = tmp_rot.next()
            op(ACT, lambda e: e.activation(out=tz_ap, in_=ps[:, bZ, :], func=AF.Tanh, scale=0.5),
               reads=[B_bank[bZ]], writes=[tz_buf])
            acc_ap, acc_buf = tmp_rot.next()
            op(ACT, lambda e: e.activation(out=acc_ap, in_=hc_ap[:, 0:T], func=AF.Copy,
                                           scale=convw[:, 3 * j:3 * j + 1]),
               reads=[hc_buf, B_convw], writes=[acc_buf])
            a_ap, a_buf = tmp_rot.next()
            op(DVE, lambda e: e.scalar_tensor_tensor(out=a_ap, in0=tz_ap, scalar=1.0, in1=ps[:, bZ, :],
                                                     op0=ALU.add, op1=ALU.mult),
               reads=[tz_buf, B_bank[bZ]], writes=[a_buf])
            for kk in (1, 2):
                op(DVE, lambda e, kk=kk: e.scalar_tensor_tensor(
                    out=acc_ap, in0=hc_ap[:, kk:kk + T], scalar=convw[:, 3 * j + kk:3 * j + kk + 1], in1=acc_ap,
                    op0=ALU.mult, op1=ALU.add),
                   reads=[hc_buf, acc_buf, B_convw], writes=[acc_buf])
            op(ACT, lambda e: e.activation(out=halo[:, j, :], in_=hc_ap[:, T:T + 2], func=AF.Copy),
               reads=[hc_buf], writes=[B_halo[j]])
            op(DVE, lambda e: e.tensor_tensor(out=acc_ap, in0=acc_ap, in1=ps[:, bB, :], op=ALU.mult),
               reads=[acc_buf, B_bank[bB]], writes=[acc_buf])
            op(POOL, lambda e: e.tensor_tensor(out=yB[:, j, :], in0=acc_ap, in1=a_ap, op=ALU.mult),
               reads=[acc_buf, a_buf], writes=[B_yB[j]])

        def P2(i, g):
            b = i % 2
            rr = ring_of(i, S_P2[g // 2])
            q0 = (g % 2) * 2
            hTb = [B_hT[b][s] for s in range(NS)]
            bS = alloc_bank()

            def sp(e):
                ins = None
                for s in range(NS):
                    ins = e.matmul(ps[:, bS, s * 128:(s + 1) * 128], lhsT=vpp[:, s, g * 128:(g + 1) * 128],
                                   rhs=WT[:, g, :], start=True, stop=True)
                return ins
            op(PE, sp, reads=B_vpp + [B_WT], writes=[B_bank[bS]])
            bU = alloc_bank()
            wu = blk(rr, q0)
            mm_group(ps[:, bU, :], [(wu[:, k, :], hT[:, b, k, :]) for k in range(KC)],
                     reads=hTb + [B_ring[rr]], writes=[B_bank[bU]])
            bZ = alloc_bank()
            wz = blk(rr, q0 + 1)
            mm_group(ps[:, bZ, :], [(wz[:, k, :], hT[:, b, k, :]) for k in range(KC)],
                     reads=hTb + [B_ring[rr]], writes=[B_bank[bZ]])
            if g % 2 == 1:
                slot_done(i, S_P2[g // 2])

            gu_ap, gu_buf = tmp_rot.next()
            op(ACT, lambda e: e.activation(out=gu_ap, in_=ps[:, bU, :], func=AF.Gelu),
               reads=[B_bank[bU]], writes=[gu_buf])
            tz_ap, tz_buf = tmp_rot.next()
            op(ACT, lambda e: e.activation(out=tz_ap, in_=ps[:, bZ, :], func=AF.Tanh, scale=0.5),
               reads=[B_bank[bZ]], writes=[tz_buf])
            m_ap, m_buf = tmp_rot.next()
            op(DVE, lambda e: e.tensor_tensor(
                out=m_ap.rearrange("p (a t) -> p a t", a=NS),
                in0=ps[:, bS, :].rearrange("p (a t) -> p a t", a=NS),
                in1=bt[:, g:g + 1, :].to_broadcast([128, NS, 128]), op=ALU.add),
               reads=[B_bank[bS], B_bt], writes=[m_buf])
            op(DVE, lambda e: e.tensor_tensor(out=m_ap, in0=m_ap, in1=gu_ap, op=ALU.mult),
               reads=[m_buf, gu_buf], writes=[m_buf])
            a_ap, a_buf = tmp_rot.next()
            op(DVE, lambda e: e.scalar_tensor_tensor(out=a_ap, in0=tz_ap, scalar=1.0, in1=ps[:, bZ, :],
                                                     op0=ALU.add, op1=ALU.mult),
               reads=[tz_buf, B_bank[bZ]], writes=[a_buf])
            op(POOL, lambda e: e.tensor_tensor(out=yA[:, g, :], in0=m_ap, in1=a_ap, op=ALU.mult),
               reads=[m_buf, a_buf], writes=[B_yA[g]])

        def P4(i, j):
            b = i % 2
            rr = ring_of(i, S_P4[j])
            hTb = [B_hT[b][s] for s in range(NS)]
            bGA = alloc_bank()
            w = blk(rr, 0)
            mm_group(ps[:, bGA, :], [(w[:, k, :], hT[:, b, k, :]) for k in range(KC)],
                     reads=hTb + [B_ring[rr]], writes=[B_bank[bGA]])
            bGB = alloc_bank()
            w = blk(rr, 1)
            mm_group(ps[:, bGB, :], [(w[:, k, :], hT[:, b, k, :]) for k in range(KC)],
                     reads=hTb + [B_ring[rr]], writes=[B_bank[bGB]])
            bYB = alloc_bank()
            w = blk(rr, 2)
            mm_group(ps[:, bYB, :], [(w[:, k, :], yB[:, k, :]) for k in range(KC)],
                     reads=B_yB + [B_ring[rr]], writes=[B_bank[bYB]])
            bYA = alloc_bank()
            w = blk(rr, 3)
            mm_group(ps[:, bYA, :], [(w[:, k, :], yA[:, k, :]) for k in range(KC)],
                     reads=B_yA + [B_ring[rr]], writes=[B_bank[bYA]])
            slot_done(i, S_P4[j])

            tA_ap, tA_buf = tmp_rot.next()
            op(ACT, lambda e: e.activation(out=tA_ap, in_=ps[:, bGA, :], func=AF.Tanh, scale=0.5),
               reads=[B_bank[bGA]], writes=[tA_buf])
            tB_ap, tB_buf = tmp_rot.next()
            op(ACT, lambda e: e.activation(out=tB_ap, in_=ps[:, bGB, :], func=AF.Tanh, scale=0.5),
               reads=[B_bank[bGB]], writes=[tB_buf])
            m2_ap, m2_buf = tmp_rot.next()
            op(DVE, lambda e: e.scalar_tensor_tensor(out=m2_ap, in0=tB_ap, scalar=1.0, in1=ps[:, bYB, :],
                                                     op0=ALU.add, op1=ALU.mult),
               reads=[tB_buf, B_bank[bYB]], writes=[m2_buf])
            m1_ap, m1_buf = tmp_rot.next()
            op(DVE, lambda e: e.scalar_tensor_tensor(out=m1_ap, in0=tA_ap, scalar=1.0, in1=ps[:, bYA, :],
                                                     op0=ALU.add, op1=ALU.mult),
               reads=[tA_buf, B_bank[bYA]], writes=[m1_buf])
            op(POOL, lambda e: e.tensor_tensor(out=mg[:, j, :], in0=m1_ap, in1=m2_ap, op=ALU.add),
               reads=[m1_buf, m2_buf], writes=[B_mg[j]])

        def P5(i):
            b = i % 2
            r0, r1 = ring_of(i, S_WO[0]), ring_of(i, S_WO[1])

            def final_scale(s, x_ap, r, b_r):
                op(DVE, lambda e: e.scalar_tensor_tensor(
                    out=x_ap, in0=x_ap, scalar=r, in1=gb[:, 2, :], op0=ALU.mult, op1=ALU.mult),
                   reads=[B_x[b][s], b_r, B_gb], writes=[B_x[b][s]])

            pend = None
            for s in range(NS):
                pb = alloc_pair()
                for h, rr in enumerate((r0, r1)):
                    w = wide(rr)
                    pairs = [(mg[:, k, s * 128:(s + 1) * 128], w[:, k, :]) for k in range(KC)]
                    mm_group(ps[:, pb + h, :], pairs, reads=B_mg + [B_ring[rr]], writes=[B_bank[pb + h]])
                    if s == NS - 1:
                        slot_done(i, S_WO[h])
                x_ap = xbuf[:, b, s, :]
                pair_ap = ps[:, pb:pb + 2, :].rearrange("p a b -> p (a b)")
                op(DVE, lambda e, x_ap=x_ap, pair_ap=pair_ap: e.scalar_tensor_tensor(
                    out=x_ap, in0=pair_ap, scalar=0.25, in1=x_ap, op0=ALU.mult, op1=ALU.add),
                   reads=[B_bank[pb], B_bank[pb + 1], B_x[b][s]], writes=[B_x[b][s]])
                r, b_r = rms_scale(x_ap, B_x[b][s], D)
                if pend is not None:
                    final_scale(*pend)
                pend = (s, x_ap, r, b_r)
            final_scale(*pend)

        load_x(0, 0)
        op(POOL, lambda e: e.memset(mhalf[:], -0.5), writes=[B_mhalf])
        op(ACT, lambda e: e.activation(out=junk[:, 0:1], in_=mhalf[:, 0:1], func=AF.Square),
           reads=[B_mhalf], writes=[B_junk])
        op(POOL, lambda e: e.memset(ones[:], 1.0), writes=[B_ones])
        op(POOL, lambda e: e.memset(xhb[:], 0.0), writes=[B_xhb])
        op(POOL, lambda e: e.affine_select(out=ident[:], in_=ones[:], pattern=[[1, 128]], compare_op=ALU.is_equal,
                                           fill=0.0, base=0, channel_multiplier=-1),
           reads=[B_ones], writes=[B_ident])
        dma(SP, lambda e: e.dma_start(out=gb[:, 0, :], in_=gains_d[:, 0:D].partition_broadcast(128)[:, 0, :]),
            sem_g0, writes=[B_gb0])
        for s in range(1, NS):
            load_x(0, s)
        for _ in range(2):
            issue_load()
        dma(SP, lambda e: e.dma_start(out=xhb[0:2, :], in_=xh_d), sem_setup, reads=[B_ring[1]], writes=[B_xhb])
        dma(SP, lambda e: e.dma_start(out=gb[:, 1:3, :].rearrange("p a d -> p (a d)"),
                                      in_=gains_d[:, D:3 * D].partition_broadcast(128)[:, 0, :]),
            sem_setup, writes=[B_gb])
        dma(SP, lambda e: e.dma_start(out=convw[:], in_=convw_d), sem_setup2, writes=[B_convw])
        dma(SP, lambda e: e.dma_start(out=wsT32[:].rearrange("p a d -> p (a d)"), in_=wsT_d), sem_setup2,
            writes=[B_wsT32])
        dma(SP, lambda e: e.dma_start(out=bt[:].rearrange("p a d -> p (a d)"),
                                      in_=bsp_d.partition_broadcast(128)[:, 0, :]), sem_setup2, writes=[B_bt])
        for b_ in (B_xhb, B_gb):
            b_.writer = (sem_setup, sem_setup.count)
        B_ring[1].readers[sem_setup] = sem_setup.count
        for b_ in (B_wsT32, B_bt, B_convw):
            b_.writer = (sem_setup2, sem_setup2.count)

        HK = "h"
        stA = {HK: lambda: prepA(HK, xhb[:], B_xhb)}
        stB = {HK: lambda: prepB(HK, xhb[:], B_xhb)}
        stC = {HK: lambda: prepC(HK, hTh[:], B_hTh)}
        for s in range(NS):
            stA[s] = lambda s=s: P0A(0, s)
            stB[s] = lambda s=s: P0B(0, s)
            stC[s] = lambda s=s: P0C(0, s)
        order = [0, 1, 2, 3, HK]
        stA[order[0]]()
        stB[order[0]]()
        for n in range(1, len(order)):
            stA[order[n]]()
            stC[order[n - 1]]()
            stB[order[n]]()
        stC[order[-1]]()
        for _ in range(RING - 2):
            issue_load()
        op(POOL, lambda e: e.affine_select(out=WT[:], in_=wsT32[:], pattern=[[0, 8], [1, 128]],
                                           compare_op=ALU.is_ge, fill=0.0, base=0, channel_multiplier=-1),
           reads=[B_wsT32], writes=[B_WT])
        for s in range(NS):
            load_x(1, s)

        P1(0)
        for i in range(NT):
            for j in range(KC):
                P3(i, j)
                if i >= 1 and j == 0:
                    for s in range(NS):
                        store_y(i - 1, s)
                    if i + 1 < NT:
                        for s in range(NS):
                            load_x(i + 1, s)
                if i + 1 < NT:
                    if 6 <= j <= 7:
                        P0C(i + 1, j - 6)
                    if 4 <= j <= 7:
                        P0B(i + 1, j - 4)
                    if 3 <= j <= 6:
                        P0A(i + 1, j - 3)
            for g in range(KC):
                P2(i, g)
                if i + 1 < NT and g <= 1:
                    P0C(i + 1, 2 + g)
            for j in range(KC):
                P4(i, j)
            if i + 1 < NT:
                P1(i + 1)
            P5(i)
        for s in range(NS):
            store_y(NT - 1, s)
        fin = []
        for b in range(2):
            for s in range(NS):
                fin.append((sem_st[b][s], sem_st[b][s].count))

        with nc.Block() as block:
            @block.sync
            def _(e):
                emit(SP, e)
                for s_, v in fin:
                    e.wait_ge(s_.h, v)

            @block.tensor
            def _(e):
                emit(PE, e)

            @block.scalar
            def _(e):
                emit(ACT, e)

            @block.vector
            def _(e):
                emit(DVE, e)

            @block.gpsimd
            def _(e):
                emit(POOL, e)
    return nc


def _blocks(w, col0, ncols):
    return w[:, col0:col0 + ncols].reshape(KC, 128, ncols).transpose(1, 0, 2)


def build_stream(w_in, w_a, w_b, w_out):
    st = np.empty((NSLOT, 128, SLOT_ELEMS), dtype=np.float32)
    U, V, ZA, XB, CB, BB, ZB, GA, GB = [q * D for q in range(9)]
    for h in range(2):
        st[S_WV[h]] = _blocks(w_in, V + h * 512, 512).reshape(128, SLOT_ELEMS)
        st[S_WO[h]] = _blocks(w_out, h * 512, 512).reshape(128, SLOT_ELEMS)
    for j in range(KC):
        c = j * 128
        st[S_P3[j]] = np.stack([_blocks(w_in, XB + c, 128), _blocks(w_in, CB + c, 128),
                                _blocks(w_in, BB + c, 128), _blocks(w_in, ZB + c, 128)], axis=1).reshape(128, SLOT_ELEMS)
        st[S_P4[j]] = np.stack([_blocks(w_in, GA + c, 128), _blocks(w_in, GB + c, 128),
                                _blocks(w_b, c, 128), _blocks(w_a, c, 128)], axis=1).reshape(128, SLOT_ELEMS)
    for q in range(4):
        g0, g1 = 2 * q, 2 * q + 1
        st[S_P2[q]] = np.stack([_blocks(w_in, U + g0 * 128, 128), _blocks(w_in, ZA + g0 * 128, 128),
                                _blocks(w_in, U + g1 * 128, 128), _blocks(w_in, ZA + g1 * 128, 128)],
                               axis=1).reshape(128, SLOT_ELEMS)
    return st


_NC_CACHE = {}


def kernel(x, norm_g, w_in, v_norm_g, w_spatial, b_spatial, conv_w, w_branch_a, w_branch_b, w_out, final_norm_g):
    x = np.asarray(x, dtype=np.float32)
    B, S, _ = x.shape
    assert (B, S) == (4, 8192)
    w_in = np.asarray(w_in, np.float32)[0]
    wst = build_stream(w_in, np.asarray(w_branch_a, np.float32)[0], np.asarray(w_branch_b, np.float32)[0],
                       np.asarray(w_out, np.float32)[0])
    gains = np.concatenate([np.asarray(norm_g, np.float32)[0], np.asarray(v_norm_g, np.float32)[0],
                            np.asarray(final_norm_g, np.float32)]).reshape(1, 3 * D)
    bsp = np.asarray(b_spatial, np.float32)[0].reshape(1, 8 * 128)
    convw = np.ascontiguousarray(np.asarray(conv_w, np.float32)[0].reshape(3, KC, 128).transpose(2, 1, 0)).reshape(128, 24)
    wsT = np.ascontiguousarray(np.asarray(w_spatial, np.float32)[0].transpose(2, 0, 1)).reshape(128, 8 * 128)

    in_maps = []
    for c in range(NCORES):
        b, half = c // 2, c % 2
        xc = np.ascontiguousarray(x[b, half * TOK:(half + 1) * TOK, :])
        if half == 0:
            xh = np.zeros((2, D), np.float32)
        else:
            xh = np.ascontiguousarray(x[b, TOK - 2:TOK, :])
        in_maps.append({"x": xc, "xh": xh, "wst": wst, "gains": gains, "bsp": bsp, "convw": convw, "wsT": wsT})

    if "nc" not in _NC_CACHE:
        _NC_CACHE["nc"] = build_program()
    nc = _NC_CACHE["nc"]
    res = run_bass_kernel_spmd(nc, in_maps, core_ids=list(range(NCORES)))
    out = np.empty((B, S, D), np.float32)
    for c in range(NCORES):
        b, half = c // 2, c % 2
        out[b, half * TOK:(half + 1) * TOK, :] = res.results[c]["y"]
    return out
```
